# Optimizing a Trainium2 kernel written in Bass

```python
import jax, jax.numpy as jnp
from jax import lax
import numpy as np

D_MODEL = 1024
BATCH = 8
SEQ = 2048
DEPTH = 1
DEC_BATCH = 128
DEC_SEQ = 4
PAST_LEN = 16384
PAGE_SIZE = 128

RWKV_HEAD = 64
D_RWKV = D_MODEL // 2
N_RWKV_HEADS = D_RWKV // RWKV_HEAD
DECAY_LORA = 64
AAA_LORA = 64
GATE_LORA = 128
D_SHIFT = 3 * D_RWKV + DECAY_LORA + AAA_LORA + GATE_LORA
D_POOL = D_MODEL - D_RWKV
POOL_WINDOWS = (2, 4, 8, 16)
N_POOL_GROUPS = len(POOL_WINDOWS)
POOL_GROUP = D_POOL // N_POOL_GROUPS
POOL_BUF = max(POOL_WINDOWS) - 1
D_IN = D_SHIFT + D_POOL
N_KEYS = 128
N_EXPERTS = N_KEYS * N_KEYS
PEER_HEADS = 8
PEER_TOPK = 16
PEER_DK = 256
PEER_DK_HALF = PEER_DK // 2
PEER_BLOCK = 128
PLE_DIM = 256
NORM_EPS = 1e-6
LNX_EPS = 64e-5

kernel_name = 'hybrid_rwkv7_pool_peer_step'


def rmsnorm(x, g):
    xf = x.astype(jnp.float32)
    y = xf * lax.rsqrt(jnp.mean(xf * xf, axis=-1, keepdims=True) + NORM_EPS)
    return (y * g.astype(jnp.float32)).astype(x.dtype)


def wkv_scan(s0, r, w, k, v, kk, a):
    def step(s, inp):
        r_t, w_t, k_t, v_t, kk_t, a_t = inp
        s_kk = jnp.einsum('bhij,bhj->bhi', s, -kk_t)
        s = (s * w_t[:, :, None, :]
             + s_kk[..., None] * (kk_t * a_t)[:, :, None, :]
             + v_t[..., None] * k_t[:, :, None, :])
        return s, jnp.einsum('bhij,bhj->bhi', s, r_t)
    xs = tuple(jnp.moveaxis(t.astype(jnp.float32), 1, 0) for t in (r, w, k, v, kk, a))
    s_final, out = lax.scan(step, s0.astype(jnp.float32), xs)
    return jnp.moveaxis(out, 0, 1), s_final


def rwkv7_mix(z, shift_prev, s0, mu, decay_w0, decay_b, a_0, a_b, g_b, k_k, k_a, r_k, lnx_g, lnx_b):
    B, T, _ = z.shape
    z_prev = jnp.concatenate([shift_prev[:, None, :].astype(z.dtype), z[:, :-1]], axis=1)
    zs = z + (z_prev - z) * mu
    r = zs[..., :D_RWKV]
    k = zs[..., D_RWKV:2 * D_RWKV]
    v = zs[..., 2 * D_RWKV:3 * D_RWKV]
    o = 3 * D_RWKV
    zw = zs[..., o:o + DECAY_LORA]
    za = zs[..., o + DECAY_LORA:o + DECAY_LORA + AAA_LORA]
    zg = zs[..., o + DECAY_LORA + AAA_LORA:]
    w_log = -jax.nn.softplus(-(decay_w0 + jnp.tanh(zw) @ decay_b).astype(jnp.float32)) - 0.5
    decay = jnp.exp(-jnp.exp(w_log))
    a = jax.nn.sigmoid(a_0 + za @ a_b)
    g = jax.nn.sigmoid(zg) @ g_b

    def heads(t):
        return t.reshape(B, T, N_RWKV_HEADS, RWKV_HEAD)

    r, k, v, a, decay = heads(r), heads(k), heads(v), heads(a), heads(decay)
    kkf = (k * k_k.reshape(N_RWKV_HEADS, RWKV_HEAD)).astype(jnp.float32)
    kk = kkf / jnp.maximum(jnp.sqrt(jnp.sum(kkf * kkf, axis=-1, keepdims=True)), 1e-12)
    k = k * (1 + (a - 1) * k_a.reshape(N_RWKV_HEADS, RWKV_HEAD))
    out, s_final = wkv_scan(s0, r, decay, k, v, kk, a)
    mean = jnp.mean(out, axis=-1, keepdims=True)
    var = jnp.mean(jnp.square(out - mean), axis=-1, keepdims=True)
    out = ((out - mean) * lax.rsqrt(var + LNX_EPS)).reshape(B, T, D_RWKV)
    out = out * lnx_g.astype(jnp.float32) + lnx_b.astype(jnp.float32)
    bonus = jnp.sum((r * k * r_k).astype(jnp.float32), axis=-1, keepdims=True) * v.astype(jnp.float32)
    y = (out + bonus.reshape(B, T, D_RWKV)) * g.astype(jnp.float32)
    return y.astype(z.dtype), z[:, -1], s_final


def pool_mix(u, buf, start_pos, pool_w, pool_scale):
    B, T, _ = u.shape
    ext = jnp.concatenate([buf.astype(u.dtype), u], axis=1)
    c = jnp.cumsum(ext.astype(jnp.float32), axis=1)
    c = jnp.concatenate([jnp.zeros((B, 1, D_POOL), jnp.float32), c], axis=1)
    pos = start_pos + jnp.arange(T, dtype=jnp.int32)
    lo = POOL_BUF + 1
    diffs = []
    for gi, wdw in enumerate(POOL_WINDOWS):
        sl = slice(gi * POOL_GROUP, (gi + 1) * POOL_GROUP)
        s = c[:, lo:lo + T, sl] - c[:, lo - wdw:lo - wdw + T, sl]
        cnt = jnp.minimum(pos + 1, wdw).astype(jnp.float32)
        diffs.append(s / cnt[None, :, None] - u[:, :, sl].astype(jnp.float32))
    pooled = jnp.stack(diffs, axis=2)
    y = jnp.einsum('btgc,gcd->btgd', pooled, pool_w.astype(jnp.float32)).reshape(B, T, D_POOL)
    y = y * pool_scale.astype(jnp.float32)
    return y.astype(u.dtype), ext[:, -POOL_BUF:]


def peer_ffn(xn, wq, sub_keys, u_tab, v_tab):
    B, T, D = xn.shape
    flat = xn.reshape(-1, D)
    n = flat.shape[0]
    pad = (-n) % PEER_BLOCK
    blocks = jnp.pad(flat, ((0, pad), (0, 0))).reshape(-1, PEER_BLOCK, D)

    def block_fn(xb):
        q = (xb @ wq).reshape(PEER_BLOCK, PEER_HEADS, 2, PEER_DK_HALF)
        s1 = jnp.einsum('thc,nc->thn', q[:, :, 0], sub_keys[0])
        s2 = jnp.einsum('thc,nc->thn', q[:, :, 1], sub_keys[1])
        v1, i1 = lax.top_k(s1, PEER_TOPK)
        v2, i2 = lax.top_k(s2, PEER_TOPK)
        cand = (v1[..., :, None] + v2[..., None, :]).reshape(PEER_BLOCK, PEER_HEADS, PEER_TOPK * PEER_TOPK)
        sc, ci = lax.top_k(cand, PEER_TOPK)
        e1 = jnp.take_along_axis(i1, ci // PEER_TOPK, axis=-1)
        e2 = jnp.take_along_axis(i2, ci % PEER_TOPK, axis=-1)
        expert = e1 * N_KEYS + e2
        gate = jax.nn.softmax(sc.astype(jnp.float32), axis=-1)
        ue = jnp.take(u_tab, expert, axis=0)
        act = jax.nn.gelu(jnp.einsum('thkd,td->thk', ue, xb).astype(jnp.float32), approximate=False) * gate
        ve = jnp.take(v_tab, expert, axis=0)
        return jnp.einsum('thk,thkd->td', act.astype(xb.dtype), ve)

    out = lax.map(block_fn, blocks).reshape(-1, D)[:n]
    return out.reshape(B, T, D)


def hybrid_layer(x, p, start_pos, shift_prev, wkv_prev, pool_prev, lw):
    n1 = rmsnorm(x, lw['norm_mix_g'])
    z = n1 @ lw['w_in']
    y_r, shift_new, wkv_new = rwkv7_mix(
        z[..., :D_SHIFT], shift_prev, wkv_prev, lw['shift_mu'], lw['decay_w0'], lw['decay_b'],
        lw['a_0'], lw['a_b'], lw['g_b'], lw['k_k'], lw['k_a'], lw['r_k'], lw['lnx_g'], lw['lnx_b'])
    y_p, pool_new = pool_mix(z[..., D_SHIFT:], pool_prev, start_pos, lw['pool_w'], lw['pool_scale'])
    h = x + jnp.concatenate([y_r, y_p], axis=-1) @ lw['w_out']
    h = h + peer_ffn(rmsnorm(h, lw['norm_ffn_g']), lw['peer_wq'], lw['peer_keys'], lw['peer_u'], lw['peer_v'])
    gate = jax.nn.sigmoid((rmsnorm(h, lw['norm_ple_g']) @ lw['ple_gate_w']).astype(jnp.float32))
    h = h + ((p @ lw['ple_w']).astype(jnp.float32) * gate).astype(h.dtype)
    return h, shift_new, wkv_new, pool_new


def setup_inputs(seed: int = 0) -> dict:
    key = jax.random.key(seed)
    ks = list(jax.random.split(key, 40))

    def nrm(shape, scale):
        return scale * jax.random.normal(ks.pop(), shape, jnp.float32)

    def unif(shape, lo, hi):
        return jax.random.uniform(ks.pop(), shape, jnp.float32, lo, hi)

    L = DEPTH
    return {
        'x_prompt': nrm((BATCH, SEQ, D_MODEL), 1.0),
        'x_sample': nrm((DEC_BATCH, DEC_SEQ, D_MODEL), 1.0),
        'state_shift': nrm((L, DEC_BATCH, D_SHIFT), 1.0),
        'state_wkv': nrm((L, DEC_BATCH, N_RWKV_HEADS, RWKV_HEAD, RWKV_HEAD), 0.3),
        'state_pool': nrm((L, DEC_BATCH, POOL_BUF, D_POOL), 1.0),
        'p_prompt': nrm((L, BATCH, SEQ, PLE_DIM), 1.0),
        'p_sample': nrm((L, DEC_BATCH, DEC_SEQ, PLE_DIM), 1.0),
        'norm_mix_g': 1.0 + nrm((L, D_MODEL), 0.02),
        'w_in': nrm((L, D_MODEL, D_IN), D_MODEL ** -0.5),
        'shift_mu': unif((L, D_SHIFT), 0.0, 1.0),
        'decay_w0': unif((L, D_RWKV), -4.0, 0.0),
        'decay_b': nrm((L, DECAY_LORA, D_RWKV), 0.1 * DECAY_LORA ** -0.5),
        'a_0': nrm((L, D_RWKV), 0.1),
        'a_b': nrm((L, AAA_LORA, D_RWKV), 0.5 * AAA_LORA ** -0.5),
        'g_b': nrm((L, GATE_LORA, D_RWKV), GATE_LORA ** -0.5),
        'k_k': 0.85 + nrm((L, D_RWKV), 0.05),
        'k_a': 1.0 + nrm((L, D_RWKV), 0.05),
        'r_k': nrm((L, N_RWKV_HEADS, RWKV_HEAD), 0.1),
        'lnx_g': 1.0 + nrm((L, D_RWKV), 0.02),
        'lnx_b': nrm((L, D_RWKV), 0.01),
        'pool_w': nrm((L, N_POOL_GROUPS, POOL_GROUP, POOL_GROUP), POOL_GROUP ** -0.5),
        'pool_scale': 1.0 + nrm((L, D_POOL), 0.02),
        'w_out': nrm((L, D_MODEL, D_MODEL), 0.5 * D_MODEL ** -0.5),
        'norm_ffn_g': 1.0 + nrm((L, D_MODEL), 0.02),
        'peer_wq': nrm((L, D_MODEL, PEER_HEADS * PEER_DK), D_MODEL ** -0.5),
        'peer_keys': nrm((L, 2, N_KEYS, PEER_DK_HALF), PEER_DK_HALF ** -0.5),
        'peer_u': nrm((L, N_EXPERTS, D_MODEL), D_MODEL ** -0.5),
        'peer_v': nrm((L, N_EXPERTS, D_MODEL), 0.2),
        'norm_ple_g': 1.0 + nrm((L, D_MODEL), 0.02),
        'ple_w': nrm((L, PLE_DIM, D_MODEL), PLE_DIM ** -0.5),
        'ple_gate_w': nrm((L, D_MODEL, D_MODEL), D_MODEL ** -0.5),
        'final_norm_g': 1.0 + nrm((D_MODEL,), 0.02),
    }


def reference(x_prompt, x_sample, state_shift, state_wkv, state_pool, p_prompt, p_sample,
              norm_mix_g, w_in, shift_mu, decay_w0, decay_b, a_0, a_b, g_b, k_k, k_a, r_k,
              lnx_g, lnx_b, pool_w, pool_scale, w_out, norm_ffn_g, peer_wq, peer_keys,
              peer_u, peer_v, norm_ple_g, ple_w, ple_gate_w, final_norm_g):
    hp, hs = x_prompt, x_sample
    sh_p, wk_p, po_p, sh_s, wk_s, po_s = [], [], [], [], [], []
    for i in range(DEPTH):
        lw = dict(norm_mix_g=norm_mix_g[i], w_in=w_in[i], shift_mu=shift_mu[i], decay_w0=decay_w0[i],
                  decay_b=decay_b[i], a_0=a_0[i], a_b=a_b[i], g_b=g_b[i], k_k=k_k[i], k_a=k_a[i],
                  r_k=r_k[i], lnx_g=lnx_g[i], lnx_b=lnx_b[i], pool_w=pool_w[i], pool_scale=pool_scale[i],
                  w_out=w_out[i], norm_ffn_g=norm_ffn_g[i], peer_wq=peer_wq[i], peer_keys=peer_keys[i],
                  peer_u=peer_u[i], peer_v=peer_v[i], norm_ple_g=norm_ple_g[i], ple_w=ple_w[i],
                  ple_gate_w=ple_gate_w[i])
        hp, s1, s2, s3 = hybrid_layer(
            hp, p_prompt[i], 0,
            jnp.zeros((BATCH, D_SHIFT), hp.dtype),
            jnp.zeros((BATCH, N_RWKV_HEADS, RWKV_HEAD, RWKV_HEAD), jnp.float32),
            jnp.zeros((BATCH, POOL_BUF, D_POOL), hp.dtype), lw)
        sh_p.append(s1)
        wk_p.append(s2.astype(state_wkv.dtype))
        po_p.append(s3)
        hs, t1, t2, t3 = hybrid_layer(hs, p_sample[i], PAST_LEN, state_shift[i], state_wkv[i], state_pool[i], lw)
        sh_s.append(t1)
        wk_s.append(t2.astype(state_wkv.dtype))
        po_s.append(t3)
    y_prompt = rmsnorm(hp, final_norm_g)
    y_sample = rmsnorm(hs, final_norm_g)
    return (y_prompt, y_sample, jnp.stack(sh_p), jnp.stack(wk_p), jnp.stack(po_p),
            jnp.stack(sh_s), jnp.stack(wk_s), jnp.stack(po_s))
```

```python
import numpy as np
import ml_dtypes
from contextlib import ExitStack
import concourse.bass as bass
import concourse.mybir as mybir
from concourse.bass_utils import run_bass_kernel_spmd

F32 = mybir.dt.float32
BF16 = mybir.dt.bfloat16
ALU = mybir.AluOpType
AF = mybir.ActivationFunctionType
AX = mybir.AxisListType

NCORES = 8
D = 1024
SEQ = 2048
NS = 64
NTOK = SEQ + NS
DSH = 1792
DIN = 2304
NE = 16384
SAME_SYNC = True
EPOCH = 20000
NSLOT = 24
TC = 16


class Buf:
    __slots__ = ("name", "w", "r")

    def __init__(self, name):
        self.name = name
        self.w = None
        self.r = []


class Sched:
    ENG = ["tensor", "vector", "scalar", "gpsimd", "sync"]

    def __init__(self, sems):
        self.free = list(sems)
        self.ops = {e: [] for e in self.ENG}
        self.cur = {e: [self.free.pop(), 0] for e in ["tensor", "vector", "scalar", "gpsimd"]}
        self.seen = {e: {} for e in self.ENG}
        self.slots = [[self.free.pop(), 0] for _ in range(NSLOT)]
        self.rr = 0
        self.pending = {e: [] for e in self.ENG}

    def _need(self, eng, tok, waits, strict=True):
        if tok is None:
            return
        sem, val, src = tok
        if src == eng and (eng == "tensor" or not SAME_SYNC or not (strict or eng == "scalar")):
            return
        k = id(sem)
        if self.seen[eng].get(k, 0) >= val:
            return
        self.seen[eng][k] = val
        waits.append((sem, val))

    def op(self, eng, fn, reads=(), writes=(), dma=False, strict=True):
        waits = self.pending[eng]
        self.pending[eng] = []
        strict = True
        for b in reads:
            self._need(eng, b.w, waits, strict)
        for b in writes:
            self._need(eng, b.w, waits, strict)
            for t in b.r:
                self._need(eng, t, waits, strict)
        if dma:
            slot = self.slots[self.rr]
            self.rr = (self.rr + 1) % NSLOT
            if slot[1] > 0:
                self._need(eng, (slot[0], slot[1], "dma"), waits)
            if slot[1] + 16 > EPOCH * 2:
                slot[0] = self.free.pop()
                slot[1] = 0
            slot[1] += 16
            tok = (slot[0], slot[1], "dma")
            inc = 16
        else:
            c = self.cur[eng]
            if c[1] >= EPOCH:
                c[0] = self.free.pop()
                c[1] = 0
            c[1] += 1
            tok = (c[0], c[1], eng)
            inc = 1
        self.ops[eng].append((waits, fn, tok[0], inc))
        for b in reads:
            b.r.append(tok)
        for b in writes:
            b.w = tok
            b.r = []
        return tok

    def barrier(self):
        toks = [(c[0], c[1], e) for e, c in self.cur.items() if c[1] > 0]
        toks += [(sl[0], sl[1], "dma") for sl in self.slots if sl[1] > 0]
        for e in self.ENG:
            for t in toks:
                if t[2] == e:
                    continue
                k = id(t[0])
                if self.seen[e].get(k, 0) >= t[1]:
                    continue
                self.seen[e][k] = t[1]
                self.pending[e].append((t[0], t[1]))

    def final_waits(self):
        return [(s[0], s[1]) for s in self.slots if s[1] > 0]


def build_nc():
    nc = bass.Bass("TRN2", target_bir_lowering=False)

    def din(name, shape, dt=F32):
        return nc.dram_tensor(name, list(shape), dt, kind="ExternalInput").ap()

    def dout(name, shape):
        return nc.dram_tensor(name, list(shape), F32, kind="ExternalOutput").ap()

    def dscr(name, shape, dt=F32):
        return nc.dram_tensor(name, list(shape), dt, kind="Internal").ap()

    x_p = din("x_p", [SEQ, D]); x_s = din("x_s", [NS, D])
    st_shift = din("st_shift", [16, DSH]); st_wkv = din("st_wkv", [128, 4096]); st_pool = din("st_pool", [16, 15, 512])
    p_p = din("p_p", [SEQ, 256]); p_s = din("p_s", [NS, 256])
    w_in = din("w_in", [D, DIN]); w_out = din("w_out", [D, D]); wq = din("wq", [D, 2048])
    ple_w = din("ple_w", [256, D]); ple_gw = din("ple_gw", [D, D])
    decay_b = din("decay_b", [64, 512]); a_b = din("a_b", [64, 512]); g_b = din("g_b", [128, 512])
    pool_w = din("pool_w", [4, 128, 128]); keys = din("keys", [2, 128, 128])
    peer_u = din("peer_u", [NE, D]); peer_v = din("peer_v", [NE, D])
    gcols = din("gcols", [128, 24])
    rowv = din("rowv", [1, 6912])
    ident_in = din("ident", [128, 128])
    pool_rc = din("pool_rc", [2, 128, 4])
    cmask = din("cmask", [128, 4, 128])
    y_p = dout("y_p", [SEQ, D]); y_s = dout("y_s", [NS, D])
    o_shp = dout("o_shp", [1, DSH]); o_wkp = dout("o_wkp", [128, 256]); o_pop = dout("o_pop", [15, 512])
    o_shs = dout("o_shs", [16, DSH]); o_wks = dout("o_wks", [128, 4096]); o_pos = dout("o_pos", [16, 15, 512])
    Zp = dscr("Zp", [SEQ + 1, DSH]); Ep = dscr("Ep", [SEQ + 15, 512])
    Zs = dscr("Zs", [NS, DSH]); Zsp = dscr("Zsp", [NS, DSH]); Us = dscr("Us", [NS, 512])
    Es = dscr("Es", [16, 19, 512]); Esf = dscr("Esf", [16, NS, 512])
    SC = dscr("SC", [NTOK, 8, 8, 64]); BN = dscr("BN", [NTOK, 8]); OS = dscr("OS", [NTOK, 512])
    UT = dscr("UT", [8, 128, NE], BF16); VB = dscr("VB", [NE, D], BF16)
    WOs = dscr("WOs", [128, 8, D], BF16); WQs = dscr("WQs", [128, 8, 2048], BF16); WGs = dscr("WGs", [128, 8, D], BF16)

    es = ExitStack()
    sems = [es.enter_context(nc.semaphore(f"s{i}")) for i in range(100)]
    S = Sched(sems)

    cur = [es]

    class T:
        def __init__(self, name, shape, dt=F32):
            self.h = cur[0].enter_context(nc.sbuf_tensor(name, list(shape), dt))
            self.a = self.h.ap()
            self.b = Buf(name)

    class PS:
        def __init__(self, name):
            self.h = es.enter_context(nc.psum_tensor(name, [128, 512], F32))
            self.a = self.h.ap()
            self.bf = self.a.bitcast(BF16)
            self.b = Buf(name)

    banks = [PS(f"ps{i}") for i in range(8)]
    bank_i = [0]
    nbank = [8]

    def bank():
        b = banks[bank_i[0] % nbank[0]]
        bank_i[0] += 1
        return b

    ARENA_W = 43000
    arena = es.enter_context(nc.sbuf_tensor("arena", [128, ARENA_W], F32)).ap()
    apos = [0]

    class A:
        def __init__(self, name, shape, dt=F32):
            n = 1
            for d_ in shape[1:]:
                n *= d_
            words = (n * (2 if dt == BF16 else 4) + 3) // 4
            words = (words + 15) // 16 * 16
            assert apos[0] + words <= ARENA_W, (name, apos[0], words)
            v = arena[:, apos[0]:apos[0] + words]
            apos[0] += words
            if dt == BF16:
                v = v.bitcast(BF16)
            v = v[:, 0:n]
            if len(shape) == 3:
                v = v.rearrange("p (a b) -> p a b", a=shape[1])
            elif len(shape) == 4:
                v = v.rearrange("p (a b c) -> p a b c", a=shape[1], b=shape[2])
            self.a = v[0:shape[0]]
            self.b = Buf(name)

    def arena_reset(mark):
        S.barrier()
        apos[0] = mark

    def tt(out, in0, in1, op, r, w, eng="vector"):
        S.op(eng, lambda e: e.tensor_tensor(out=out, in0=in0, in1=in1, op=op), r, w)

    def ts(out, in0, s1, s2, op0, op1, r, w, eng="vector"):
        st = not (isinstance(s1, (int, float)) and (s2 is None or isinstance(s2, (int, float))))
        if s2 is None:
            S.op(eng, lambda e: e.tensor_scalar(out=out, in0=in0, scalar1=s1, scalar2=None, op0=op0), r, w, strict=st)
        else:
            S.op(eng, lambda e: e.tensor_scalar(out=out, in0=in0, scalar1=s1, scalar2=s2, op0=op0, op1=op1), r, w, strict=st)

    def stt(out, in0, scalar, in1, op0, op1, r, w, eng="vector"):
        S.op(eng, lambda e: e.scalar_tensor_tensor(out=out, in0=in0, scalar=scalar, in1=in1, op0=op0, op1=op1), r, w,
             strict=not isinstance(scalar, (int, float)))

    def cp(out, in_, r, w, eng="vector"):
        if eng == "scalar":
            S.op(eng, lambda e: e.activation(out=out, in_=in_, func=AF.Copy), r, w)
        else:
            S.op(eng, lambda e: e.tensor_copy(out=out, in_=in_), r, w)

    def act(out, in_, func, r, w, bias=None, scale=None, accum=None):
        kw = {}
        if bias is not None:
            kw["bias"] = bias
        if scale is not None:
            kw["scale"] = scale
        if accum is not None:
            kw["accum_out"] = accum
        S.op("scalar", lambda e: e.activation(out=out, in_=in_, func=func, **kw), r, w)

    def rsqrt(ap, buf):
        act(ap, ap, AF.Sqrt, [buf], [buf])
        S.op("vector", lambda e: e.reciprocal(out=ap, in_=ap), [buf], [buf])

    def sumsq(src, nt):
        tt(junk.a[:nt], src.a[:nt], src.a[:nt], ALU.mult, [src.b], [junk.b])
        red(ssq.a[:nt], junk.a[:nt], [junk.b], [ssq.b])

    def red(out, in_, r, w, op=ALU.add):
        S.op("vector", lambda e: e.tensor_reduce(out=out, in_=in_, axis=AX.X, op=op), r, w)

    def mm(out, lhsT, rhs, start, stop, r, w):
        S.op("tensor", lambda e: e.matmul(out, lhsT, rhs, start=start, stop=stop), r, w)

    def tr(out, in_, ident, r, w):
        S.op("tensor", lambda e: e.transpose(out, in_, ident), r, w)

    dq = [0]

    def dma(out, in_, r, w, q=None):
        if q is None:
            q = "sync" if dq[0] % 2 == 0 else "gpsimd"
            dq[0] += 1
        S.op(q, lambda e: e.dma_start(out=out, in_=in_), r, w, dma=True)

    def memset(ap, val, w):
        S.op("vector", lambda e: e.memset(ap, val), [], w)

    ident_f = T("ident_f", [128, 128]); ident = T("ident_b", [128, 128], BF16)
    dma(ident_f.a, ident_in, [], [ident_f.b])
    cp(ident.a, ident_f.a, [ident_f.b], [ident.b])
    gc = T("gc", [128, 24])
    dma(gc.a, gcols, [], [gc.b])
    rc = T("rc", [128, 2, 4])
    dma(rc.a, pool_rc.rearrange("a p g -> p a g"), [], [rc.b])

    def bc_row(name, off, n, cls=T):
        t = cls(name, [128, n])
        dma(t.a, rowv[0:1, off:off + n].broadcast_to([128, n]), [], [t.b])
        return t

    lg_bc = bc_row("lg_bc", 4352, 512); lb_bc = bc_row("lb_bc", 4864, 512); psc_bc = bc_row("psc_bc", 5376, 512)
    fg_bc = bc_row("fg_bc", 5888, 1024)

    xt = T("xt", [128, D]); junk = T("junk", [128, D]); xnb = T("xnb", [128, D], BF16)
    ssq = T("ssq", [128, 1]); rstd = T("rstd", [128, 1])
    t1 = T("t1", [128, 512]); t2 = T("t2", [128, 512])
    h8 = T("h8", [128, 8]); rn8 = T("rn8", [128, 8]); bn8 = T("bn8", [128, 8])
    win = A("win", [128, 8, DIN], BF16)
    mark1 = apos[0]
    zero = A("zero", [128, DSH])
    memset(zero.a, 0.0, [zero.b])
    stg = [A("stg0", [128, 4096]), A("stg1", [128, 4096])]
    stb = [A("stb0", [128, 4096], BF16), A("stb1", [128, 4096], BF16)]
    sti = [0]

    def stage():
        i = sti[0] % 2
        sti[0] += 1
        return stg[i], stb[i]

    decb = T("decb", [64, 512], BF16); abb = T("abb", [128, 512], BF16); gbb = T("gbb", [128, 512], BF16)
    plw = T("plw", [128, 2, D], BF16); poolw = T("poolw", [128, 4, 128], BF16); keysT = T("keysT", [128, 2, 128], BF16)
    sf, sb = stage()
    dma(sf.a[0:64, 0:512], decay_b, [], [sf.b])
    cp(decb.a, sf.a[0:64, 0:512], [sf.b], [decb.b])
    sf, sb = stage()
    dma(sf.a[64:128, 0:512], a_b, [], [sf.b])
    cp(abb.a[64:128, :], sf.a[64:128, 0:512], [sf.b], [abb.b])
    sf, sb = stage()
    dma(sf.a[:, 0:512], g_b, [], [sf.b])
    cp(gbb.a, sf.a[:, 0:512], [sf.b], [gbb.b])
    sf, sb = stage()
    dma(sf.a[:, 0:2048].rearrange("p (k n) -> p k n", k=2), ple_w.rearrange("(k p) n -> p k n", p=128), [], [sf.b])
    cp(plw.a, sf.a[:, 0:2048].rearrange("p (k n) -> p k n", k=2), [sf.b], [plw.b])
    sf, sb = stage()
    dma(sf.a[:, 0:512].rearrange("p (g d) -> p g d", g=4), pool_w.rearrange("g c d -> c g d"), [], [sf.b])
    cp(poolw.a, sf.a[:, 0:512].rearrange("p (g d) -> p g d", g=4), [sf.b], [poolw.b])
    sf, sb = stage()
    dma(sf.a[:, 0:256].rearrange("p (s c) -> p s c", s=2), keys.rearrange("s n c -> n s c"), [], [sf.b])
    cp(sb.a[:, 0:256], sf.a[:, 0:256], [sf.b], [sb.b])
    pb = bank()
    for s_ in range(2):
        tr(pb.bf[:, s_ * 128:(s_ + 1) * 128], sb.a[:, s_ * 128:(s_ + 1) * 128], ident.a, [sb.b, ident.b], [pb.b])
    cp(keysT.a.rearrange("p s n -> p (s n)"), pb.bf[:, 0:256], [pb.b], [keysT.b])

    for kd in range(8):
        sf, sb = stage()
        dma(sf.a[:, 0:DIN], w_in[kd * 128:(kd + 1) * 128, :], [], [sf.b])
        ts(win.a[:, kd, :], sf.a[:, 0:DIN], gc.a[:, kd:kd + 1], None, ALU.mult, None, [sf.b, gc.b], [win.b])
    wscr_b = {"WO": Buf("WOs"), "WQ": Buf("WQs"), "WG": Buf("WGs")}
    for (src, dst, n, goff, key) in ((w_out, WOs, D, None, "WO"), (wq, WQs, 2048, 8, "WQ"), (ple_gw, WGs, D, 16, "WG")):
        for kd in range(8):
            for c0 in range(0, n, 1024):
                sf, sb = stage()
                dma(sf.a[:, 0:1024], src[kd * 128:(kd + 1) * 128, c0:c0 + 1024], [], [sf.b])
                if goff is None:
                    cp(sb.a[:, 0:1024], sf.a[:, 0:1024], [sf.b], [sb.b])
                else:
                    ts(sb.a[:, 0:1024], sf.a[:, 0:1024], gc.a[:, goff + kd:goff + kd + 1], None, ALU.mult, None,
                       [sf.b, gc.b], [sb.b])
                dma(dst[:, kd, c0:c0 + 1024], sb.a[:, 0:1024], [sb.b], [wscr_b[key]])

    ut_b = Buf("UT"); vb_b = Buf("VB")
    for bt in range(NE // 512):
        e0 = bt * 512
        sf, sb = stage()
        dma(sf.a.rearrange("p (a d) -> p a d", a=4), peer_u[e0:e0 + 512, :].rearrange("(a p) d -> p a d", p=128), [], [sf.b])
        cp(sb.a, sf.a, [sf.b], [sb.b])
        sf2, utsb = stage()
        for a in range(4):
            pb = bank()
            for kd in range(8):
                tr(pb.bf[:, kd * 128:(kd + 1) * 128], sb.a[:, a * 1024 + kd * 128:a * 1024 + (kd + 1) * 128], ident.a,
                   [sb.b, ident.b], [pb.b])
            tt(utsb.a.rearrange("p (k e) -> p k e", k=8)[:, :, a * 128:(a + 1) * 128],
               pb.bf.rearrange("p (k e) -> p k e", k=8),
               gc.a[:, 8:16].unsqueeze(2).broadcast_to([128, 8, 128]), ALU.mult, [pb.b, gc.b], [utsb.b])
        dma(UT[:, :, e0:e0 + 512].rearrange("k p e -> p k e"), utsb.a.rearrange("p (k e) -> p k e", k=8), [utsb.b], [ut_b])
        sf, sb = stage()
        dma(sf.a.rearrange("p (a d) -> p a d", a=4), peer_v[e0:e0 + 512, :].rearrange("(a p) d -> p a d", p=128), [], [sf.b])
        cp(sb.a, sf.a, [sf.b], [sb.b], eng="scalar")
        dma(VB[e0:e0 + 512, :].rearrange("(a p) d -> p a d", p=128), sb.a.rearrange("p (a d) -> p a d", a=4), [sb.b], [vb_b])

    ntile = 17
    sc_b = [Buf(f"SC{i}") for i in range(ntile)]
    z_b = [Buf(f"Z{i}") for i in range(ntile)]
    e_b = [Buf(f"E{i}") for i in range(ntile)]
    zinit = Buf("zinit")
    dma(Zp[0:1, :], zero.a[0:1, :], [zero.b], [zinit])
    dma(Ep[0:15, :], zero.a[0:15, 0:512], [zero.b], [zinit])
    arena_reset(mark1)

    mu_bc = bc_row("mu_bc", 0, DSH, A)
    w0_bc = bc_row("w0_bc", 1792, 512, A); a0_bc = bc_row("a0_bc", 2304, 512, A)
    kk_bc = bc_row("kk_bc", 2816, 512, A); ka_bc = bc_row("ka_bc", 3328, 512, A); rk_bc = bc_row("rk_bc", 3840, 512, A)
    xnT = A("xnT", [128, 8, 128], BF16)
    zsb = A("zsb", [128, DIN]); zpv = A("zpv", [128, DSH]); zs = A("zs", [128, DSH])
    Lb = A("Lb", [128, 256], BF16); LT = A("LT", [128, 2, 128], BF16)
    scst = A("scst", [128, 8, 8, 64]); aa = A("aa", [128, 512])

    def rms_T(src, nt, dstT):
        sumsq(src, nt)
        ts(rstd.a[:nt], ssq.a[:nt], 1.0 / D, 1e-6, ALU.mult, ALU.add, [ssq.b], [rstd.b])
        rsqrt(rstd.a[:nt], rstd.b)
        ts(xnb.a[:nt], src.a[:nt], rstd.a[:nt, 0:1], None, ALU.mult, None, [src.b, rstd.b], [xnb.b])
        pb = bank()
        for kd in range(8):
            tr(pb.bf[:, kd * 128:kd * 128 + nt], xnb.a[:nt, kd * 128:(kd + 1) * 128], ident.a[:nt, :nt], [xnb.b, ident.b], [pb.b])
        cp(dstT.a[:, :, :nt], pb.bf.rearrange("p (k t) -> p k t", k=8)[:, :, :nt], [pb.b], [dstT.b], eng="scalar")

    def v3(ap):
        return ap.rearrange("p (h n) -> p h n", h=8)

    def phase1(ti):
        samp = ti == 16
        nt = NS if samp else 128
        t0 = ti * 128
        dma(xt.a[:nt], x_s if samp else x_p[t0:t0 + nt, :], [], [xt.b])
        rms_T(xt, nt, xnT)
        for c0 in range(0, DIN, 512):
            n = min(512, DIN - c0)
            pb = bank()
            for kd in range(8):
                mm(pb.a[:nt, :n], xnT.a[:, kd, :nt], win.a[:, kd, c0:c0 + n], kd == 0, kd == 7, [xnT.b, win.b], [pb.b])
            cp(zsb.a[:nt, c0:c0 + n], pb.a[:nt, :n], [pb.b], [zsb.b], eng="scalar" if (c0 // 512) % 2 else "vector")
        if not samp:
            dma(Zp[1 + t0:1 + t0 + nt, :], zsb.a[:nt, 0:DSH], [zsb.b], [z_b[ti]])
            dma(Ep[15 + t0:15 + t0 + nt, :], zsb.a[:nt, DSH:DIN], [zsb.b], [e_b[ti]])
            rd = [z_b[ti], zinit] + ([z_b[ti - 1]] if ti > 0 else [])
            dma(zpv.a[:nt], Zp[t0:t0 + nt, :], rd, [zpv.b])
        else:
            dma(Zs, zsb.a[:nt, 0:DSH], [zsb.b], [z_b[ti]])
            dma(Us, zsb.a[:nt, DSH:DIN], [zsb.b], [e_b[ti]])
            zq = Buf("zsp")
            dma(Zsp.rearrange("(s t) f -> s t f", t=4)[:, 0, :], st_shift, [], [zq], q="sync")
            dma(Zsp.rearrange("(s t) f -> s t f", t=4)[:, 1:4, :], Zs.rearrange("(s t) f -> s t f", t=4)[:, 0:3, :],
                [z_b[ti]], [zq], q="sync")
            dma(zpv.a[:nt], Zsp, [zq], [zpv.b], q="sync")
            eq = Buf("es")
            dma(Es[:, 0:15, :], st_pool, [], [eq], q="sync")
            dma(Es[:, 15:19, :], Us.rearrange("(s t) c -> s t c", t=4), [e_b[ti]], [eq], q="sync")
            eq2 = Buf("esf")
            for k in range(16):
                dma(Esf[k].rearrange("(s t) c -> s t c", t=4), Es[:, 15 - k:19 - k, :], [eq], [eq2], q="sync")
            e_b[ti] = eq2
            dma(o_shs, Zs.rearrange("(s t) f -> s t f", t=4)[:, 3, :], [z_b[ti]], [Buf("o")], q="sync")
            dma(o_pos, Es[:, 4:19, :], [eq], [Buf("o")], q="sync")
        tt(zs.a[:nt], zpv.a[:nt], zsb.a[:nt, 0:DSH], ALU.subtract, [zpv.b, zsb.b], [zs.b])
        tt(zs.a[:nt], zs.a[:nt], mu_bc.a[:nt], ALU.mult, [zs.b, mu_bc.b], [zs.b])
        tt(zs.a[:nt], zs.a[:nt], zsb.a[:nt, 0:DSH], ALU.add, [zs.b, zsb.b], [zs.b])
        r_ = zs.a[:nt, 0:512]; k_ = zs.a[:nt, 512:1024]; v_ = zs.a[:nt, 1024:1536]
        sq = lambda q: scst.a[:nt, :, q, :]
        act(Lb.a[:nt, 0:64], zs.a[:nt, 1536:1600], AF.Tanh, [zs.b], [Lb.b])
        cp(Lb.a[:nt, 64:128], zs.a[:nt, 1600:1664], [zs.b], [Lb.b])
        act(Lb.a[:nt, 128:256], zs.a[:nt, 1664:1792], AF.Sigmoid, [zs.b], [Lb.b])
        pb = bank()
        for j in range(2):
            tr(pb.bf[:, j * 128:j * 128 + nt], Lb.a[:nt, j * 128:(j + 1) * 128], ident.a[:nt, :nt], [Lb.b, ident.b], [pb.b])
        cp(LT.a[:, :, :nt], pb.bf[:, 0:256].rearrange("p (k t) -> p k t", k=2)[:, :, :nt], [pb.b], [LT.b])
        pd = bank(); pa = bank(); pg = bank()
        mm(pd.a[:nt], LT.a[0:64, 0, :nt], decb.a[0:64, :], True, True, [LT.b, decb.b], [pd.b])
        mm(pa.a[:nt], LT.a[64:128, 0, :nt], abb.a[64:128, :], True, True, [LT.b, abb.b], [pa.b])
        mm(pg.a[:nt], LT.a[:, 1, :nt], gbb.a, True, True, [LT.b, gbb.b], [pg.b])
        tt(t1.a[:nt], pd.a[:nt], w0_bc.a[:nt], ALU.add, [pd.b, w0_bc.b], [t1.b])
        act(t1.a[:nt], t1.a[:nt], AF.Sigmoid, [t1.b], [t1.b])
        act(sq(1), v3(t1.a[:nt]), AF.Exp, [t1.b], [scst.b], scale=-0.6065306597126334)
        ts(sq(7), v3(t1.a[:nt]), -0.6065306597126334, None, ALU.mult, None, [t1.b], [scst.b])
        tt(aa.a[:nt], pa.a[:nt], a0_bc.a[:nt], ALU.add, [pa.b, a0_bc.b], [aa.b])
        act(aa.a[:nt], aa.a[:nt], AF.Sigmoid, [aa.b], [aa.b])
        cp(sq(6), v3(pg.a[:nt]), [pg.b], [scst.b], eng="scalar")
        cp(sq(0), v3(r_), [zs.b], [scst.b])
        cp(sq(5), v3(v_), [zs.b], [scst.b], eng="scalar")
        tt(t1.a[:nt], k_, kk_bc.a[:nt], ALU.mult, [zs.b, kk_bc.b], [t1.b])
        tt(t2.a[:nt], t1.a[:nt], t1.a[:nt], ALU.mult, [t1.b], [t2.b])
        red(h8.a[:nt], v3(t2.a[:nt]), [t2.b], [h8.b])
        ts(rn8.a[:nt], h8.a[:nt], 1e-24, None, ALU.add, None, [h8.b], [rn8.b])
        rsqrt(rn8.a[:nt], rn8.b)
        ts(rn8.a[:nt], rn8.a[:nt], -1.0, None, ALU.mult, None, [rn8.b], [rn8.b])
        tt(sq(3), v3(t1.a[:nt]), rn8.a[:nt].unsqueeze(2).broadcast_to([nt, 8, 64]), ALU.mult,
           [t1.b, rn8.b], [scst.b])
        stt(sq(4), sq(3), -1.0, v3(aa.a[:nt]), ALU.mult, ALU.mult, [scst.b, aa.b], [scst.b])
        stt(t1.a[:nt], aa.a[:nt], -1.0, ka_bc.a[:nt], ALU.add, ALU.mult, [aa.b, ka_bc.b], [t1.b])
        stt(sq(2), v3(t1.a[:nt]), 1.0, v3(k_), ALU.add, ALU.mult, [t1.b, zs.b], [scst.b])
        tt(v3(t2.a[:nt]), v3(r_), sq(2), ALU.mult, [zs.b, scst.b], [t2.b])
        tt(t2.a[:nt], t2.a[:nt], rk_bc.a[:nt], ALU.mult, [t2.b, rk_bc.b], [t2.b])
        red(bn8.a[:nt], v3(t2.a[:nt]), [t2.b], [bn8.b])
        tk0 = SEQ if samp else t0
        dma(SC.rearrange("t h q j -> t (h q j)")[tk0:tk0 + nt], scst.a[:nt].rearrange("p h q j -> p (h q j)"), [scst.b], [sc_b[ti]])
        dma(BN[tk0:tk0 + nt, :], bn8.a[:nt], [bn8.b], [sc_b[ti]])

    for ti in range(ntile):
        phase1(ti)
    dma(o_shp, Zp[SEQ:SEQ + 1, :], [z_b[15]], [Buf("o")], q="sync")
    dma(o_pop, Ep[SEQ:SEQ + 15, :], [e_b[15]], [Buf("o")], q="sync")

    os_b = [Buf(f"OS{i}") for i in range(ntile)]
    arena_reset(0)
    cm = A("cm", [128, 4, 128])
    dma(cm.a, cmask, [], [cm.b])
    mk2 = A("mk2", [128, 4, 128])
    for i_ in range(4):
        cp(mk2.a[:, i_, :], cm.a[:, i_ % 2, :], [cm.b], [mk2.b])
    sct = [A("sct0", [128, 8, 8, 64]), A("sct1", [128, 8, 8, 64])]
    lwc = A("lwc", [128, 512]); cum = A("cum", [128, 512]); tmpc = A("tmpc", [128, 512])
    Epl = A("Epl", [128, 512]); Emi = A("Emi", [128, 512]); Eprv = A("Eprv", [128, 512]); Ehat = A("Ehat", [128, 512])
    gamT = A("gamT", [128, 4])
    At = A("At", [128, 512]); Rt = A("Rt", [128, 512]); Bt = A("Bt", [128, 512]); Kt = A("Kt", [128, 512])
    Bh = A("Bh", [128, 512]); Kh = A("Kh", [128, 512]); Vc = A("Vc", [128, 512])
    ART = A("ART", [128, 4, 2, 128]); BT = A("BT", [128, 4, 128]); KT = A("KT", [128, 4, 128])
    ARTz = [A("ARTz0", [128, 4, 2, 128]), A("ARTz1", [128, 4, 2, 128])]
    BTz = [A("BTz0", [128, 4, 128]), A("BTz1", [128, 4, 128])]
    Hz = [A("Hz0", [128, 4, 64]), A("Hz1", [128, 4, 64])]
    for z_ in ARTz + BTz + Hz:
        memset(z_.a, 0.0, [z_.b])
    G = A("G", [128, 8, 512])
    Yb = [A("Yb0", [128, 8, 128], BF16), A("Yb1", [128, 8, 128], BF16)]
    Lb2 = [A("Lb0", [128, 8, 128]), A("Lb1", [128, 8, 128], BF16), A("Lb2", [128, 8, 128], BF16)]
    TTm = A("TTm", [128, 8, 128]); TLm = A("TLm", [128, 8, 128])
    TTb = A("TTb", [128, 8, 128], BF16); TLb = A("TLb", [128, 8, 128], BF16)
    Wsb = A("Wsb", [128, 512]); Usb = A("Usb", [128, 512]); Osb = [A("Osb0", [128, 512]), A("Osb1", [128, 512])]
    Hst = A("Hst", [128, 4, 64]); Hs2 = A("Hs2", [128, 4, 64])
    ones_col = cm.a[:, 3, 0:1]
    mcol = [cm.a[:, 0, 64:65], cm.a[:, 2, 63:64]]
    evi = [0]

    def evac(out, in_, r, w):
        evi[0] += 1
        cp(out, in_, r, w, eng="scalar" if evi[0] % 2 else "vector")

    import os
    LVL = int(os.environ.get("KLVL", "9"))
    for ci in range(SEQ // 128):
        t0 = ci * 128
        sc_ = sct[ci % 2]
        dma(sc_.a.rearrange("p h q j -> p (h q j)"), SC.rearrange("t h q j -> t (h q j)")[t0:t0 + 128], [sc_b[ci]], [sc_.b])
        X = lambda q: sc_.a[:, :, q, :]
        cp(v3(lwc.a), X(7), [sc_.b], [lwc.b])
        cp(v3(Vc.a), X(5), [sc_.b], [Vc.b], eng="scalar")
        pc = bank(); pt = bank(); pg = bank()
        mm(pc.a, cm.a[:, 1, :], lwc.a, True, True, [cm.b, lwc.b], [pc.b])
        mm(pt.a, cm.a[:, 3, :], lwc.a, True, True, [cm.b, lwc.b], [pt.b])
        for hp in range(4):
            mm(pg.a[:, hp:hp + 1], lwc.a[:, hp * 128:(hp + 1) * 128], ones_col, True, True, [lwc.b, cm.b], [pg.b])
        cp(cum.a, pc.a, [pc.b], [cum.b])
        act(Epl.a, cum.a, AF.Exp, [cum.b], [Epl.b])
        act(Emi.a, cum.a, AF.Exp, [cum.b], [Emi.b], scale=-1.0)
        tt(tmpc.a, cum.a, lwc.a, ALU.subtract, [cum.b, lwc.b], [tmpc.b])
        act(Eprv.a, tmpc.a, AF.Exp, [tmpc.b], [Eprv.b])
        tt(tmpc.a, pt.a, cum.a, ALU.subtract, [pt.b, cum.b], [tmpc.b])
        act(Ehat.a, tmpc.a, AF.Exp, [tmpc.b], [Ehat.b])
        act(gamT.a, pg.a[:, 0:4], AF.Exp, [pg.b], [gamT.b])
        tt(v3(At.a), X(3), v3(Eprv.a), ALU.mult, [sc_.b, Eprv.b], [At.b])
        tt(v3(Bt.a), X(4), v3(Emi.a), ALU.mult, [sc_.b, Emi.b], [Bt.b])
        tt(v3(Kt.a), X(2), v3(Emi.a), ALU.mult, [sc_.b, Emi.b], [Kt.b])
        tt(v3(Rt.a), X(0), v3(Epl.a), ALU.mult, [sc_.b, Epl.b], [Rt.b])
        tt(v3(Bh.a), X(4), v3(Ehat.a), ALU.mult, [sc_.b, Ehat.b], [Bh.b])
        tt(v3(Kh.a), X(2), v3(Ehat.a), ALU.mult, [sc_.b, Ehat.b], [Kh.b])
        for (src_, dst_) in ((At, ART.a[:, :, 0, :]), (Rt, ART.a[:, :, 1, :]), (Bt, BT.a), (Kt, KT.a)):
            pb = bank()
            for hp in range(4):
                tr(pb.a[:, hp * 128:(hp + 1) * 128], src_.a[:, hp * 128:(hp + 1) * 128], ident_f.a, [src_.b, ident_f.b], [pb.b])
            dbuf = ART.b if src_ in (At, Rt) else (BT.b if src_ is Bt else KT.b)
            p3_ = pb.a.rearrange("p (a t) -> p a t", a=4)
            evac(dst_, p3_, [pb.b], [dbuf])
            if src_ is not Kt:
                for hh in range(2):
                    if src_ is Bt:
                        zt, zd = BTz[hh], BTz[hh].a
                    else:
                        zt, zd = ARTz[hh], ARTz[hh].a[:, :, 0 if src_ is At else 1, :]
                    ts(zd, dst_, mcol[hh], None, ALU.mult, None, [dbuf, cm.b], [zt.b])
        if LVL < 2:
            continue
        for h in range(8):
            hp, hh = divmod(h, 2)
            ps_ = slice(hh * 64, (hh + 1) * 64)
            pb = bank()
            arr = ARTz[hh].a[:, hp, :, :].rearrange("p a t -> p (a t)")
            mm(pb.a[:, 0:256], BT.a[:, hp, :], arr, True, True, [BT.b, ARTz[hh].b], [pb.b])
            mm(pb.a[:, 256:512], KT.a[:, hp, :], arr, True, True, [KT.b, ARTz[hh].b], [pb.b])
            tt(G.a[:, h, :], pb.a, mk2.a.rearrange("p a t -> p (a t)"), ALU.mult, [pb.b, mk2.b], [G.b])
        for g4 in range(2):
            pb = bank()
            for hq in range(4):
                h = g4 * 4 + hq
                hp, hh = divmod(h, 2)
                ps_ = slice(hh * 64, (hh + 1) * 64)
                mm(pb.a[:, hq * 128:(hq + 1) * 128], ART.a[:, hp, 0, :], BTz[hh].a[:, hp, :], True, True, [ART.b, BTz[hh].b], [pb.b])
            tt(Lb2[0].a[:, g4 * 4:(g4 + 1) * 4, :], pb.a.rearrange("p (a t) -> p a t", a=4),
               cm.a[:, 2:3, :].broadcast_to([128, 4, 128]), ALU.mult, [pb.b, cm.b], [Lb2[0].b])
        if LVL < 3:
            continue
        idb = ident_f.a.unsqueeze(1).broadcast_to([128, 8, 128])
        tt(TTm.a, G.a[:, :, 0:128], idb, ALU.add, [G.b, ident_f.b], [TTm.b])
        tt(TLm.a, Lb2[0].a, idb, ALU.add, [Lb2[0].b, ident_f.b], [TLm.b], eng="gpsimd")
        cp(Yb[0].a, G.a[:, :, 0:128], [G.b], [Yb[0].b], eng="scalar")
        cp(Lb2[1].a, Lb2[0].a, [Lb2[0].b], [Lb2[1].b], eng="scalar")
        cp(TTb.a, TTm.a, [TTm.b], [TTb.b])
        cp(TLb.a, TLm.a, [TLm.b], [TLb.b], eng="scalar")
        Yp_t = Yb[0]
        Lp = Lb2[1]
        for k in range(1, 7):
            Ln_ = Lb2[1 + (k % 2)]; Yn = Yb[k % 2]
            for g4 in range(2):
                py = bank(); pl_ = bank()
                for hq in range(4):
                    h = g4 * 4 + hq
                    mm(py.a[:, hq * 128:(hq + 1) * 128], Lp.a[:, h, :], Yp_t.a[:, h, :], True, True, [Lp.b, Yp_t.b], [py.b])
                for hq in range(4):
                    h = g4 * 4 + hq
                    mm(pl_.a[:, hq * 128:(hq + 1) * 128], Yp_t.a[:, h, :], Lp.a[:, h, :], True, True, [Lp.b, Yp_t.b], [pl_.b])
                cp(Yn.a[:, g4 * 4:(g4 + 1) * 4, :], py.a.rearrange("p (a t) -> p a t", a=4), [py.b], [Yn.b])
                cp(Ln_.a[:, g4 * 4:(g4 + 1) * 4, :], pl_.a.rearrange("p (a t) -> p a t", a=4), [pl_.b], [Ln_.b], eng="scalar")
            upd = []
            for g4 in range(2):
                p1 = bank()
                for hq in range(4):
                    h = g4 * 4 + hq
                    mm(p1.a[:, hq * 128:(hq + 1) * 128], TLb.a[:, h, :], Yn.a[:, h, :], True, True, [TLb.b, Yn.b], [p1.b])
                p2 = None
                if k < 6:
                    p2 = bank()
                    for hq in range(4):
                        h = g4 * 4 + hq
                        mm(p2.a[:, hq * 128:(hq + 1) * 128], TTb.a[:, h, :], Ln_.a[:, h, :], True, True, [TTb.b, Ln_.b], [p2.b])
                upd.append((g4, p1, p2))
            for (g4, p1, p2) in upd:
                sl_ = slice(g4 * 4, (g4 + 1) * 4)
                tt(TTm.a[:, sl_, :], TTm.a[:, sl_, :], p1.a.rearrange("p (a t) -> p a t", a=4), ALU.add, [TTm.b, p1.b], [TTm.b])
                if p2 is not None:
                    tt(TLm.a[:, sl_, :], TLm.a[:, sl_, :], p2.a.rearrange("p (a t) -> p a t", a=4), ALU.add, [TLm.b, p2.b], [TLm.b])
            if k < 6:
                cp(TTb.a, TTm.a, [TTm.b], [TTb.b], eng="scalar")
                cp(TLb.a, TLm.a, [TLm.b], [TLb.b], eng="scalar")
            Yp_t = Yn
            Lp = Ln_
        first = ci == 0
        hsl = lambda h: slice(h * 64, (h + 1) * 64)
        pw = bank()
        for h in range(8):
            hp, hh = divmod(h, 2)
            ps_ = slice(hh * 64, (hh + 1) * 64)
            if not first:
                mm(pw.a[:, hsl(h)], ART.a[:, hp, 0, :], Hz[hh].a[:, hp, :], True, False, [ART.b, Hz[hh].b], [pw.b])
            mm(pw.a[:, hsl(h)], G.a[:, h, 256:384], Vc.a[:, hsl(h)], first, True, [G.b, Vc.b], [pw.b])
        cp(Wsb.a, pw.a, [pw.b], [Wsb.b])
        pu = bank()
        for h in range(8):
            mm(pu.a[:, hsl(h)], TTm.a[:, h, :], Wsb.a[:, hsl(h)], True, True, [TTm.b, Wsb.b], [pu.b])
        cp(Usb.a, pu.a, [pu.b], [Usb.b], eng="scalar")
        po = bank()
        for h in range(8):
            hp, hh = divmod(h, 2)
            ps_ = slice(hh * 64, (hh + 1) * 64)
            if not first:
                mm(po.a[:, hsl(h)], ART.a[:, hp, 1, :], Hz[hh].a[:, hp, :], True, False, [ART.b, Hz[hh].b], [po.b])
            mm(po.a[:, hsl(h)], G.a[:, h, 128:256], Usb.a[:, hsl(h)], first, False, [G.b, Usb.b], [po.b])
            mm(po.a[:, hsl(h)], G.a[:, h, 384:512], Vc.a[:, hsl(h)], False, True, [G.b, Vc.b], [po.b])
        ob_ = Osb[ci % 2]
        cp(ob_.a, po.a, [po.b], [ob_.b])
        dma(OS[t0:t0 + 128, :], ob_.a, [ob_.b], [os_b[ci]])
        ph = bank()
        for h in range(8):
            hp, hh = divmod(h, 2)
            ps_ = slice(hh * 64, (hh + 1) * 64)
            mm(ph.a[:, h * 64:(h + 1) * 64], Bh.a[:, hp * 128:(hp + 1) * 128], Usb.a[:, hsl(h)], True, False, [Bh.b, Usb.b], [ph.b])
            mm(ph.a[:, h * 64:(h + 1) * 64], Kh.a[:, hp * 128:(hp + 1) * 128], Vc.a[:, hsl(h)], False, True, [Kh.b, Vc.b], [ph.b])
        ph4 = ph.a.rearrange("p (a b i) -> p a b i", a=4, b=2)
        for hh in range(2):
            if first:
                ts(Hz[hh].a, ph4[:, :, hh, :], mcol[hh], None, ALU.mult, None, [ph.b, cm.b], [Hz[hh].b])
            else:
                tt(Hs2.a, Hz[hh].a, gamT.a.unsqueeze(2).broadcast_to([128, 4, 64]), ALU.mult, [Hz[hh].b, gamT.b], [Hs2.b])
                tt(Hs2.a, Hs2.a, ph4[:, :, hh, :], ALU.add, [Hs2.b, ph.b], [Hs2.b])
                ts(Hz[hh].a, Hs2.a, mcol[hh], None, ALU.mult, None, [Hs2.b, cm.b], [Hz[hh].b])
    Sfin = A("Sfin", [64, 512])
    tt(Hst.a, Hz[0].a, Hz[1].a, ALU.add, [Hz[0].b, Hz[1].b], [Hst.b])
    pb = bank()
    for hp in range(4):
        tr(pb.a[0:64, hp * 128:(hp + 1) * 128], Hst.a[:, hp, :], ident_f.a, [Hst.b, ident_f.b], [pb.b])
    cp(Sfin.a, pb.a[0:64, :], [pb.b], [Sfin.b])
    dma(bass.AP(o_wkp.tensor, 0, [[64, 64], [4096, 8], [1, 64]]), Sfin.a.rearrange("p (h j) -> p h j", h=8), [Sfin.b], [Buf("o")])

    arena_reset(0)
    Sst = A("Sst", [128, 4096]); stmp = A("stmp", [128, 4096])
    vecs = [A("vec0", [128, 4, 5, 64])]
    vk = A("vk", [128, 64]); skk = A("skk", [128, 64])

    def scan_step(Sv, nI, kkn, w, kka, vkt, r, oslot, rb, tmpv):
        bcv = lambda a: a.unsqueeze(1).broadcast_to([128, nI, 64])
        tt(tmpv, Sv, bcv(kkn), ALU.mult, [Sst.b] + rb, [stmp.b])
        red(skk.a[:, :nI], tmpv, [stmp.b], [skk.b])
        tt(Sv, Sv, bcv(w), ALU.mult, [Sst.b] + rb, [Sst.b])
        tt(tmpv, skk.a[:, :nI].unsqueeze(2).broadcast_to([128, nI, 64]), bcv(kka), ALU.mult, [skk.b] + rb, [stmp.b])
        tt(Sv, Sv, tmpv, ALU.add, [Sst.b, stmp.b], [Sst.b])
        tt(Sv, Sv, vkt, ALU.add, [Sst.b, vk.b], [Sst.b])
        tt(tmpv, Sv, bcv(r), ALU.mult, [Sst.b] + rb, [stmp.b])
        return tmpv

    vec = vecs[0]; vt = A("vts", [128, 4, 64]); oo_s = A("oo_s", [128, 4, 64])
    dma(Sst.a, st_wkv, [], [Sst.b])
    for s_ in range(16):
        src = bass.AP(SC.tensor, (SEQ + s_ * 4) * 4096, [[512, 8], [4096, 4], [1, 320]])
        dma(vec.a[s_ * 8:(s_ + 1) * 8, 0:4].rearrange("p t q j -> p t (q j)"), src, [sc_b[16]], [vec.b])
        srcv = bass.AP(SC.tensor, (SEQ + s_ * 4) * 4096 + 320, [[512, 8], [4096, 4], [1, 64]])
        dma(vt.a[s_ * 8:(s_ + 1) * 8], srcv, [sc_b[16]], [vt.b])
    Ss = Sst.a.rearrange("p (i j) -> p i j", i=64)
    tmps = stmp.a.rearrange("p (i j) -> p i j", i=64)
    vks = A("vks", [128, 4096])
    vks3 = vks.a.rearrange("p (i j) -> p i j", i=64)
    for t in range(4):
        tt(vks3, vt.a[:, t, :].unsqueeze(2).broadcast_to([128, 64, 64]),
           vec.a[:, t, 2, :].unsqueeze(1).broadcast_to([128, 64, 64]), ALU.mult, [vt.b, vec.b], [vk.b])
        tmpv = scan_step(Ss, 64, vec.a[:, t, 3, :], vec.a[:, t, 1, :], vec.a[:, t, 4, :], vks3, vec.a[:, t, 0, :],
                         None, [vec.b], tmps)
        red(oo_s.a[:, t, :], tmpv, [stmp.b], [oo_s.b])
    dma(o_wks, Sst.a, [Sst.b], [Buf("o")])
    for s_ in range(16):
        dst = bass.AP(OS.tensor, (SEQ + s_ * 4) * 512, [[64, 8], [512, 4], [1, 64]])
        dma(dst, oo_s.a[s_ * 8:(s_ + 1) * 8], [oo_s.b], [os_b[16]])

    arena_reset(0)
    nbank[0] = 6
    wbuf = A("wbuf", [128, 8, 1024], BF16)
    Mb = [A("Mb0", [128, 8, 1024], BF16), A("Mb1", [128, 8, 1024], BF16)]
    actT = A("actT", [128, 16, 128], BF16)
    NLA = 2
    sig = [A(f"sig{i}", [128, 1024]) for i in range(NLA)]
    ebf = [A(f"ebf{i}", [128, 1024], BF16) for i in range(NLA)]
    utb = [A("utb0", [128, 8, 1024], BF16), A("utb1", [128, 8, 1024], BF16)]
    vbb = [A("vbb0", [128, 4, D], BF16), A("vbb1", [128, 4, D], BF16)]
    gl2 = [A(f"gl{i}", [128, 512], BF16) for i in range(2)]
    actb = [A(f"actb{i}", [128, 512], BF16) for i in range(2)]
    ht = A("ht", [128, D]); ob = A("ob", [128, 512]); vv = A("vv", [128, 512]); gg = A("gg", [128, 512]); ub = A("ub", [128, 512])
    shb = [A(f"shb{i}", [128, 512]) for i in range(3)]
    ysb = A("ysb", [128, D], BF16); yT = A("yT", [128, 8, 128], BF16); x2T = A("x2T", [128, 8, 128], BF16)
    qT = A("qT", [128, 16, 128], BF16); sall = A("sall", [128, 16, 128])
    v16 = A("v16", [128, 16, 16]); srep = A("srep", [128, 128]); cand = A("cand", [128, 256]); crep = A("crep", [128, 256])
    c16 = A("c16", [128, 8, 16]); thr = A("thr", [128, 8]); bia = A("bia", [128, 8]); zz = A("zz", [128, 8]); nm = A("nm", [128, 8])
    gel = A("gel", [128, 512], BF16); ptb = A("ptb", [128, 256]); ptbb = A("ptbb", [128, 256], BF16); pT = A("pT", [128, 2, 128], BF16)
    m8 = A("m8", [128, 8]); m4 = A("m4", [128, 8]); pl = A("pl", [128, D])
    psO = [PSfix for PSfix in banks[6:8]]

    def bank6():
        b = banks[bank_i[0] % 6]
        bank_i[0] += 1
        return b

    def loadw(scr, n, key):
        dma(wbuf.a[:, :, 0:n], scr, [wscr_b[key]], [wbuf.b], q="sync")

    def phase2(ti):
        samp = ti == 16
        nt = NS if samp else 128
        t0 = ti * 128
        tk0 = SEQ if samp else t0
        dma(ob.a[:nt], OS[tk0:tk0 + nt, :], [os_b[ti]], [ob.b])
        dma(v3(vv.a[:nt]), SC[tk0:tk0 + nt, :, 5, :], [sc_b[ti]], [vv.b])
        dma(v3(gg.a[:nt]), SC[tk0:tk0 + nt, :, 6, :], [sc_b[ti]], [gg.b])
        dma(bn8.a[:nt], BN[tk0:tk0 + nt, :], [sc_b[ti]], [bn8.b])
        red(h8.a[:nt], v3(ob.a[:nt]), [ob.b], [h8.b])
        ts(h8.a[:nt], h8.a[:nt], 1.0 / 64, None, ALU.mult, None, [h8.b], [h8.b])
        tt(v3(ob.a[:nt]), v3(ob.a[:nt]), h8.a[:nt].unsqueeze(2).broadcast_to([nt, 8, 64]), ALU.subtract, [ob.b, h8.b], [ob.b])
        tt(t2.a[:nt], ob.a[:nt], ob.a[:nt], ALU.mult, [ob.b], [t2.b])
        red(rn8.a[:nt], v3(t2.a[:nt]), [t2.b], [rn8.b])
        ts(rn8.a[:nt], rn8.a[:nt], 1.0 / 64, 64e-5, ALU.mult, ALU.add, [rn8.b], [rn8.b])
        rsqrt(rn8.a[:nt], rn8.b)
        tt(v3(ob.a[:nt]), v3(ob.a[:nt]), rn8.a[:nt].unsqueeze(2).broadcast_to([nt, 8, 64]), ALU.mult, [ob.b, rn8.b], [ob.b])
        tt(ob.a[:nt], ob.a[:nt], lg_bc.a[:nt], ALU.mult, [ob.b, lg_bc.b], [ob.b])
        tt(ob.a[:nt], ob.a[:nt], lb_bc.a[:nt], ALU.add, [ob.b, lb_bc.b], [ob.b])
        tt(v3(vv.a[:nt]), v3(vv.a[:nt]), bn8.a[:nt].unsqueeze(2).broadcast_to([nt, 8, 64]), ALU.mult, [vv.b, bn8.b], [vv.b])
        tt(ob.a[:nt], ob.a[:nt], vv.a[:nt], ALU.add, [ob.b, vv.b], [ob.b])
        tt(ysb.a[:nt, 0:512], ob.a[:nt], gg.a[:nt], ALU.mult, [ob.b, gg.b], [ysb.b])
        if samp:
            dma(ub.a[:nt], Esf[0], [e_b[ti]], [ub.b])
        else:
            dma(ub.a[:nt], Ep[15 + t0:15 + t0 + nt, :], [e_b[ti]], [ub.b])
        cp(t1.a[:nt], ub.a[:nt], [ub.b], [t1.b])
        rdE = [e_b[ti]] + ([e_b[ti - 1]] if (0 < ti < 16) else []) + [zinit]
        for k in range(1, 16):
            c0 = 0 if k < 2 else (128 if k < 4 else (256 if k < 8 else 384))
            sh = shb[k % 3]
            if samp:
                dma(sh.a[:nt, c0:512], Esf[k][:, c0:512], rdE, [sh.b])
            else:
                dma(sh.a[:nt, c0:512], Ep[15 + t0 - k:15 + t0 - k + nt, c0:512], rdE, [sh.b])
            tt(t1.a[:nt, c0:512], t1.a[:nt, c0:512], sh.a[:nt, c0:512], ALU.add, [t1.b, sh.b], [t1.b])
        rci = 0 if ti == 0 else 1
        tt(t1.a[:nt].rearrange("p (g c) -> p g c", g=4), t1.a[:nt].rearrange("p (g c) -> p g c", g=4),
           rc.a[:nt, rci, :].unsqueeze(2).broadcast_to([nt, 4, 128]), ALU.mult, [t1.b, rc.b], [t1.b])
        tt(ptbb_p(nt), t1.a[:nt], ub.a[:nt], ALU.subtract, [t1.b, ub.b], [gel.b])
        pb = bank6()
        for g in range(4):
            tr(pb.bf[:, g * 128:g * 128 + nt], gel.a[:nt, g * 128:(g + 1) * 128], ident.a[:nt, :nt], [gel.b, ident.b], [pb.b])
        cp(qT.a[:, 0:4, :nt], pb.bf[:, 0:512].rearrange("p (g t) -> p g t", g=4)[:, :, :nt], [pb.b], [qT.b])
        pb2 = bank6()
        for g in range(4):
            mm(pb2.a[:nt, g * 128:(g + 1) * 128], qT.a[:, g, :nt], poolw.a[:, g, :], True, True, [qT.b, poolw.b], [pb2.b])
        tt(ysb.a[:nt, 512:1024], pb2.a[:nt], psc_bc.a[:nt], ALU.mult, [pb2.b, psc_bc.b], [ysb.b])
        pb = bank6()
        for kd in range(8):
            tr(pb.bf[:, kd * 128:kd * 128 + nt], ysb.a[:nt, kd * 128:(kd + 1) * 128], ident.a[:nt, :nt], [ysb.b, ident.b], [pb.b])
        cp(yT.a[:, :, :nt], pb.bf.rearrange("p (k t) -> p k t", k=8)[:, :, :nt], [pb.b], [yT.b], eng="scalar")
        dma(xt.a[:nt], x_s if samp else x_p[t0:t0 + nt, :], [], [xt.b])
        loadw(WOs, D, "WO")
        for half in range(2):
            pb = bank6()
            for kd in range(8):
                mm(pb.a[:nt], yT.a[:, kd, :nt], wbuf.a[:, kd, half * 512:(half + 1) * 512], kd == 0, kd == 7, [yT.b, wbuf.b], [pb.b])
            tt(ht.a[:nt, half * 512:(half + 1) * 512], pb.a[:nt], xt.a[:nt, half * 512:(half + 1) * 512], ALU.add,
               [pb.b, xt.b], [ht.b])
        rms_T(ht, nt, x2T)
        for qb in range(4):
            if qb % 2 == 0:
                loadw(WQs[:, :, (qb // 2) * 1024:(qb // 2 + 1) * 1024], 1024, "WQ")
            pb = bank6()
            for j in range(4):
                c = qb * 4 + j
                cl = (qb % 2) * 4 + j
                for kd in range(8):
                    mm(pb.a[:, j * 128:j * 128 + nt], wbuf.a[:, kd, cl * 128:(cl + 1) * 128], x2T.a[:, kd, :nt], kd == 0, kd == 7,
                       [wbuf.b, x2T.b], [pb.b])
            cp(qT.a[:, qb * 4:(qb + 1) * 4, :nt], pb.a.rearrange("p (j t) -> p j t", j=4)[:, :, :nt], [pb.b], [qT.b],
               eng="scalar" if qb % 2 else "vector")
        for qb in range(4):
            pb = bank6()
            for j in range(4):
                c = qb * 4 + j
                mm(pb.a[:nt, j * 128:(j + 1) * 128], qT.a[:, c, :nt], keysT.a[:, c % 2, :], True, True, [qT.b, keysT.b], [pb.b])
            cp(sall.a[:nt, qb * 4:(qb + 1) * 4, :], pb.a[:nt].rearrange("p (j n) -> p j n", j=4), [pb.b], [sall.b],
               eng="scalar" if qb % 2 else "vector")
        for c in range(16):
            S.op("vector", lambda e, c=c: e.max(out=v16.a[:nt, c, 0:8], in_=sall.a[:nt, c, :]), [sall.b], [v16.b])
            S.op("vector", lambda e, c=c: e.match_replace(out=srep.a[:nt], in_to_replace=v16.a[:nt, c, 0:8],
                                                          in_values=sall.a[:nt, c, :], imm_value=-1e30), [sall.b, v16.b], [srep.b])
            S.op("vector", lambda e, c=c: e.max(out=v16.a[:nt, c, 8:16], in_=srep.a[:nt]), [srep.b], [v16.b])
        for h in range(8):
            tt(cand.a[:nt, :].rearrange("p (a b) -> p a b", a=16),
               v16.a[:nt, 2 * h, :].unsqueeze(2).broadcast_to([nt, 16, 16]),
               v16.a[:nt, 2 * h + 1, :].unsqueeze(1).broadcast_to([nt, 16, 16]), ALU.add, [v16.b], [cand.b])
            S.op("vector", lambda e, h=h: e.max(out=c16.a[:nt, h, 0:8], in_=cand.a[:nt, :]), [cand.b], [c16.b])
            S.op("vector", lambda e, h=h: e.match_replace(out=crep.a[:nt], in_to_replace=c16.a[:nt, h, 0:8],
                                                          in_values=cand.a[:nt, :], imm_value=-1e30), [cand.b, c16.b], [crep.b])
            S.op("vector", lambda e, h=h: e.max(out=c16.a[:nt, h, 8:16], in_=crep.a[:nt]), [crep.b], [c16.b])
        cp(thr.a[:nt], c16.a[:nt, :, 15], [c16.b], [thr.b])
        ts(nm.a[:nt], c16.a[:nt, :, 0], -1.0, None, ALU.mult, None, [c16.b], [nm.b])
        memset(zz.a, 0.0, [zz.b])
        for h in range(8):
            act(c16.a[:nt, h, :], c16.a[:nt, h, :], AF.Exp, [c16.b, nm.b], [c16.b, zz.b],
                bias=nm.a[:nt, h:h + 1], accum=zz.a[:nt, h:h + 1])
        act(zz.a[:nt], zz.a[:nt], AF.Ln, [zz.b], [zz.b])
        tt(bia.a[:nt], nm.a[:nt], zz.a[:nt], ALU.subtract, [nm.b, zz.b], [bia.b])
        def emit_sigma(i):
            q, h = divmod(i, 8)
            sb_, eb_ = sig[i % NLA], ebf[i % NLA]
            tt(sb_.a[:nt].rearrange("p (a b) -> p a b", a=8),
               sall.a[:nt, 2 * h, q * 8:q * 8 + 8].unsqueeze(2).broadcast_to([nt, 8, 128]),
               sall.a[:nt, 2 * h + 1, :].unsqueeze(1).broadcast_to([nt, 8, 128]), ALU.add, [sall.b], [sb_.b])
            act(eb_.a[:nt], sb_.a[:nt], AF.Exp, [sb_.b, bia.b], [eb_.b], bias=bia.a[:nt, h:h + 1])

        def emit_mask(i):
            q, h = divmod(i, 8)
            sb_, eb_ = sig[i % NLA], ebf[i % NLA]
            stt(Mb[q % 2].a[:nt, h, :], sb_.a[:nt], thr.a[:nt, h:h + 1], eb_.a[:nt], ALU.is_ge, ALU.mult,
                [sb_.b, thr.b, eb_.b], [Mb[q % 2].b])

        grp = {}

        def stage_A(q):
            ub_ = utb[q % 2]
            for hf in range(2):
                pG = bank(); pP = bank()
                grp[(q, hf)] = [pG, pP, None]
                for h in range(8):
                    mm(pG.a[:nt, :], ident.a[:nt, :nt], Mb[q % 2].a[:nt, h, hf * 512:(hf + 1) * 512], h == 0, h == 7,
                       [Mb[q % 2].b, ident.b], [pG.b])
                for kd in range(8):
                    mm(pP.a[:nt, :], x2T.a[:, kd, :nt], ub_.a[:, kd, hf * 512:(hf + 1) * 512], kd == 0, kd == 7,
                       [x2T.b, ub_.b], [pP.b])

        def stage_B(q, hf):
            pG, pP, _ = grp[(q, hf)]
            gl = gl2[hf]; ab = actb[hf]
            act(gl.a[:nt], pP.a[:nt], AF.Gelu, [pP.b], [gl.b])
            tt(ab.a[:nt], pG.a[:nt], gl.a[:nt], ALU.mult, [pG.b, gl.b], [ab.b])

        def stage_C(q, hf):
            pT = bank()
            grp[(q, hf)][2] = pT
            ab = actb[hf]
            for j in range(4):
                tr(pT.bf[:, j * 128:j * 128 + nt], ab.a[:nt, j * 128:(j + 1) * 128], ident.a[:nt, :nt], [ab.b, ident.b], [pT.b])

        def stage_D(q, hf):
            pT = grp[(q, hf)][2]
            sl = (q % 2) * 8 + hf * 4
            cp(actT.a[:, sl:sl + 4, :nt], pT.bf[:, 0:512].rearrange("p (j t) -> p j t", j=4)[:, :, :nt], [pT.b], [actT.b],
               eng="scalar" if hf else "vector")

        def stage_E(q):
            for c in range(8):
                cg = q * 8 + c
                vb_ = vbb[c // 4]
                for half in range(2):
                    mm(psO[half].a[:nt], actT.a[:, (q % 2) * 8 + c, :nt], vb_.a[:, c % 4, half * 512:(half + 1) * 512],
                       cg == 0, cg == 127, [actT.b, vb_.b], [psO[half].b])

        def emit_loads(q):
            dma(utb[q % 2].a, UT[:, :, q * 1024:(q + 1) * 1024].rearrange("k p e -> p k e"), [ut_b], [utb[q % 2].b])

        def emit_vloads(q):
            for hv in range(2):
                r0 = q * 1024 + hv * 512
                dma(vbb[hv].a, VB[r0:r0 + 512, :].rearrange("(a p) d -> p a d", p=128), [vb_b], [vbb[hv].b])

        def lazy(qp, h):
            if h == 1:
                stage_B(qp, 0)
            elif h == 2:
                stage_B(qp, 1)
            elif h == 3:
                stage_C(qp, 0); stage_C(qp, 1)
            elif h == 4:
                stage_D(qp, 0); stage_D(qp, 1)
            elif h == 5:
                emit_vloads(qp)
                stage_E(qp)

        emit_loads(0)
        for i0_ in range(NLA - 1):
            emit_sigma(i0_)
        for i in range(128):
            q, h = divmod(i, 8)
            if h == 0 and q + 1 < 16:
                emit_loads(q + 1)
            if i + NLA - 1 < 128:
                emit_sigma(i + NLA - 1)
            emit_mask(i)
            if q > 0:
                lazy(q - 1, h)
            if h == 7:
                stage_A(q)
        for h in range(1, 6):
            lazy(15, h)
        for half in range(2):
            tt(ht.a[:nt, half * 512:(half + 1) * 512], ht.a[:nt, half * 512:(half + 1) * 512], psO[half].a[:nt], ALU.add,
               [ht.b, psO[half].b], [ht.b])
        rms_T(ht, nt, x2T)
        loadw(WGs, D, "WG")
        dma(ptb.a[:nt], p_s if samp else p_p[t0:t0 + nt, :], [], [ptb.b])
        cp(ptbb.a[:nt], ptb.a[:nt], [ptb.b], [ptbb.b])
        pb = bank6()
        for j in range(2):
            tr(pb.bf[:, j * 128:j * 128 + nt], ptbb.a[:nt, j * 128:(j + 1) * 128], ident.a[:nt, :nt], [ptbb.b, ident.b], [pb.b])
        cp(pT.a[:, :, :nt], pb.bf[:, 0:256].rearrange("p (k t) -> p k t", k=2)[:, :, :nt], [pb.b], [pT.b])
        for half in range(2):
            pg_ = bank6(); pp_ = bank6()
            for kd in range(8):
                mm(pg_.a[:nt], x2T.a[:, kd, :nt], wbuf.a[:, kd, half * 512:(half + 1) * 512], kd == 0, kd == 7, [x2T.b, wbuf.b], [pg_.b])
            for kd in range(2):
                mm(pp_.a[:nt], pT.a[:, kd, :nt], plw.a[:, kd, half * 512:(half + 1) * 512], kd == 0, kd == 1, [pT.b, plw.b], [pp_.b])
            act(pl.a[:nt, half * 512:(half + 1) * 512], pg_.a[:nt], AF.Sigmoid, [pg_.b], [pl.b])
            tt(pl.a[:nt, half * 512:(half + 1) * 512], pl.a[:nt, half * 512:(half + 1) * 512], pp_.a[:nt], ALU.mult, [pl.b, pp_.b], [pl.b])
        tt(ht.a[:nt], ht.a[:nt], pl.a[:nt], ALU.add, [ht.b, pl.b], [ht.b])
        sumsq(ht, nt)
        ts(rstd.a[:nt], ssq.a[:nt], 1.0 / D, 1e-6, ALU.mult, ALU.add, [ssq.b], [rstd.b])
        rsqrt(rstd.a[:nt], rstd.b)
        stt(pl.a[:nt], ht.a[:nt], rstd.a[:nt, 0:1], fg_bc.a[:nt], ALU.mult, ALU.mult, [ht.b, rstd.b, fg_bc.b], [pl.b])
        dma(y_s if samp else y_p[t0:t0 + nt, :], pl.a[:nt], [pl.b], [Buf("o")])

    def ptbb_p(nt):
        return gel.a[:nt]

    for ti in range(ntile):
        phase2(ti)

    fin = S.final_waits()
    with nc.Block() as block:
        def emit(engine, name):
            for waits, fn, sem, inc in S.ops[name]:
                for (s_, v_) in waits:
                    engine.wait_ge(s_, v_)
                fn(engine).then_inc(sem, inc)

        @block.tensor
        def _(e):
            emit(e, "tensor")

        @block.vector
        def _(e):
            emit(e, "vector")

        @block.scalar
        def _(e):
            emit(e, "scalar")

        @block.gpsimd
        def _(e):
            emit(e, "gpsimd")

        @block.sync
        def _(e):
            emit(e, "sync")
            for (s_, v_) in fin:
                e.wait_ge(s_, v_)
    es.close()
    return nc


_NC = None


def kernel(**inp):
    global _NC
    f = lambda a: np.ascontiguousarray(a, dtype=np.float32)
    rowv = np.concatenate([inp[k][0].reshape(-1) for k in
                           ("shift_mu", "decay_w0", "a_0", "k_k", "k_a", "r_k", "lnx_g", "lnx_b", "pool_scale")]
                          + [inp["final_norm_g"].reshape(-1)]).astype(np.float32)[None, :]
    gcols = np.concatenate([inp[k][0].reshape(8, 128).T for k in ("norm_mix_g", "norm_ffn_g", "norm_ple_g")], axis=1)
    pos = np.arange(128)
    wd = np.array([2, 4, 8, 16])
    rc = np.stack([1.0 / np.minimum(pos[:, None] + 1, wd[None, :]), np.broadcast_to(1.0 / wd[None, :], (128, 4))]).astype(np.float32)
    ii = np.arange(128)
    su = (ii[:, None] < ii[None, :]).astype(np.float32)
    iu = (ii[:, None] <= ii[None, :]).astype(np.float32)
    cmk = np.stack([su, iu, su.T, np.ones((128, 128), np.float32)], axis=1)
    common = dict(
        w_in=f(inp["w_in"][0]), w_out=f(inp["w_out"][0]), wq=f(inp["peer_wq"][0]), ple_w=f(inp["ple_w"][0]),
        ple_gw=f(inp["ple_gate_w"][0]), decay_b=f(inp["decay_b"][0]), a_b=f(inp["a_b"][0]), g_b=f(inp["g_b"][0]),
        pool_w=f(inp["pool_w"][0]), keys=f(inp["peer_keys"][0]), peer_u=f(inp["peer_u"][0]), peer_v=f(inp["peer_v"][0]),
        gcols=f(gcols), rowv=f(rowv), ident=np.eye(128, dtype=np.float32), pool_rc=f(rc), cmask=f(cmk))
    in_maps = []
    for b in range(NCORES):
        m = dict(common)
        sl = slice(16 * b, 16 * b + 16)
        m["x_p"] = f(inp["x_prompt"][b]); m["x_s"] = f(inp["x_sample"][sl].reshape(NS, D))
        m["st_shift"] = f(inp["state_shift"][0, sl]); m["st_wkv"] = f(inp["state_wkv"][0, sl].reshape(128, 4096))
        m["st_pool"] = f(inp["state_pool"][0, sl]); m["p_p"] = f(inp["p_prompt"][0, b]); m["p_s"] = f(inp["p_sample"][0, sl].reshape(NS, 256))
        in_maps.append(m)
    if _NC is None:
        _NC = build_nc()
    res = run_bass_kernel_spmd(_NC, in_maps, core_ids=list(range(NCORES)))
    R = res.results
    y_p = np.stack([R[b]["y_p"] for b in range(NCORES)])
    y_s = np.concatenate([R[b]["y_s"].reshape(16, 4, D) for b in range(NCORES)])
    shp = np.stack([R[b]["o_shp"].reshape(DSH) for b in range(NCORES)])[None]
    wkp = np.stack([R[b]["o_wkp"].reshape(8, 64, 64) for b in range(NCORES)])[None]
    pop = np.stack([R[b]["o_pop"] for b in range(NCORES)])[None]
    shs = np.concatenate([R[b]["o_shs"] for b in range(NCORES)])[None]
    wks = np.concatenate([R[b]["o_wks"].reshape(16, 8, 64, 64) for b in range(NCORES)])[None]
    pos_ = np.concatenate([R[b]["o_pos"] for b in range(NCORES)])[None]
    return tuple(np.ascontiguousarray(a, dtype=np.float32) for a in (y_p, y_s, shp, wkp, pop, shs, wks, pos_))


if __name__ == "__main__":
    import time
    t = time.time()
    nc = build_nc()
    print("built", time.time() - t)
```

```python
import numpy as np
import ml_dtypes
from contextlib import ExitStack
import concourse.bass as bass
import concourse.mybir as mybir
from concourse.bass_utils import run_bass_kernel_spmd

F32 = mybir.dt.float32
BF16 = mybir.dt.bfloat16
ALU = mybir.AluOpType
AF = mybir.ActivationFunctionType
AX = mybir.AxisListType

NCORES = 8
D = 1024
SEQ = 2048
NS = 64
NTOK = SEQ + NS
DSH = 1792
DIN = 2304
NE = 16384
SAME_SYNC = True
EPOCH = 20000
NSLOT = 24
TC = 16


class Buf:
    __slots__ = ("name", "w", "r")

    def __init__(self, name):
        self.name = name
        self.w = None
        self.r = []


class Sched:
    ENG = ["tensor", "vector", "scalar", "gpsimd", "sync"]

    def __init__(self, sems):
        self.free = list(sems)
        self.ops = {e: [] for e in self.ENG}
        self.cur = {e: [self.free.pop(), 0] for e in ["tensor", "vector", "scalar", "gpsimd"]}
        self.seen = {e: {} for e in self.ENG}
        self.slots = [[self.free.pop(), 0] for _ in range(NSLOT)]
        self.rr = 0
        self.pending = {e: [] for e in self.ENG}

    def _need(self, eng, tok, waits, strict=True):
        if tok is None:
            return
        sem, val, src = tok
        if src == eng and (eng == "tensor" or not SAME_SYNC or not (strict or eng == "scalar")):
            return
        k = id(sem)
        if self.seen[eng].get(k, 0) >= val:
            return
        self.seen[eng][k] = val
        waits.append((sem, val))

    def op(self, eng, fn, reads=(), writes=(), dma=False, strict=True):
        waits = self.pending[eng]
        self.pending[eng] = []
        strict = True
        for b in reads:
            self._need(eng, b.w, waits, strict)
        for b in writes:
            self._need(eng, b.w, waits, strict)
            for t in b.r:
                self._need(eng, t, waits, strict)
        if dma:
            slot = self.slots[self.rr]
            self.rr = (self.rr + 1) % NSLOT
            if slot[1] > 0:
                self._need(eng, (slot[0], slot[1], "dma"), waits)
            if slot[1] + 16 > EPOCH * 2:
                slot[0] = self.free.pop()
                slot[1] = 0
            slot[1] += 16
            tok = (slot[0], slot[1], "dma")
            inc = 16
        else:
            c = self.cur[eng]
            if c[1] >= EPOCH:
                c[0] = self.free.pop()
                c[1] = 0
            c[1] += 1
            tok = (c[0], c[1], eng)
            inc = 1
        self.ops[eng].append((waits, fn, tok[0], inc))
        for b in reads:
            b.r.append(tok)
        for b in writes:
            b.w = tok
            b.r = []
        return tok

    def barrier(self):
        toks = [(c[0], c[1], e) for e, c in self.cur.items() if c[1] > 0]
        toks += [(sl[0], sl[1], "dma") for sl in self.slots if sl[1] > 0]
        for e in self.ENG:
            for t in toks:
                if t[2] == e:
                    continue
                k = id(t[0])
                if self.seen[e].get(k, 0) >= t[1]:
                    continue
                self.seen[e][k] = t[1]
                self.pending[e].append((t[0], t[1]))

    def final_waits(self):
        return [(s[0], s[1]) for s in self.slots if s[1] > 0]


def build_nc():
    nc = bass.Bass("TRN2", target_bir_lowering=False)

    def din(name, shape, dt=F32):
        return nc.dram_tensor(name, list(shape), dt, kind="ExternalInput").ap()

    def dout(name, shape):
        return nc.dram_tensor(name, list(shape), F32, kind="ExternalOutput").ap()

    def dscr(name, shape, dt=F32):
        return nc.dram_tensor(name, list(shape), dt, kind="Internal").ap()

    x_p = din("x_p", [SEQ, D]); x_s = din("x_s", [NS, D])
    st_shift = din("st_shift", [16, DSH]); st_wkv = din("st_wkv", [128, 4096]); st_pool = din("st_pool", [16, 15, 512])
    p_p = din("p_p", [SEQ, 256]); p_s = din("p_s", [NS, 256])
    w_in = din("w_in", [D, DIN]); w_out = din("w_out", [D, D]); wq = din("wq", [D, 2048])
    ple_w = din("ple_w", [256, D]); ple_gw = din("ple_gw", [D, D])
    decay_b = din("decay_b", [64, 512]); a_b = din("a_b", [64, 512]); g_b = din("g_b", [128, 512])
    pool_w = din("pool_w", [4, 128, 128]); keys = din("keys", [2, 128, 128])
    peer_u = din("peer_u", [NE, D]); peer_v = din("peer_v", [NE, D])
    gcols = din("gcols", [128, 24])
    rowv = din("rowv", [1, 6912])
    ident_in = din("ident", [128, 128])
    pool_rc = din("pool_rc", [2, 128, 4])
    cmask = din("cmask", [128, 4, 128])
    y_p = dout("y_p", [SEQ, D]); y_s = dout("y_s", [NS, D])
    o_shp = dout("o_shp", [1, DSH]); o_wkp = dout("o_wkp", [128, 256]); o_pop = dout("o_pop", [15, 512])
    o_shs = dout("o_shs", [16, DSH]); o_wks = dout("o_wks", [128, 4096]); o_pos = dout("o_pos", [16, 15, 512])
    Zp = dscr("Zp", [SEQ + 1, DSH]); Ep = dscr("Ep", [SEQ + 15, 512])
    Zs = dscr("Zs", [NS, DSH]); Zsp = dscr("Zsp", [NS, DSH]); Us = dscr("Us", [NS, 512])
    Es = dscr("Es", [16, 19, 512]); Esf = dscr("Esf", [16, NS, 512])
    SC = dscr("SC", [NTOK, 8, 8, 64]); BN = dscr("BN", [NTOK, 8]); OS = dscr("OS", [NTOK, 512])
    UT = dscr("UT", [8, 128, NE], BF16); VB = dscr("VB", [NE, D], BF16)
    WOs = dscr("WOs", [128, 8, D], BF16); WQs = dscr("WQs", [128, 8, 2048], BF16); WGs = dscr("WGs", [128, 8, D], BF16)

    es = ExitStack()
    sems = [es.enter_context(nc.semaphore(f"s{i}")) for i in range(100)]
    S = Sched(sems)

    cur = [es]

    class T:
        def __init__(self, name, shape, dt=F32):
            self.h = cur[0].enter_context(nc.sbuf_tensor(name, list(shape), dt))
            self.a = self.h.ap()
            self.b = Buf(name)

    class PS:
        def __init__(self, name):
            self.h = es.enter_context(nc.psum_tensor(name, [128, 512], F32))
            self.a = self.h.ap()
            self.bf = self.a.bitcast(BF16)
            self.b = Buf(name)

    banks = [PS(f"ps{i}") for i in range(8)]
    bank_i = [0]
    nbank = [8]

    def bank():
        b = banks[bank_i[0] % nbank[0]]
        bank_i[0] += 1
        return b

    ARENA_W = 43000
    arena = es.enter_context(nc.sbuf_tensor("arena", [128, ARENA_W], F32)).ap()
    apos = [0]

    class A:
        def __init__(self, name, shape, dt=F32):
            n = 1
            for d_ in shape[1:]:
                n *= d_
            words = (n * (2 if dt == BF16 else 4) + 3) // 4
            words = (words + 15) // 16 * 16
            assert apos[0] + words <= ARENA_W, (name, apos[0], words)
            v = arena[:, apos[0]:apos[0] + words]
            apos[0] += words
            if dt == BF16:
                v = v.bitcast(BF16)
            v = v[:, 0:n]
            if len(shape) == 3:
                v = v.rearrange("p (a b) -> p a b", a=shape[1])
            elif len(shape) == 4:
                v = v.rearrange("p (a b c) -> p a b c", a=shape[1], b=shape[2])
            self.a = v[0:shape[0]]
            self.b = Buf(name)

    def arena_reset(mark):
        S.barrier()
        apos[0] = mark

    def tt(out, in0, in1, op, r, w, eng="vector"):
        S.op(eng, lambda e: e.tensor_tensor(out=out, in0=in0, in1=in1, op=op), r, w)

    def ts(out, in0, s1, s2, op0, op1, r, w, eng="vector"):
        st = not (isinstance(s1, (int, float)) and (s2 is None or isinstance(s2, (int, float))))
        if s2 is None:
            S.op(eng, lambda e: e.tensor_scalar(out=out, in0=in0, scalar1=s1, scalar2=None, op0=op0), r, w, strict=st)
        else:
            S.op(eng, lambda e: e.tensor_scalar(out=out, in0=in0, scalar1=s1, scalar2=s2, op0=op0, op1=op1), r, w, strict=st)

    def stt(out, in0, scalar, in1, op0, op1, r, w, eng="vector"):
        S.op(eng, lambda e: e.scalar_tensor_tensor(out=out, in0=in0, scalar=scalar, in1=in1, op0=op0, op1=op1), r, w,
             strict=not isinstance(scalar, (int, float)))

    def cp(out, in_, r, w, eng="vector"):
        if eng == "scalar":
            S.op(eng, lambda e: e.activation(out=out, in_=in_, func=AF.Copy), r, w)
        else:
            S.op(eng, lambda e: e.tensor_copy(out=out, in_=in_), r, w)

    def act(out, in_, func, r, w, bias=None, scale=None, accum=None):
        kw = {}
        if bias is not None:
            kw["bias"] = bias
        if scale is not None:
            kw["scale"] = scale
        if accum is not None:
            kw["accum_out"] = accum
        S.op("scalar", lambda e: e.activation(out=out, in_=in_, func=func, **kw), r, w)

    def rsqrt(ap, buf):
        act(ap, ap, AF.Sqrt, [buf], [buf])
        S.op("vector", lambda e: e.reciprocal(out=ap, in_=ap), [buf], [buf])

    def sumsq(src, nt):
        tt(junk.a[:nt], src.a[:nt], src.a[:nt], ALU.mult, [src.b], [junk.b])
        red(ssq.a[:nt], junk.a[:nt], [junk.b], [ssq.b])

    def red(out, in_, r, w, op=ALU.add):
        S.op("vector", lambda e: e.tensor_reduce(out=out, in_=in_, axis=AX.X, op=op), r, w)

    def mm(out, lhsT, rhs, start, stop, r, w):
        S.op("tensor", lambda e: e.matmul(out, lhsT, rhs, start=start, stop=stop), r, w)

    def tr(out, in_, ident, r, w):
        S.op("tensor", lambda e: e.transpose(out, in_, ident), r, w)

    dq = [0]

    def dma(out, in_, r, w, q=None):
        if q is None:
            q = "sync" if dq[0] % 2 == 0 else "gpsimd"
            dq[0] += 1
        S.op(q, lambda e: e.dma_start(out=out, in_=in_), r, w, dma=True)

    def memset(ap, val, w):
        S.op("vector", lambda e: e.memset(ap, val), [], w)

    ident_f = T("ident_f", [128, 128]); ident = T("ident_b", [128, 128], BF16)
    dma(ident_f.a, ident_in, [], [ident_f.b])
    cp(ident.a, ident_f.a, [ident_f.b], [ident.b])
    gc = T("gc", [128, 24])
    dma(gc.a, gcols, [], [gc.b])
    rc = T("rc", [128, 2, 4])
    dma(rc.a, pool_rc.rearrange("a p g -> p a g"), [], [rc.b])

    def bc_row(name, off, n, cls=T):
        t = cls(name, [128, n])
        dma(t.a, rowv[0:1, off:off + n].broadcast_to([128, n]), [], [t.b])
        return t

    lg_bc = bc_row("lg_bc", 4352, 512); lb_bc = bc_row("lb_bc", 4864, 512); psc_bc = bc_row("psc_bc", 5376, 512)
    fg_bc = bc_row("fg_bc", 5888, 1024)

    xt = T("xt", [128, D]); junk = T("junk", [128, D]); xnb = T("xnb", [128, D], BF16)
    ssq = T("ssq", [128, 1]); rstd = T("rstd", [128, 1])
    t1 = T("t1", [128, 512]); t2 = T("t2", [128, 512])
    h8 = T("h8", [128, 8]); rn8 = T("rn8", [128, 8]); bn8 = T("bn8", [128, 8])
    win = A("win", [128, 8, DIN], BF16)
    mark1 = apos[0]
    zero = A("zero", [128, DSH])
    memset(zero.a, 0.0, [zero.b])
    stg = [A("stg0", [128, 4096]), A("stg1", [128, 4096])]
    stb = [A("stb0", [128, 4096], BF16), A("stb1", [128, 4096], BF16)]
    sti = [0]

    def stage():
        i = sti[0] % 2
        sti[0] += 1
        return stg[i], stb[i]

    decb = T("decb", [64, 512], BF16); abb = T("abb", [128, 512], BF16); gbb = T("gbb", [128, 512], BF16)
    plw = T("plw", [128, 2, D], BF16); poolw = T("poolw", [128, 4, 128], BF16); keysT = T("keysT", [128, 2, 128], BF16)
    sf, sb = stage()
    dma(sf.a[0:64, 0:512], decay_b, [], [sf.b])
    cp(decb.a, sf.a[0:64, 0:512], [sf.b], [decb.b])
    sf, sb = stage()
    dma(sf.a[64:128, 0:512], a_b, [], [sf.b])
    cp(abb.a[64:128, :], sf.a[64:128, 0:512], [sf.b], [abb.b])
    sf, sb = stage()
    dma(sf.a[:, 0:512], g_b, [], [sf.b])
    cp(gbb.a, sf.a[:, 0:512], [sf.b], [gbb.b])
    sf, sb = stage()
    dma(sf.a[:, 0:2048].rearrange("p (k n) -> p k n", k=2), ple_w.rearrange("(k p) n -> p k n", p=128), [], [sf.b])
    cp(plw.a, sf.a[:, 0:2048].rearrange("p (k n) -> p k n", k=2), [sf.b], [plw.b])
    sf, sb = stage()
    dma(sf.a[:, 0:512].rearrange("p (g d) -> p g d", g=4), pool_w.rearrange("g c d -> c g d"), [], [sf.b])
    cp(poolw.a, sf.a[:, 0:512].rearrange("p (g d) -> p g d", g=4), [sf.b], [poolw.b])
    sf, sb = stage()
    dma(sf.a[:, 0:256].rearrange("p (s c) -> p s c", s=2), keys.rearrange("s n c -> n s c"), [], [sf.b])
    cp(sb.a[:, 0:256], sf.a[:, 0:256], [sf.b], [sb.b])
    pb = bank()
    for s_ in range(2):
        tr(pb.bf[:, s_ * 128:(s_ + 1) * 128], sb.a[:, s_ * 128:(s_ + 1) * 128], ident.a, [sb.b, ident.b], [pb.b])
    cp(keysT.a.rearrange("p s n -> p (s n)"), pb.bf[:, 0:256], [pb.b], [keysT.b])

    for kd in range(8):
        sf, sb = stage()
        dma(sf.a[:, 0:DIN], w_in[kd * 128:(kd + 1) * 128, :], [], [sf.b])
        ts(win.a[:, kd, :], sf.a[:, 0:DIN], gc.a[:, kd:kd + 1], None, ALU.mult, None, [sf.b, gc.b], [win.b])
    wscr_b = {"WO": Buf("WOs"), "WQ": Buf("WQs"), "WG": Buf("WGs")}
    for (src, dst, n, goff, key) in ((w_out, WOs, D, None, "WO"), (wq, WQs, 2048, 8, "WQ"), (ple_gw, WGs, D, 16, "WG")):
        for kd in range(8):
            for c0 in range(0, n, 1024):
                sf, sb = stage()
                dma(sf.a[:, 0:1024], src[kd * 128:(kd + 1) * 128, c0:c0 + 1024], [], [sf.b])
                if goff is None:
                    cp(sb.a[:, 0:1024], sf.a[:, 0:1024], [sf.b], [sb.b])
                else:
                    ts(sb.a[:, 0:1024], sf.a[:, 0:1024], gc.a[:, goff + kd:goff + kd + 1], None, ALU.mult, None,
                       [sf.b, gc.b], [sb.b])
                dma(dst[:, kd, c0:c0 + 1024], sb.a[:, 0:1024], [sb.b], [wscr_b[key]])

    ut_b = Buf("UT"); vb_b = Buf("VB")
    for bt in range(NE // 512):
        e0 = bt * 512
        sf, sb = stage()
        dma(sf.a.rearrange("p (a d) -> p a d", a=4), peer_u[e0:e0 + 512, :].rearrange("(a p) d -> p a d", p=128), [], [sf.b])
        cp(sb.a, sf.a, [sf.b], [sb.b])
        sf2, utsb = stage()
        for a in range(4):
            pb = bank()
            for kd in range(8):
                tr(pb.bf[:, kd * 128:(kd + 1) * 128], sb.a[:, a * 1024 + kd * 128:a * 1024 + (kd + 1) * 128], ident.a,
                   [sb.b, ident.b], [pb.b])
            tt(utsb.a.rearrange("p (k e) -> p k e", k=8)[:, :, a * 128:(a + 1) * 128],
               pb.bf.rearrange("p (k e) -> p k e", k=8),
               gc.a[:, 8:16].unsqueeze(2).broadcast_to([128, 8, 128]), ALU.mult, [pb.b, gc.b], [utsb.b])
        dma(UT[:, :, e0:e0 + 512].rearrange("k p e -> p k e"), utsb.a.rearrange("p (k e) -> p k e", k=8), [utsb.b], [ut_b])
        sf, sb = stage()
        dma(sf.a.rearrange("p (a d) -> p a d", a=4), peer_v[e0:e0 + 512, :].rearrange("(a p) d -> p a d", p=128), [], [sf.b])
        cp(sb.a, sf.a, [sf.b], [sb.b], eng="scalar")
        dma(VB[e0:e0 + 512, :].rearrange("(a p) d -> p a d", p=128), sb.a.rearrange("p (a d) -> p a d", a=4), [sb.b], [vb_b])

    ntile = 17
    sc_b = [Buf(f"SC{i}") for i in range(ntile)]
    z_b = [Buf(f"Z{i}") for i in range(ntile)]
    e_b = [Buf(f"E{i}") for i in range(ntile)]
    zinit = Buf("zinit")
    dma(Zp[0:1, :], zero.a[0:1, :], [zero.b], [zinit])
    dma(Ep[0:15, :], zero.a[0:15, 0:512], [zero.b], [zinit])
    arena_reset(mark1)

    mu_bc = bc_row("mu_bc", 0, DSH, A)
    w0_bc = bc_row("w0_bc", 1792, 512, A); a0_bc = bc_row("a0_bc", 2304, 512, A)
    kk_bc = bc_row("kk_bc", 2816, 512, A); ka_bc = bc_row("ka_bc", 3328, 512, A); rk_bc = bc_row("rk_bc", 3840, 512, A)
    xnT = A("xnT", [128, 8, 128], BF16)
    zsb = A("zsb", [128, DIN]); zpv = A("zpv", [128, DSH]); zs = A("zs", [128, DSH])
    Lb = A("Lb", [128, 256], BF16); LT = A("LT", [128, 2, 128], BF16)
    scst = A("scst", [128, 8, 8, 64]); aa = A("aa", [128, 512])

    def rms_T(src, nt, dstT):
        sumsq(src, nt)
        ts(rstd.a[:nt], ssq.a[:nt], 1.0 / D, 1e-6, ALU.mult, ALU.add, [ssq.b], [rstd.b])
        rsqrt(rstd.a[:nt], rstd.b)
        ts(xnb.a[:nt], src.a[:nt], rstd.a[:nt, 0:1], None, ALU.mult, None, [src.b, rstd.b], [xnb.b])
        pb = bank()
        for kd in range(8):
            tr(pb.bf[:, kd * 128:kd * 128 + nt], xnb.a[:nt, kd * 128:(kd + 1) * 128], ident.a[:nt, :nt], [xnb.b, ident.b], [pb.b])
        cp(dstT.a[:, :, :nt], pb.bf.rearrange("p (k t) -> p k t", k=8)[:, :, :nt], [pb.b], [dstT.b], eng="scalar")

    def v3(ap):
        return ap.rearrange("p (h n) -> p h n", h=8)

    def phase1(ti):
        samp = ti == 16
        nt = NS if samp else 128
        t0 = ti * 128
        dma(xt.a[:nt], x_s if samp else x_p[t0:t0 + nt, :], [], [xt.b])
        rms_T(xt, nt, xnT)
        for c0 in range(0, DIN, 512):
            n = min(512, DIN - c0)
            pb = bank()
            for kd in range(8):
                mm(pb.a[:nt, :n], xnT.a[:, kd, :nt], win.a[:, kd, c0:c0 + n], kd == 0, kd == 7, [xnT.b, win.b], [pb.b])
            cp(zsb.a[:nt, c0:c0 + n], pb.a[:nt, :n], [pb.b], [zsb.b], eng="scalar" if (c0 // 512) % 2 else "vector")
        if not samp:
            dma(Zp[1 + t0:1 + t0 + nt, :], zsb.a[:nt, 0:DSH], [zsb.b], [z_b[ti]])
            dma(Ep[15 + t0:15 + t0 + nt, :], zsb.a[:nt, DSH:DIN], [zsb.b], [e_b[ti]])
            rd = [z_b[ti], zinit] + ([z_b[ti - 1]] if ti > 0 else [])
            dma(zpv.a[:nt], Zp[t0:t0 + nt, :], rd, [zpv.b])
        else:
            dma(Zs, zsb.a[:nt, 0:DSH], [zsb.b], [z_b[ti]])
            dma(Us, zsb.a[:nt, DSH:DIN], [zsb.b], [e_b[ti]])
            zq = Buf("zsp")
            dma(Zsp.rearrange("(s t) f -> s t f", t=4)[:, 0, :], st_shift, [], [zq], q="sync")
            dma(Zsp.rearrange("(s t) f -> s t f", t=4)[:, 1:4, :], Zs.rearrange("(s t) f -> s t f", t=4)[:, 0:3, :],
                [z_b[ti]], [zq], q="sync")
            dma(zpv.a[:nt], Zsp, [zq], [zpv.b], q="sync")
            eq = Buf("es")
            dma(Es[:, 0:15, :], st_pool, [], [eq], q="sync")
            dma(Es[:, 15:19, :], Us.rearrange("(s t) c -> s t c", t=4), [e_b[ti]], [eq], q="sync")
            eq2 = Buf("esf")
            for k in range(16):
                dma(Esf[k].rearrange("(s t) c -> s t c", t=4), Es[:, 15 - k:19 - k, :], [eq], [eq2], q="sync")
            e_b[ti] = eq2
            dma(o_shs, Zs.rearrange("(s t) f -> s t f", t=4)[:, 3, :], [z_b[ti]], [Buf("o")], q="sync")
            dma(o_pos, Es[:, 4:19, :], [eq], [Buf("o")], q="sync")
        tt(zs.a[:nt], zpv.a[:nt], zsb.a[:nt, 0:DSH], ALU.subtract, [zpv.b, zsb.b], [zs.b])
        tt(zs.a[:nt], zs.a[:nt], mu_bc.a[:nt], ALU.mult, [zs.b, mu_bc.b], [zs.b])
        tt(zs.a[:nt], zs.a[:nt], zsb.a[:nt, 0:DSH], ALU.add, [zs.b, zsb.b], [zs.b])
        r_ = zs.a[:nt, 0:512]; k_ = zs.a[:nt, 512:1024]; v_ = zs.a[:nt, 1024:1536]
        sq = lambda q: scst.a[:nt, :, q, :]
        act(Lb.a[:nt, 0:64], zs.a[:nt, 1536:1600], AF.Tanh, [zs.b], [Lb.b])
        cp(Lb.a[:nt, 64:128], zs.a[:nt, 1600:1664], [zs.b], [Lb.b])
        act(Lb.a[:nt, 128:256], zs.a[:nt, 1664:1792], AF.Sigmoid, [zs.b], [Lb.b])
        pb = bank()
        for j in range(2):
            tr(pb.bf[:, j * 128:j * 128 + nt], Lb.a[:nt, j * 128:(j + 1) * 128], ident.a[:nt, :nt], [Lb.b, ident.b], [pb.b])
        cp(LT.a[:, :, :nt], pb.bf[:, 0:256].rearrange("p (k t) -> p k t", k=2)[:, :, :nt], [pb.b], [LT.b])
        pd = bank(); pa = bank(); pg = bank()
        mm(pd.a[:nt], LT.a[0:64, 0, :nt], decb.a[0:64, :], True, True, [LT.b, decb.b], [pd.b])
        mm(pa.a[:nt], LT.a[64:128, 0, :nt], abb.a[64:128, :], True, True, [LT.b, abb.b], [pa.b])
        mm(pg.a[:nt], LT.a[:, 1, :nt], gbb.a, True, True, [LT.b, gbb.b], [pg.b])
        tt(t1.a[:nt], pd.a[:nt], w0_bc.a[:nt], ALU.add, [pd.b, w0_bc.b], [t1.b])
        act(t1.a[:nt], t1.a[:nt], AF.Sigmoid, [t1.b], [t1.b])
        act(sq(1), v3(t1.a[:nt]), AF.Exp, [t1.b], [scst.b], scale=-0.6065306597126334)
        ts(sq(7), v3(t1.a[:nt]), -0.6065306597126334, None, ALU.mult, None, [t1.b], [scst.b])
        tt(aa.a[:nt], pa.a[:nt], a0_bc.a[:nt], ALU.add, [pa.b, a0_bc.b], [aa.b])
        act(aa.a[:nt], aa.a[:nt], AF.Sigmoid, [aa.b], [aa.b])
        cp(sq(6), v3(pg.a[:nt]), [pg.b], [scst.b], eng="scalar")
        cp(sq(0), v3(r_), [zs.b], [scst.b])
        cp(sq(5), v3(v_), [zs.b], [scst.b], eng="scalar")
        tt(t1.a[:nt], k_, kk_bc.a[:nt], ALU.mult, [zs.b, kk_bc.b], [t1.b])
        tt(t2.a[:nt], t1.a[:nt], t1.a[:nt], ALU.mult, [t1.b], [t2.b])
        red(h8.a[:nt], v3(t2.a[:nt]), [t2.b], [h8.b])
        ts(rn8.a[:nt], h8.a[:nt], 1e-24, None, ALU.add, None, [h8.b], [rn8.b])
        rsqrt(rn8.a[:nt], rn8.b)
        ts(rn8.a[:nt], rn8.a[:nt], -1.0, None, ALU.mult, None, [rn8.b], [rn8.b])
        tt(sq(3), v3(t1.a[:nt]), rn8.a[:nt].unsqueeze(2).broadcast_to([nt, 8, 64]), ALU.mult,
           [t1.b, rn8.b], [scst.b])
        stt(sq(4), sq(3), -1.0, v3(aa.a[:nt]), ALU.mult, ALU.mult, [scst.b, aa.b], [scst.b])
        stt(t1.a[:nt], aa.a[:nt], -1.0, ka_bc.a[:nt], ALU.add, ALU.mult, [aa.b, ka_bc.b], [t1.b])
        stt(sq(2), v3(t1.a[:nt]), 1.0, v3(k_), ALU.add, ALU.mult, [t1.b, zs.b], [scst.b])
        tt(v3(t2.a[:nt]), v3(r_), sq(2), ALU.mult, [zs.b, scst.b], [t2.b])
        tt(t2.a[:nt], t2.a[:nt], rk_bc.a[:nt], ALU.mult, [t2.b, rk_bc.b], [t2.b])
        red(bn8.a[:nt], v3(t2.a[:nt]), [t2.b], [bn8.b])
        tk0 = SEQ if samp else t0
        dma(SC.rearrange("t h q j -> t (h q j)")[tk0:tk0 + nt], scst.a[:nt].rearrange("p h q j -> p (h q j)"), [scst.b], [sc_b[ti]])
        dma(BN[tk0:tk0 + nt, :], bn8.a[:nt], [bn8.b], [sc_b[ti]])

    for ti in range(ntile):
        phase1(ti)
    dma(o_shp, Zp[SEQ:SEQ + 1, :], [z_b[15]], [Buf("o")], q="sync")
    dma(o_pop, Ep[SEQ:SEQ + 15, :], [e_b[15]], [Buf("o")], q="sync")

    os_b = [Buf(f"OS{i}") for i in range(ntile)]
    arena_reset(0)
    cm = A("cm", [128, 4, 128])
    dma(cm.a, cmask, [], [cm.b])
    mk2 = A("mk2", [128, 4, 128])
    for i_ in range(4):
        cp(mk2.a[:, i_, :], cm.a[:, i_ % 2, :], [cm.b], [mk2.b])
    sct = [A("sct0", [128, 8, 8, 64]), A("sct1", [128, 8, 8, 64])]
    lwc = A("lwc", [128, 512]); cum = A("cum", [128, 512]); tmpc = A("tmpc", [128, 512])
    Epl = A("Epl", [128, 512]); Emi = A("Emi", [128, 512]); Eprv = A("Eprv", [128, 512]); Ehat = A("Ehat", [128, 512])
    gamT = A("gamT", [128, 4])
    At = A("At", [128, 512]); Rt = A("Rt", [128, 512]); Bt = A("Bt", [128, 512]); Kt = A("Kt", [128, 512])
    Bh = A("Bh", [128, 512]); Kh = A("Kh", [128, 512]); Vc = A("Vc", [128, 512])
    ART = A("ART", [128, 4, 2, 128]); BT = A("BT", [128, 4, 128]); KT = A("KT", [128, 4, 128])
    ARTz = [A("ARTz0", [128, 4, 2, 128]), A("ARTz1", [128, 4, 2, 128])]
    BTz = [A("BTz0", [128, 4, 128]), A("BTz1", [128, 4, 128])]
    Hz = [A("Hz0", [128, 4, 64]), A("Hz1", [128, 4, 64])]
    for z_ in ARTz + BTz + Hz:
        memset(z_.a, 0.0, [z_.b])
    G = A("G", [128, 8, 512])
    Yb = [A("Yb0", [128, 8, 128], BF16), A("Yb1", [128, 8, 128], BF16)]
    Lb2 = [A("Lb0", [128, 8, 128]), A("Lb1", [128, 8, 128], BF16), A("Lb2", [128, 8, 128], BF16)]
    TTm = A("TTm", [128, 8, 128]); TLm = A("TLm", [128, 8, 128])
    TTb = A("TTb", [128, 8, 128], BF16); TLb = A("TLb", [128, 8, 128], BF16)
    Wsb = A("Wsb", [128, 512]); Usb = A("Usb", [128, 512]); Osb = [A("Osb0", [128, 512]), A("Osb1", [128, 512])]
    Hst = A("Hst", [128, 4, 64]); Hs2 = A("Hs2", [128, 4, 64])
    ones_col = cm.a[:, 3, 0:1]
    mcol = [cm.a[:, 0, 64:65], cm.a[:, 2, 63:64]]
    evi = [0]

    def evac(out, in_, r, w):
        evi[0] += 1
        cp(out, in_, r, w, eng="scalar" if evi[0] % 2 else "vector")

    import os
    LVL = int(os.environ.get("KLVL", "9"))
    for ci in range(SEQ // 128):
        t0 = ci * 128
        sc_ = sct[ci % 2]
        dma(sc_.a.rearrange("p h q j -> p (h q j)"), SC.rearrange("t h q j -> t (h q j)")[t0:t0 + 128], [sc_b[ci]], [sc_.b])
        X = lambda q: sc_.a[:, :, q, :]
        cp(v3(lwc.a), X(7), [sc_.b], [lwc.b])
        cp(v3(Vc.a), X(5), [sc_.b], [Vc.b], eng="scalar")
        pc = bank(); pt = bank(); pg = bank()
        mm(pc.a, cm.a[:, 1, :], lwc.a, True, True, [cm.b, lwc.b], [pc.b])
        mm(pt.a, cm.a[:, 3, :], lwc.a, True, True, [cm.b, lwc.b], [pt.b])
        for hp in range(4):
            mm(pg.a[:, hp:hp + 1], lwc.a[:, hp * 128:(hp + 1) * 128], ones_col, True, True, [lwc.b, cm.b], [pg.b])
        cp(cum.a, pc.a, [pc.b], [cum.b])
        act(Epl.a, cum.a, AF.Exp, [cum.b], [Epl.b])
        act(Emi.a, cum.a, AF.Exp, [cum.b], [Emi.b], scale=-1.0)
        tt(tmpc.a, cum.a, lwc.a, ALU.subtract, [cum.b, lwc.b], [tmpc.b])
        act(Eprv.a, tmpc.a, AF.Exp, [tmpc.b], [Eprv.b])
        tt(tmpc.a, pt.a, cum.a, ALU.subtract, [pt.b, cum.b], [tmpc.b])
        act(Ehat.a, tmpc.a, AF.Exp, [tmpc.b], [Ehat.b])
        act(gamT.a, pg.a[:, 0:4], AF.Exp, [pg.b], [gamT.b])
        tt(v3(At.a), X(3), v3(Eprv.a), ALU.mult, [sc_.b, Eprv.b], [At.b])
        tt(v3(Bt.a), X(4), v3(Emi.a), ALU.mult, [sc_.b, Emi.b], [Bt.b])
        tt(v3(Kt.a), X(2), v3(Emi.a), ALU.mult, [sc_.b, Emi.b], [Kt.b])
        tt(v3(Rt.a), X(0), v3(Epl.a), ALU.mult, [sc_.b, Epl.b], [Rt.b])
        tt(v3(Bh.a), X(4), v3(Ehat.a), ALU.mult, [sc_.b, Ehat.b], [Bh.b])
        tt(v3(Kh.a), X(2), v3(Ehat.a), ALU.mult, [sc_.b, Ehat.b], [Kh.b])
        for (src_, dst_) in ((At, ART.a[:, :, 0, :]), (Rt, ART.a[:, :, 1, :]), (Bt, BT.a), (Kt, KT.a)):
            pb = bank()
            for hp in range(4):
                tr(pb.a[:, hp * 128:(hp + 1) * 128], src_.a[:, hp * 128:(hp + 1) * 128], ident_f.a, [src_.b, ident_f.b], [pb.b])
            dbuf = ART.b if src_ in (At, Rt) else (BT.b if src_ is Bt else KT.b)
            p3_ = pb.a.rearrange("p (a t) -> p a t", a=4)
            evac(dst_, p3_, [pb.b], [dbuf])
            if src_ is not Kt:
                for hh in range(2):
                    if src_ is Bt:
                        zt, zd = BTz[hh], BTz[hh].a
                    else:
                        zt, zd = ARTz[hh], ARTz[hh].a[:, :, 0 if src_ is At else 1, :]
                    ts(zd, dst_, mcol[hh], None, ALU.mult, None, [dbuf, cm.b], [zt.b])
        if LVL < 2:
            continue
        for h in range(8):
            hp, hh = divmod(h, 2)
            ps_ = slice(hh * 64, (hh + 1) * 64)
            pb = bank()
            arr = ARTz[hh].a[:, hp, :, :].rearrange("p a t -> p (a t)")
            mm(pb.a[:, 0:256], BT.a[:, hp, :], arr, True, True, [BT.b, ARTz[hh].b], [pb.b])
            mm(pb.a[:, 256:512], KT.a[:, hp, :], arr, True, True, [KT.b, ARTz[hh].b], [pb.b])
            tt(G.a[:, h, :], pb.a, mk2.a.rearrange("p a t -> p (a t)"), ALU.mult, [pb.b, mk2.b], [G.b])
        for g4 in range(2):
            pb = bank()
            for hq in range(4):
                h = g4 * 4 + hq
                hp, hh = divmod(h, 2)
                ps_ = slice(hh * 64, (hh + 1) * 64)
                mm(pb.a[:, hq * 128:(hq + 1) * 128], ART.a[:, hp, 0, :], BTz[hh].a[:, hp, :], True, True, [ART.b, BTz[hh].b], [pb.b])
            tt(Lb2[0].a[:, g4 * 4:(g4 + 1) * 4, :], pb.a.rearrange("p (a t) -> p a t", a=4),
               cm.a[:, 2:3, :].broadcast_to([128, 4, 128]), ALU.mult, [pb.b, cm.b], [Lb2[0].b])
        if LVL < 3:
            continue
        idb = ident_f.a.unsqueeze(1).broadcast_to([128, 8, 128])
        tt(TTm.a, G.a[:, :, 0:128], idb, ALU.add, [G.b, ident_f.b], [TTm.b])
        tt(TLm.a, Lb2[0].a, idb, ALU.add, [Lb2[0].b, ident_f.b], [TLm.b], eng="gpsimd")
        cp(Yb[0].a, G.a[:, :, 0:128], [G.b], [Yb[0].b], eng="scalar")
        cp(Lb2[1].a, Lb2[0].a, [Lb2[0].b], [Lb2[1].b], eng="scalar")
        cp(TTb.a, TTm.a, [TTm.b], [TTb.b])
        cp(TLb.a, TLm.a, [TLm.b], [TLb.b], eng="scalar")
        Yp_t = Yb[0]
        Lp = Lb2[1]
        for k in range(1, 7):
            Ln_ = Lb2[1 + (k % 2)]; Yn = Yb[k % 2]
            for g4 in range(2):
                py = bank(); pl_ = bank()
                for hq in range(4):
                    h = g4 * 4 + hq
                    mm(py.a[:, hq * 128:(hq + 1) * 128], Lp.a[:, h, :], Yp_t.a[:, h, :], True, True, [Lp.b, Yp_t.b], [py.b])
                for hq in range(4):
                    h = g4 * 4 + hq
                    mm(pl_.a[:, hq * 128:(hq + 1) * 128], Yp_t.a[:, h, :], Lp.a[:, h, :], True, True, [Lp.b, Yp_t.b], [pl_.b])
                cp(Yn.a[:, g4 * 4:(g4 + 1) * 4, :], py.a.rearrange("p (a t) -> p a t", a=4), [py.b], [Yn.b])
                cp(Ln_.a[:, g4 * 4:(g4 + 1) * 4, :], pl_.a.rearrange("p (a t) -> p a t", a=4), [pl_.b], [Ln_.b], eng="scalar")
            upd = []
            for g4 in range(2):
                p1 = bank()
                for hq in range(4):
                    h = g4 * 4 + hq
                    mm(p1.a[:, hq * 128:(hq + 1) * 128], TLb.a[:, h, :], Yn.a[:, h, :], True, True, [TLb.b, Yn.b], [p1.b])
                p2 = None
                if k < 6:
                    p2 = bank()
                    for hq in range(4):
                        h = g4 * 4 + hq
                        mm(p2.a[:, hq * 128:(hq + 1) * 128], TTb.a[:, h, :], Ln_.a[:, h, :], True, True, [TTb.b, Ln_.b], [p2.b])
                upd.append((g4, p1, p2))
            for (g4, p1, p2) in upd:
                sl_ = slice(g4 * 4, (g4 + 1) * 4)
                tt(TTm.a[:, sl_, :], TTm.a[:, sl_, :], p1.a.rearrange("p (a t) -> p a t", a=4), ALU.add, [TTm.b, p1.b], [TTm.b])
                if p2 is not None:
                    tt(TLm.a[:, sl_, :], TLm.a[:, sl_, :], p2.a.rearrange("p (a t) -> p a t", a=4), ALU.add, [TLm.b, p2.b], [TLm.b])
            if k < 6:
                cp(TTb.a, TTm.a, [TTm.b], [TTb.b], eng="scalar")
                cp(TLb.a, TLm.a, [TLm.b], [TLb.b], eng="scalar")
            Yp_t = Yn
            Lp = Ln_
        first = ci == 0
        hsl = lambda h: slice(h * 64, (h + 1) * 64)
        pw = bank()
        for h in range(8):
            hp, hh = divmod(h, 2)
            ps_ = slice(hh * 64, (hh + 1) * 64)
            if not first:
                mm(pw.a[:, hsl(h)], ART.a[:, hp, 0, :], Hz[hh].a[:, hp, :], True, False, [ART.b, Hz[hh].b], [pw.b])
            mm(pw.a[:, hsl(h)], G.a[:, h, 256:384], Vc.a[:, hsl(h)], first, True, [G.b, Vc.b], [pw.b])
        cp(Wsb.a, pw.a, [pw.b], [Wsb.b])
        pu = bank()
        for h in range(8):
            mm(pu.a[:, hsl(h)], TTm.a[:, h, :], Wsb.a[:, hsl(h)], True, True, [TTm.b, Wsb.b], [pu.b])
        cp(Usb.a, pu.a, [pu.b], [Usb.b], eng="scalar")
        po = bank()
        for h in range(8):
            hp, hh = divmod(h, 2)
            ps_ = slice(hh * 64, (hh + 1) * 64)
            if not first:
                mm(po.a[:, hsl(h)], ART.a[:, hp, 1, :], Hz[hh].a[:, hp, :], True, False, [ART.b, Hz[hh].b], [po.b])
            mm(po.a[:, hsl(h)], G.a[:, h, 128:256], Usb.a[:, hsl(h)], first, False, [G.b, Usb.b], [po.b])
            mm(po.a[:, hsl(h)], G.a[:, h, 384:512], Vc.a[:, hsl(h)], False, True, [G.b, Vc.b], [po.b])
        ob_ = Osb[ci % 2]
        cp(ob_.a, po.a, [po.b], [ob_.b])
        dma(OS[t0:t0 + 128, :], ob_.a, [ob_.b], [os_b[ci]])
        ph = bank()
        for h in range(8):
            hp, hh = divmod(h, 2)
            ps_ = slice(hh * 64, (hh + 1) * 64)
            mm(ph.a[:, h * 64:(h + 1) * 64], Bh.a[:, hp * 128:(hp + 1) * 128], Usb.a[:, hsl(h)], True, False, [Bh.b, Usb.b], [ph.b])
            mm(ph.a[:, h * 64:(h + 1) * 64], Kh.a[:, hp * 128:(hp + 1) * 128], Vc.a[:, hsl(h)], False, True, [Kh.b, Vc.b], [ph.b])
        ph4 = ph.a.rearrange("p (a b i) -> p a b i", a=4, b=2)
        for hh in range(2):
            if first:
                ts(Hz[hh].a, ph4[:, :, hh, :], mcol[hh], None, ALU.mult, None, [ph.b, cm.b], [Hz[hh].b])
            else:
                tt(Hs2.a, Hz[hh].a, gamT.a.unsqueeze(2).broadcast_to([128, 4, 64]), ALU.mult, [Hz[hh].b, gamT.b], [Hs2.b])
                tt(Hs2.a, Hs2.a, ph4[:, :, hh, :], ALU.add, [Hs2.b, ph.b], [Hs2.b])
                ts(Hz[hh].a, Hs2.a, mcol[hh], None, ALU.mult, None, [Hs2.b, cm.b], [Hz[hh].b])
    Sfin = A("Sfin", [64, 512])
    tt(Hst.a, Hz[0].a, Hz[1].a, ALU.add, [Hz[0].b, Hz[1].b], [Hst.b])
    pb = bank()
    for hp in range(4):
        tr(pb.a[0:64, hp * 128:(hp + 1) * 128], Hst.a[:, hp, :], ident_f.a, [Hst.b, ident_f.b], [pb.b])
    cp(Sfin.a, pb.a[0:64, :], [pb.b], [Sfin.b])
    dma(bass.AP(o_wkp.tensor, 0, [[64, 64], [4096, 8], [1, 64]]), Sfin.a.rearrange("p (h j) -> p h j", h=8), [Sfin.b], [Buf("o")])

    arena_reset(0)
    Sst = A("Sst", [128, 4096]); stmp = A("stmp", [128, 4096])
    vecs = [A("vec0", [128, 4, 5, 64])]
    vk = A("vk", [128, 64]); skk = A("skk", [128, 64])

    def scan_step(Sv, nI, kkn, w, kka, vkt, r, oslot, rb, tmpv):
        bcv = lambda a: a.unsqueeze(1).broadcast_to([128, nI, 64])
        tt(tmpv, Sv, bcv(kkn), ALU.mult, [Sst.b] + rb, [stmp.b])
        red(skk.a[:, :nI], tmpv, [stmp.b], [skk.b])
        tt(Sv, Sv, bcv(w), ALU.mult, [Sst.b] + rb, [Sst.b])
        tt(tmpv, skk.a[:, :nI].unsqueeze(2).broadcast_to([128, nI, 64]), bcv(kka), ALU.mult, [skk.b] + rb, [stmp.b])
        tt(Sv, Sv, tmpv, ALU.add, [Sst.b, stmp.b], [Sst.b])
        tt(Sv, Sv, vkt, ALU.add, [Sst.b, vk.b], [Sst.b])
        tt(tmpv, Sv, bcv(r), ALU.mult, [Sst.b] + rb, [stmp.b])
        return tmpv

    vec = vecs[0]; vt = A("vts", [128, 4, 64]); oo_s = A("oo_s", [128, 4, 64])
    dma(Sst.a, st_wkv, [], [Sst.b])
    for s_ in range(16):
        src = bass.AP(SC.tensor, (SEQ + s_ * 4) * 4096, [[512, 8], [4096, 4], [1, 320]])
        dma(vec.a[s_ * 8:(s_ + 1) * 8, 0:4].rearrange("p t q j -> p t (q j)"), src, [sc_b[16]], [vec.b])
        srcv = bass.AP(SC.tensor, (SEQ + s_ * 4) * 4096 + 320, [[512, 8], [4096, 4], [1, 64]])
        dma(vt.a[s_ * 8:(s_ + 1) * 8], srcv, [sc_b[16]], [vt.b])
    Ss = Sst.a.rearrange("p (i j) -> p i j", i=64)
    tmps = stmp.a.rearrange("p (i j) -> p i j", i=64)
    vks = A("vks", [128, 4096])
    vks3 = vks.a.rearrange("p (i j) -> p i j", i=64)
    for t in range(4):
        tt(vks3, vt.a[:, t, :].unsqueeze(2).broadcast_to([128, 64, 64]),
           vec.a[:, t, 2, :].unsqueeze(1).broadcast_to([128, 64, 64]), ALU.mult, [vt.b, vec.b], [vk.b])
        tmpv = scan_step(Ss, 64, vec.a[:, t, 3, :], vec.a[:, t, 1, :], vec.a[:, t, 4, :], vks3, vec.a[:, t, 0, :],
                         None, [vec.b], tmps)
        red(oo_s.a[:, t, :], tmpv, [stmp.b], [oo_s.b])
    dma(o_wks, Sst.a, [Sst.b], [Buf("o")])
    for s_ in range(16):
        dst = bass.AP(OS.tensor, (SEQ + s_ * 4) * 512, [[64, 8], [512, 4], [1, 64]])
        dma(dst, oo_s.a[s_ * 8:(s_ + 1) * 8], [oo_s.b], [os_b[16]])

    arena_reset(0)
    nbank[0] = 6
    wbuf = A("wbuf", [128, 8, 1024], BF16)
    Mb = [A("Mb0", [128, 8, 1024], BF16), A("Mb1", [128, 8, 1024], BF16)]
    actT = A("actT", [128, 16, 128], BF16)
    NLA = 2
    sig = [A(f"sig{i}", [128, 1024]) for i in range(NLA)]
    ebf = [A(f"ebf{i}", [128, 1024], BF16) for i in range(NLA)]
    utb = [A("utb0", [128, 8, 1024], BF16), A("utb1", [128, 8, 1024], BF16)]
    vbb = [A("vbb0", [128, 4, D], BF16), A("vbb1", [128, 4, D], BF16)]
    gl2 = [A(f"gl{i}", [128, 512], BF16) for i in range(2)]
    actb = [A(f"actb{i}", [128, 512], BF16) for i in range(2)]
    ht = A("ht", [128, D]); ob = A("ob", [128, 512]); vv = A("vv", [128, 512]); gg = A("gg", [128, 512]); ub = A("ub", [128, 512])
    shb = [A(f"shb{i}", [128, 512]) for i in range(3)]
    ysb = A("ysb", [128, D], BF16); yT = A("yT", [128, 8, 128], BF16); x2T = A("x2T", [128, 8, 128], BF16)
    qT = A("qT", [128, 16, 128], BF16); sall = A("sall", [128, 16, 128])
    v16 = A("v16", [128, 16, 16]); srep = A("srep", [128, 128]); cand = A("cand", [128, 256]); crep = A("crep", [128, 256])
    c16 = A("c16", [128, 8, 16]); thr = A("thr", [128, 8]); bia = A("bia", [128, 8]); zz = A("zz", [128, 8]); nm = A("nm", [128, 8])
    gel = A("gel", [128, 512], BF16); ptb = A("ptb", [128, 256]); ptbb = A("ptbb", [128, 256], BF16); pT = A("pT", [128, 2, 128], BF16)
    m8 = A("m8", [128, 8]); m4 = A("m4", [128, 8]); pl = A("pl", [128, D])
    psO = [PSfix for PSfix in banks[6:8]]

    def bank6():
        b = banks[bank_i[0] % 6]
        bank_i[0] += 1
        return b

    def loadw(scr, n, key):
        dma(wbuf.a[:, :, 0:n], scr, [wscr_b[key]], [wbuf.b], q="sync")

    def phase2(ti):
        samp = ti == 16
        nt = NS if samp else 128
        t0 = ti * 128
        tk0 = SEQ if samp else t0
        dma(ob.a[:nt], OS[tk0:tk0 + nt, :], [os_b[ti]], [ob.b])
        dma(v3(vv.a[:nt]), SC[tk0:tk0 + nt, :, 5, :], [sc_b[ti]], [vv.b])
        dma(v3(gg.a[:nt]), SC[tk0:tk0 + nt, :, 6, :], [sc_b[ti]], [gg.b])
        dma(bn8.a[:nt], BN[tk0:tk0 + nt, :], [sc_b[ti]], [bn8.b])
        red(h8.a[:nt], v3(ob.a[:nt]), [ob.b], [h8.b])
        ts(h8.a[:nt], h8.a[:nt], 1.0 / 64, None, ALU.mult, None, [h8.b], [h8.b])
        tt(v3(ob.a[:nt]), v3(ob.a[:nt]), h8.a[:nt].unsqueeze(2).broadcast_to([nt, 8, 64]), ALU.subtract, [ob.b, h8.b], [ob.b])
        tt(t2.a[:nt], ob.a[:nt], ob.a[:nt], ALU.mult, [ob.b], [t2.b])
        red(rn8.a[:nt], v3(t2.a[:nt]), [t2.b], [rn8.b])
        ts(rn8.a[:nt], rn8.a[:nt], 1.0 / 64, 64e-5, ALU.mult, ALU.add, [rn8.b], [rn8.b])
        rsqrt(rn8.a[:nt], rn8.b)
        tt(v3(ob.a[:nt]), v3(ob.a[:nt]), rn8.a[:nt].unsqueeze(2).broadcast_to([nt, 8, 64]), ALU.mult, [ob.b, rn8.b], [ob.b])
        tt(ob.a[:nt], ob.a[:nt], lg_bc.a[:nt], ALU.mult, [ob.b, lg_bc.b], [ob.b])
        tt(ob.a[:nt], ob.a[:nt], lb_bc.a[:nt], ALU.add, [ob.b, lb_bc.b], [ob.b])
        tt(v3(vv.a[:nt]), v3(vv.a[:nt]), bn8.a[:nt].unsqueeze(2).broadcast_to([nt, 8, 64]), ALU.mult, [vv.b, bn8.b], [vv.b])
        tt(ob.a[:nt], ob.a[:nt], vv.a[:nt], ALU.add, [ob.b, vv.b], [ob.b])
        tt(ysb.a[:nt, 0:512], ob.a[:nt], gg.a[:nt], ALU.mult, [ob.b, gg.b], [ysb.b])
        if samp:
            dma(ub.a[:nt], Esf[0], [e_b[ti]], [ub.b])
        else:
            dma(ub.a[:nt], Ep[15 + t0:15 + t0 + nt, :], [e_b[ti]], [ub.b])
        cp(t1.a[:nt], ub.a[:nt], [ub.b], [t1.b])
        rdE = [e_b[ti]] + ([e_b[ti - 1]] if (0 < ti < 16) else []) + [zinit]
        for k in range(1, 16):
            c0 = 0 if k < 2 else (128 if k < 4 else (256 if k < 8 else 384))
            sh = shb[k % 3]
            if samp:
                dma(sh.a[:nt, c0:512], Esf[k][:, c0:512], rdE, [sh.b])
            else:
                dma(sh.a[:nt, c0:512], Ep[15 + t0 - k:15 + t0 - k + nt, c0:512], rdE, [sh.b])
            tt(t1.a[:nt, c0:512], t1.a[:nt, c0:512], sh.a[:nt, c0:512], ALU.add, [t1.b, sh.b], [t1.b])
        rci = 0 if ti == 0 else 1
        tt(t1.a[:nt].rearrange("p (g c) -> p g c", g=4), t1.a[:nt].rearrange("p (g c) -> p g c", g=4),
           rc.a[:nt, rci, :].unsqueeze(2).broadcast_to([nt, 4, 128]), ALU.mult, [t1.b, rc.b], [t1.b])
        tt(ptbb_p(nt), t1.a[:nt], ub.a[:nt], ALU.subtract, [t1.b, ub.b], [gel.b])
        pb = bank6()
        for g in range(4):
            tr(pb.bf[:, g * 128:g * 128 + nt], gel.a[:nt, g * 128:(g + 1) * 128], ident.a[:nt, :nt], [gel.b, ident.b], [pb.b])
        cp(qT.a[:, 0:4, :nt], pb.bf[:, 0:512].rearrange("p (g t) -> p g t", g=4)[:, :, :nt], [pb.b], [qT.b])
        pb2 = bank6()
        for g in range(4):
            mm(pb2.a[:nt, g * 128:(g + 1) * 128], qT.a[:, g, :nt], poolw.a[:, g, :], True, True, [qT.b, poolw.b], [pb2.b])
        tt(ysb.a[:nt, 512:1024], pb2.a[:nt], psc_bc.a[:nt], ALU.mult, [pb2.b, psc_bc.b], [ysb.b])
        pb = bank6()
        for kd in range(8):
            tr(pb.bf[:, kd * 128:kd * 128 + nt], ysb.a[:nt, kd * 128:(kd + 1) * 128], ident.a[:nt, :nt], [ysb.b, ident.b], [pb.b])
        cp(yT.a[:, :, :nt], pb.bf.rearrange("p (k t) -> p k t", k=8)[:, :, :nt], [pb.b], [yT.b], eng="scalar")
        dma(xt.a[:nt], x_s if samp else x_p[t0:t0 + nt, :], [], [xt.b])
        loadw(WOs, D, "WO")
        for half in range(2):
            pb = bank6()
            for kd in range(8):
                mm(pb.a[:nt], yT.a[:, kd, :nt], wbuf.a[:, kd, half * 512:(half + 1) * 512], kd == 0, kd == 7, [yT.b, wbuf.b], [pb.b])
            tt(ht.a[:nt, half * 512:(half + 1) * 512], pb.a[:nt], xt.a[:nt, half * 512:(half + 1) * 512], ALU.add,
               [pb.b, xt.b], [ht.b])
        rms_T(ht, nt, x2T)
        for qb in range(4):
            if qb % 2 == 0:
                loadw(WQs[:, :, (qb // 2) * 1024:(qb // 2 + 1) * 1024], 1024, "WQ")
            pb = bank6()
            for j in range(4):
                c = qb * 4 + j
                cl = (qb % 2) * 4 + j
                for kd in range(8):
                    mm(pb.a[:, j * 128:j * 128 + nt], wbuf.a[:, kd, cl * 128:(cl + 1) * 128], x2T.a[:, kd, :nt], kd == 0, kd == 7,
                       [wbuf.b, x2T.b], [pb.b])
            cp(qT.a[:, qb * 4:(qb + 1) * 4, :nt], pb.a.rearrange("p (j t) -> p j t", j=4)[:, :, :nt], [pb.b], [qT.b],
               eng="scalar" if qb % 2 else "vector")
        for qb in range(4):
            pb = bank6()
            for j in range(4):
                c = qb * 4 + j
                mm(pb.a[:nt, j * 128:(j + 1) * 128], qT.a[:, c, :nt], keysT.a[:, c % 2, :], True, True, [qT.b, keysT.b], [pb.b])
            cp(sall.a[:nt, qb * 4:(qb + 1) * 4, :], pb.a[:nt].rearrange("p (j n) -> p j n", j=4), [pb.b], [sall.b],
               eng="scalar" if qb % 2 else "vector")
        for c in range(16):
            S.op("vector", lambda e, c=c: e.max(out=v16.a[:nt, c, 0:8], in_=sall.a[:nt, c, :]), [sall.b], [v16.b])
            S.op("vector", lambda e, c=c: e.match_replace(out=srep.a[:nt], in_to_replace=v16.a[:nt, c, 0:8],
                                                          in_values=sall.a[:nt, c, :], imm_value=-1e30), [sall.b, v16.b], [srep.b])
            S.op("vector", lambda e, c=c: e.max(out=v16.a[:nt, c, 8:16], in_=srep.a[:nt]), [srep.b], [v16.b])
        for h in range(8):
            tt(cand.a[:nt, :].rearrange("p (a b) -> p a b", a=16),
               v16.a[:nt, 2 * h, :].unsqueeze(2).broadcast_to([nt, 16, 16]),
               v16.a[:nt, 2 * h + 1, :].unsqueeze(1).broadcast_to([nt, 16, 16]), ALU.add, [v16.b], [cand.b])
            S.op("vector", lambda e, h=h: e.max(out=c16.a[:nt, h, 0:8], in_=cand.a[:nt, :]), [cand.b], [c16.b])
            S.op("vector", lambda e, h=h: e.match_replace(out=crep.a[:nt], in_to_replace=c16.a[:nt, h, 0:8],
                                                          in_values=cand.a[:nt, :], imm_value=-1e30), [cand.b, c16.b], [crep.b])
            S.op("vector", lambda e, h=h: e.max(out=c16.a[:nt, h, 8:16], in_=crep.a[:nt]), [crep.b], [c16.b])
        cp(thr.a[:nt], c16.a[:nt, :, 15], [c16.b], [thr.b])
        ts(nm.a[:nt], c16.a[:nt, :, 0], -1.0, None, ALU.mult, None, [c16.b], [nm.b])
        memset(zz.a, 0.0, [zz.b])
        for h in range(8):
            act(c16.a[:nt, h, :], c16.a[:nt, h, :], AF.Exp, [c16.b, nm.b], [c16.b, zz.b],
                bias=nm.a[:nt, h:h + 1], accum=zz.a[:nt, h:h + 1])
        act(zz.a[:nt], zz.a[:nt], AF.Ln, [zz.b], [zz.b])
        tt(bia.a[:nt], nm.a[:nt], zz.a[:nt], ALU.subtract, [nm.b, zz.b], [bia.b])
        def emit_sigma(i):
            q, h = divmod(i, 8)
            sb_, eb_ = sig[i % NLA], ebf[i % NLA]
            tt(sb_.a[:nt].rearrange("p (a b) -> p a b", a=8),
               sall.a[:nt, 2 * h, q * 8:q * 8 + 8].unsqueeze(2).broadcast_to([nt, 8, 128]),
               sall.a[:nt, 2 * h + 1, :].unsqueeze(1).broadcast_to([nt, 8, 128]), ALU.add, [sall.b], [sb_.b])
            act(eb_.a[:nt], sb_.a[:nt], AF.Exp, [sb_.b, bia.b], [eb_.b], bias=bia.a[:nt, h:h + 1])

        def emit_mask(i):
            q, h = divmod(i, 8)
            sb_, eb_ = sig[i % NLA], ebf[i % NLA]
            stt(Mb[q % 2].a[:nt, h, :], sb_.a[:nt], thr.a[:nt, h:h + 1], eb_.a[:nt], ALU.is_ge, ALU.mult,
                [sb_.b, thr.b, eb_.b], [Mb[q % 2].b])

        grp = {}

        def stage_A(q):
            ub_ = utb[q % 2]
            for hf in range(2):
                pG = bank(); pP = bank()
                grp[(q, hf)] = [pG, pP, None]
                for h in range(8):
                    mm(pG.a[:nt, :], ident.a[:nt, :nt], Mb[q % 2].a[:nt, h, hf * 512:(hf + 1) * 512], h == 0, h == 7,
                       [Mb[q % 2].b, ident.b], [pG.b])
                for kd in range(8):
                    mm(pP.a[:nt, :], x2T.a[:, kd, :nt], ub_.a[:, kd, hf * 512:(hf + 1) * 512], kd == 0, kd == 7,
                       [x2T.b, ub_.b], [pP.b])

        def stage_B(q, hf):
            pG, pP, _ = grp[(q, hf)]
            gl = gl2[hf]; ab = actb[hf]
            act(gl.a[:nt], pP.a[:nt], AF.Gelu, [pP.b], [gl.b])
            tt(ab.a[:nt], pG.a[:nt], gl.a[:nt], ALU.mult, [pG.b, gl.b], [ab.b])

        def stage_C(q, hf):
            pT = bank()
            grp[(q, hf)][2] = pT
            ab = actb[hf]
            for j in range(4):
                tr(pT.bf[:, j * 128:j * 128 + nt], ab.a[:nt, j * 128:(j + 1) * 128], ident.a[:nt, :nt], [ab.b, ident.b], [pT.b])

        def stage_D(q, hf):
            pT = grp[(q, hf)][2]
            sl = (q % 2) * 8 + hf * 4
            cp(actT.a[:, sl:sl + 4, :nt], pT.bf[:, 0:512].rearrange("p (j t) -> p j t", j=4)[:, :, :nt], [pT.b], [actT.b],
               eng="scalar" if hf else "vector")

        def stage_E(q):
            for c in range(8):
                cg = q * 8 + c
                vb_ = vbb[c // 4]
                for half in range(2):
                    mm(psO[half].a[:nt], actT.a[:, (q % 2) * 8 + c, :nt], vb_.a[:, c % 4, half * 512:(half + 1) * 512],
                       cg == 0, cg == 127, [actT.b, vb_.b], [psO[half].b])

        def emit_loads(q):
            dma(utb[q % 2].a, UT[:, :, q * 1024:(q + 1) * 1024].rearrange("k p e -> p k e"), [ut_b], [utb[q % 2].b])

        def emit_vloads(q):
            for hv in range(2):
                r0 = q * 1024 + hv * 512
                dma(vbb[hv].a, VB[r0:r0 + 512, :].rearrange("(a p) d -> p a d", p=128), [vb_b], [vbb[hv].b])

        def lazy(qp, h):
            if h == 1:
                pG0, pP0, _ = grp[(qp, 0)]; pG1, pP1, _ = grp[(qp, 1)]
                act(gl2[0].a[:nt], pP0.a[:nt], AF.Gelu, [pP0.b], [gl2[0].b])
                act(gl2[1].a[:nt], pP1.a[:nt], AF.Gelu, [pP1.b], [gl2[1].b])
            elif h == 2:
                for hf_ in range(2):
                    pG_, _, _ = grp[(qp, hf_)]
                    tt(actb[hf_].a[:nt], pG_.a[:nt], gl2[hf_].a[:nt], ALU.mult, [pG_.b, gl2[hf_].b], [actb[hf_].b])
            elif h == 3:
                stage_C(qp, 0); stage_C(qp, 1)
            elif h == 4:
                stage_D(qp, 0); stage_D(qp, 1)
            elif h == 5:
                emit_vloads(qp)
                stage_E(qp)

        emit_loads(0)
        for i0_ in range(NLA - 1):
            emit_sigma(i0_)
        for i in range(128):
            q, h = divmod(i, 8)
            if h == 0 and q + 1 < 16:
                emit_loads(q + 1)
            if i + NLA - 1 < 128:
                emit_sigma(i + NLA - 1)
            emit_mask(i)
            if q > 0:
                lazy(q - 1, h)
            if h == 7:
                stage_A(q)
        for h in range(1, 6):
            lazy(15, h)
        for half in range(2):
            tt(ht.a[:nt, half * 512:(half + 1) * 512], ht.a[:nt, half * 512:(half + 1) * 512], psO[half].a[:nt], ALU.add,
               [ht.b, psO[half].b], [ht.b])
        rms_T(ht, nt, x2T)
        loadw(WGs, D, "WG")
        dma(ptb.a[:nt], p_s if samp else p_p[t0:t0 + nt, :], [], [ptb.b])
        cp(ptbb.a[:nt], ptb.a[:nt], [ptb.b], [ptbb.b])
        pb = bank6()
        for j in range(2):
            tr(pb.bf[:, j * 128:j * 128 + nt], ptbb.a[:nt, j * 128:(j + 1) * 128], ident.a[:nt, :nt], [ptbb.b, ident.b], [pb.b])
        cp(pT.a[:, :, :nt], pb.bf[:, 0:256].rearrange("p (k t) -> p k t", k=2)[:, :, :nt], [pb.b], [pT.b])
        for half in range(2):
            pg_ = bank6(); pp_ = bank6()
            for kd in range(8):
                mm(pg_.a[:nt], x2T.a[:, kd, :nt], wbuf.a[:, kd, half * 512:(half + 1) * 512], kd == 0, kd == 7, [x2T.b, wbuf.b], [pg_.b])
            for kd in range(2):
                mm(pp_.a[:nt], pT.a[:, kd, :nt], plw.a[:, kd, half * 512:(half + 1) * 512], kd == 0, kd == 1, [pT.b, plw.b], [pp_.b])
            act(pl.a[:nt, half * 512:(half + 1) * 512], pg_.a[:nt], AF.Sigmoid, [pg_.b], [pl.b])
            tt(pl.a[:nt, half * 512:(half + 1) * 512], pl.a[:nt, half * 512:(half + 1) * 512], pp_.a[:nt], ALU.mult, [pl.b, pp_.b], [pl.b])
        tt(ht.a[:nt], ht.a[:nt], pl.a[:nt], ALU.add, [ht.b, pl.b], [ht.b])
        sumsq(ht, nt)
        ts(rstd.a[:nt], ssq.a[:nt], 1.0 / D, 1e-6, ALU.mult, ALU.add, [ssq.b], [rstd.b])
        rsqrt(rstd.a[:nt], rstd.b)
        stt(pl.a[:nt], ht.a[:nt], rstd.a[:nt, 0:1], fg_bc.a[:nt], ALU.mult, ALU.mult, [ht.b, rstd.b, fg_bc.b], [pl.b])
        dma(y_s if samp else y_p[t0:t0 + nt, :], pl.a[:nt], [pl.b], [Buf("o")])

    def ptbb_p(nt):
        return gel.a[:nt]

    for ti in range(ntile):
        phase2(ti)

    fin = S.final_waits()
    with nc.Block() as block:
        def emit(engine, name):
            for waits, fn, sem, inc in S.ops[name]:
                for (s_, v_) in waits:
                    engine.wait_ge(s_, v_)
                fn(engine).then_inc(sem, inc)

        @block.tensor
        def _(e):
            emit(e, "tensor")

        @block.vector
        def _(e):
            emit(e, "vector")

        @block.scalar
        def _(e):
            emit(e, "scalar")

        @block.gpsimd
        def _(e):
            emit(e, "gpsimd")

        @block.sync
        def _(e):
            emit(e, "sync")
            for (s_, v_) in fin:
                e.wait_ge(s_, v_)
    es.close()
    return nc


_NC = None


def kernel(**inp):
    global _NC
    f = lambda a: np.ascontiguousarray(a, dtype=np.float32)
    rowv = np.concatenate([inp[k][0].reshape(-1) for k in
                           ("shift_mu", "decay_w0", "a_0", "k_k", "k_a", "r_k", "lnx_g", "lnx_b", "pool_scale")]
                          + [inp["final_norm_g"].reshape(-1)]).astype(np.float32)[None, :]
    gcols = np.concatenate([inp[k][0].reshape(8, 128).T for k in ("norm_mix_g", "norm_ffn_g", "norm_ple_g")], axis=1)
    pos = np.arange(128)
    wd = np.array([2, 4, 8, 16])
    rc = np.stack([1.0 / np.minimum(pos[:, None] + 1, wd[None, :]), np.broadcast_to(1.0 / wd[None, :], (128, 4))]).astype(np.float32)
    ii = np.arange(128)
    su = (ii[:, None] < ii[None, :]).astype(np.float32)
    iu = (ii[:, None] <= ii[None, :]).astype(np.float32)
    cmk = np.stack([su, iu, su.T, np.ones((128, 128), np.float32)], axis=1)
    common = dict(
        w_in=f(inp["w_in"][0]), w_out=f(inp["w_out"][0]), wq=f(inp["peer_wq"][0]), ple_w=f(inp["ple_w"][0]),
        ple_gw=f(inp["ple_gate_w"][0]), decay_b=f(inp["decay_b"][0]), a_b=f(inp["a_b"][0]), g_b=f(inp["g_b"][0]),
        pool_w=f(inp["pool_w"][0]), keys=f(inp["peer_keys"][0]), peer_u=f(inp["peer_u"][0]), peer_v=f(inp["peer_v"][0]),
        gcols=f(gcols), rowv=f(rowv), ident=np.eye(128, dtype=np.float32), pool_rc=f(rc), cmask=f(cmk))
    in_maps = []
    for b in range(NCORES):
        m = dict(common)
        sl = slice(16 * b, 16 * b + 16)
        m["x_p"] = f(inp["x_prompt"][b]); m["x_s"] = f(inp["x_sample"][sl].reshape(NS, D))
        m["st_shift"] = f(inp["state_shift"][0, sl]); m["st_wkv"] = f(inp["state_wkv"][0, sl].reshape(128, 4096))
        m["st_pool"] = f(inp["state_pool"][0, sl]); m["p_p"] = f(inp["p_prompt"][0, b]); m["p_s"] = f(inp["p_sample"][0, sl].reshape(NS, 256))
        in_maps.append(m)
    if _NC is None:
        _NC = build_nc()
    res = run_bass_kernel_spmd(_NC, in_maps, core_ids=list(range(NCORES)))
    R = res.results
    y_p = np.stack([R[b]["y_p"] for b in range(NCORES)])
    y_s = np.concatenate([R[b]["y_s"].reshape(16, 4, D) for b in range(NCORES)])
    shp = np.stack([R[b]["o_shp"].reshape(DSH) for b in range(NCORES)])[None]
    wkp = np.stack([R[b]["o_wkp"].reshape(8, 64, 64) for b in range(NCORES)])[None]
    pop = np.stack([R[b]["o_pop"] for b in range(NCORES)])[None]
    shs = np.concatenate([R[b]["o_shs"] for b in range(NCORES)])[None]
    wks = np.concatenate([R[b]["o_wks"].reshape(16, 8, 64, 64) for b in range(NCORES)])[None]
    pos_ = np.concatenate([R[b]["o_pos"] for b in range(NCORES)])[None]
    return tuple(np.ascontiguousarray(a, dtype=np.float32) for a in (y_p, y_s, shp, wkp, pop, shs, wks, pos_))


if __name__ == "__main__":
    import time
    t = time.time()
    nc = build_nc()
    print("built", time.time() - t)
```

```python
import numpy as np
import ml_dtypes
from contextlib import ExitStack
import concourse.bass as bass
import concourse.mybir as mybir
from concourse.bass_utils import run_bass_kernel_spmd

F32 = mybir.dt.float32
BF16 = mybir.dt.bfloat16
ALU = mybir.AluOpType
AF = mybir.ActivationFunctionType
AX = mybir.AxisListType

NCORES = 8
D = 1024
SEQ = 2048
NS = 64
NTOK = SEQ + NS
DSH = 1792
DIN = 2304
NE = 16384
SAME_SYNC = True
EPOCH = 20000
NSLOT = 24
TC = 16


class Buf:
    __slots__ = ("name", "w", "r")

    def __init__(self, name):
        self.name = name
        self.w = None
        self.r = []


class Sched:
    ENG = ["tensor", "vector", "scalar", "gpsimd", "sync"]

    def __init__(self, sems):
        self.free = list(sems)
        self.ops = {e: [] for e in self.ENG}
        self.cur = {e: [self.free.pop(), 0] for e in ["tensor", "vector", "scalar", "gpsimd"]}
        self.seen = {e: {} for e in self.ENG}
        self.slots = [[self.free.pop(), 0] for _ in range(NSLOT)]
        self.rr = 0
        self.pending = {e: [] for e in self.ENG}

    def _need(self, eng, tok, waits, strict=True):
        if tok is None:
            return
        sem, val, src = tok
        if src == eng and (eng == "tensor" or not SAME_SYNC or not (strict or eng == "scalar")):
            return
        k = id(sem)
        if self.seen[eng].get(k, 0) >= val:
            return
        self.seen[eng][k] = val
        waits.append((sem, val))

    def op(self, eng, fn, reads=(), writes=(), dma=False, strict=True):
        waits = self.pending[eng]
        self.pending[eng] = []
        strict = True
        for b in reads:
            self._need(eng, b.w, waits, strict)
        for b in writes:
            self._need(eng, b.w, waits, strict)
            for t in b.r:
                self._need(eng, t, waits, strict)
        if dma:
            slot = self.slots[self.rr]
            self.rr = (self.rr + 1) % NSLOT
            if slot[1] > 0:
                self._need(eng, (slot[0], slot[1], "dma"), waits)
            if slot[1] + 16 > EPOCH * 2:
                slot[0] = self.free.pop()
                slot[1] = 0
            slot[1] += 16
            tok = (slot[0], slot[1], "dma")
            inc = 16
        else:
            c = self.cur[eng]
            if c[1] >= EPOCH:
                c[0] = self.free.pop()
                c[1] = 0
            c[1] += 1
            tok = (c[0], c[1], eng)
            inc = 1
        self.ops[eng].append((waits, fn, tok[0], inc))
        for b in reads:
            b.r.append(tok)
        for b in writes:
            b.w = tok
            b.r = []
        return tok

    def barrier(self):
        toks = [(c[0], c[1], e) for e, c in self.cur.items() if c[1] > 0]
        toks += [(sl[0], sl[1], "dma") for sl in self.slots if sl[1] > 0]
        for e in self.ENG:
            for t in toks:
                if t[2] == e:
                    continue
                k = id(t[0])
                if self.seen[e].get(k, 0) >= t[1]:
                    continue
                self.seen[e][k] = t[1]
                self.pending[e].append((t[0], t[1]))

    def final_waits(self):
        return [(s[0], s[1]) for s in self.slots if s[1] > 0]


def build_nc():
    nc = bass.Bass("TRN2", target_bir_lowering=False)

    def din(name, shape, dt=F32):
        return nc.dram_tensor(name, list(shape), dt, kind="ExternalInput").ap()

    def dout(name, shape):
        return nc.dram_tensor(name, list(shape), F32, kind="ExternalOutput").ap()

    def dscr(name, shape, dt=F32):
        return nc.dram_tensor(name, list(shape), dt, kind="Internal").ap()

    x_p = din("x_p", [SEQ, D]); x_s = din("x_s", [NS, D])
    st_shift = din("st_shift", [16, DSH]); st_wkv = din("st_wkv", [128, 4096]); st_pool = din("st_pool", [16, 15, 512])
    p_p = din("p_p", [SEQ, 256]); p_s = din("p_s", [NS, 256])
    w_in = din("w_in", [D, DIN]); w_out = din("w_out", [D, D]); wq = din("wq", [D, 2048])
    ple_w = din("ple_w", [256, D]); ple_gw = din("ple_gw", [D, D])
    decay_b = din("decay_b", [64, 512]); a_b = din("a_b", [64, 512]); g_b = din("g_b", [128, 512])
    pool_w = din("pool_w", [4, 128, 128]); keys = din("keys", [2, 128, 128])
    peer_u = din("peer_u", [NE, D]); peer_v = din("peer_v", [NE, D])
    gcols = din("gcols", [128, 24])
    rowv = din("rowv", [1, 6912])
    ident_in = din("ident", [128, 128])
    pool_rc = din("pool_rc", [2, 128, 4])
    cmask = din("cmask", [128, 4, 128])
    y_p = dout("y_p", [SEQ, D]); y_s = dout("y_s", [NS, D])
    o_shp = dout("o_shp", [1, DSH]); o_wkp = dout("o_wkp", [128, 256]); o_pop = dout("o_pop", [15, 512])
    o_shs = dout("o_shs", [16, DSH]); o_wks = dout("o_wks", [128, 4096]); o_pos = dout("o_pos", [16, 15, 512])
    Zp = dscr("Zp", [SEQ + 1, DSH]); Ep = dscr("Ep", [SEQ + 15, 512])
    Zs = dscr("Zs", [NS, DSH]); Zsp = dscr("Zsp", [NS, DSH]); Us = dscr("Us", [NS, 512])
    Es = dscr("Es", [16, 19, 512]); Esf = dscr("Esf", [16, NS, 512])
    SC = dscr("SC", [NTOK, 8, 8, 64]); BN = dscr("BN", [NTOK, 8]); OS = dscr("OS", [NTOK, 512])
    UT = dscr("UT", [8, 128, NE], BF16); VB = dscr("VB", [NE, D], BF16)
    WOs = dscr("WOs", [128, 8, D], BF16); WQs = dscr("WQs", [128, 8, 2048], BF16); WGs = dscr("WGs", [128, 8, D], BF16)

    es = ExitStack()
    sems = [es.enter_context(nc.semaphore(f"s{i}")) for i in range(100)]
    S = Sched(sems)

    cur = [es]

    class T:
        def __init__(self, name, shape, dt=F32):
            self.h = cur[0].enter_context(nc.sbuf_tensor(name, list(shape), dt))
            self.a = self.h.ap()
            self.b = Buf(name)

    class PS:
        def __init__(self, name):
            self.h = es.enter_context(nc.psum_tensor(name, [128, 512], F32))
            self.a = self.h.ap()
            self.bf = self.a.bitcast(BF16)
            self.b = Buf(name)

    banks = [PS(f"ps{i}") for i in range(8)]
    bank_i = [0]
    nbank = [8]

    def bank():
        b = banks[bank_i[0] % nbank[0]]
        bank_i[0] += 1
        return b

    ARENA_W = 43000
    arena = es.enter_context(nc.sbuf_tensor("arena", [128, ARENA_W], F32)).ap()
    apos = [0]

    class A:
        def __init__(self, name, shape, dt=F32):
            n = 1
            for d_ in shape[1:]:
                n *= d_
            words = (n * (2 if dt == BF16 else 4) + 3) // 4
            words = (words + 15) // 16 * 16
            assert apos[0] + words <= ARENA_W, (name, apos[0], words)
            v = arena[:, apos[0]:apos[0] + words]
            apos[0] += words
            if dt == BF16:
                v = v.bitcast(BF16)
            v = v[:, 0:n]
            if len(shape) == 3:
                v = v.rearrange("p (a b) -> p a b", a=shape[1])
            elif len(shape) == 4:
                v = v.rearrange("p (a b c) -> p a b c", a=shape[1], b=shape[2])
            self.a = v[0:shape[0]]
            self.b = Buf(name)

    def arena_reset(mark):
        S.barrier()
        apos[0] = mark

    def tt(out, in0, in1, op, r, w, eng="vector"):
        S.op(eng, lambda e: e.tensor_tensor(out=out, in0=in0, in1=in1, op=op), r, w)

    def ts(out, in0, s1, s2, op0, op1, r, w, eng="vector"):
        st = not (isinstance(s1, (int, float)) and (s2 is None or isinstance(s2, (int, float))))
        if s2 is None:
            S.op(eng, lambda e: e.tensor_scalar(out=out, in0=in0, scalar1=s1, scalar2=None, op0=op0), r, w, strict=st)
        else:
            S.op(eng, lambda e: e.tensor_scalar(out=out, in0=in0, scalar1=s1, scalar2=s2, op0=op0, op1=op1), r, w, strict=st)

    def stt(out, in0, scalar, in1, op0, op1, r, w, eng="vector"):
        S.op(eng, lambda e: e.scalar_tensor_tensor(out=out, in0=in0, scalar=scalar, in1=in1, op0=op0, op1=op1), r, w,
             strict=not isinstance(scalar, (int, float)))

    def cp(out, in_, r, w, eng="vector"):
        if eng == "scalar":
            S.op(eng, lambda e: e.activation(out=out, in_=in_, func=AF.Copy), r, w)
        else:
            S.op(eng, lambda e: e.tensor_copy(out=out, in_=in_), r, w)

    def act(out, in_, func, r, w, bias=None, scale=None, accum=None):
        kw = {}
        if bias is not None:
            kw["bias"] = bias
        if scale is not None:
            kw["scale"] = scale
        if accum is not None:
            kw["accum_out"] = accum
        S.op("scalar", lambda e: e.activation(out=out, in_=in_, func=func, **kw), r, w)

    def rsqrt(ap, buf):
        act(ap, ap, AF.Sqrt, [buf], [buf])
        S.op("vector", lambda e: e.reciprocal(out=ap, in_=ap), [buf], [buf])

    def sumsq(src, nt):
        tt(junk.a[:nt], src.a[:nt], src.a[:nt], ALU.mult, [src.b], [junk.b])
        red(ssq.a[:nt], junk.a[:nt], [junk.b], [ssq.b])

    def red(out, in_, r, w, op=ALU.add):
        S.op("vector", lambda e: e.tensor_reduce(out=out, in_=in_, axis=AX.X, op=op), r, w)

    def mm(out, lhsT, rhs, start, stop, r, w):
        S.op("tensor", lambda e: e.matmul(out, lhsT, rhs, start=start, stop=stop), r, w)

    def tr(out, in_, ident, r, w):
        S.op("tensor", lambda e: e.transpose(out, in_, ident), r, w)

    dq = [0]

    def dma(out, in_, r, w, q=None):
        if q is None:
            q = "sync" if dq[0] % 2 == 0 else "gpsimd"
            dq[0] += 1
        S.op(q, lambda e: e.dma_start(out=out, in_=in_), r, w, dma=True)

    def memset(ap, val, w):
        S.op("vector", lambda e: e.memset(ap, val), [], w)

    ident_f = T("ident_f", [128, 128]); ident = T("ident_b", [128, 128], BF16)
    dma(ident_f.a, ident_in, [], [ident_f.b])
    cp(ident.a, ident_f.a, [ident_f.b], [ident.b])
    gc = T("gc", [128, 24])
    dma(gc.a, gcols, [], [gc.b])
    rc = T("rc", [128, 2, 4])
    dma(rc.a, pool_rc.rearrange("a p g -> p a g"), [], [rc.b])

    def bc_row(name, off, n, cls=T):
        t = cls(name, [128, n])
        dma(t.a, rowv[0:1, off:off + n].broadcast_to([128, n]), [], [t.b])
        return t

    lg_bc = bc_row("lg_bc", 4352, 512); lb_bc = bc_row("lb_bc", 4864, 512); psc_bc = bc_row("psc_bc", 5376, 512)
    fg_bc = bc_row("fg_bc", 5888, 1024)

    xt = T("xt", [128, D]); junk = T("junk", [128, D]); xnb = T("xnb", [128, D], BF16)
    ssq = T("ssq", [128, 1]); rstd = T("rstd", [128, 1])
    t1 = T("t1", [128, 512]); t2 = T("t2", [128, 512])
    h8 = T("h8", [128, 8]); rn8 = T("rn8", [128, 8]); bn8 = T("bn8", [128, 8])
    win = A("win", [128, 8, DIN], BF16)
    stg = [A("stg0", [128, 4096]), A("stg1", [128, 4096])]
    stb = [A("stb0", [128, 4096], BF16), A("stb1", [128, 4096], BF16)]
    mark1 = apos[0]
    zero = A("zero", [128, DSH])
    memset(zero.a, 0.0, [zero.b])
    sti = [0]

    def stage():
        i = sti[0] % 2
        sti[0] += 1
        return stg[i], stb[i]

    decb = T("decb", [64, 512], BF16); abb = T("abb", [128, 512], BF16); gbb = T("gbb", [128, 512], BF16)
    plw = T("plw", [128, 2, D], BF16); poolw = T("poolw", [128, 4, 128], BF16); keysT = T("keysT", [128, 2, 128], BF16)
    sf, sb = stage()
    dma(sf.a[0:64, 0:512], decay_b, [], [sf.b])
    cp(decb.a, sf.a[0:64, 0:512], [sf.b], [decb.b])
    sf, sb = stage()
    dma(sf.a[64:128, 0:512], a_b, [], [sf.b])
    cp(abb.a[64:128, :], sf.a[64:128, 0:512], [sf.b], [abb.b])
    sf, sb = stage()
    dma(sf.a[:, 0:512], g_b, [], [sf.b])
    cp(gbb.a, sf.a[:, 0:512], [sf.b], [gbb.b])
    sf, sb = stage()
    dma(sf.a[:, 0:2048].rearrange("p (k n) -> p k n", k=2), ple_w.rearrange("(k p) n -> p k n", p=128), [], [sf.b])
    cp(plw.a, sf.a[:, 0:2048].rearrange("p (k n) -> p k n", k=2), [sf.b], [plw.b])
    sf, sb = stage()
    dma(sf.a[:, 0:512].rearrange("p (g d) -> p g d", g=4), pool_w.rearrange("g c d -> c g d"), [], [sf.b])
    cp(poolw.a, sf.a[:, 0:512].rearrange("p (g d) -> p g d", g=4), [sf.b], [poolw.b])
    sf, sb = stage()
    dma(sf.a[:, 0:256].rearrange("p (s c) -> p s c", s=2), keys.rearrange("s n c -> n s c"), [], [sf.b])
    cp(sb.a[:, 0:256], sf.a[:, 0:256], [sf.b], [sb.b])
    pb = bank()
    for s_ in range(2):
        tr(pb.bf[:, s_ * 128:(s_ + 1) * 128], sb.a[:, s_ * 128:(s_ + 1) * 128], ident.a, [sb.b, ident.b], [pb.b])
    cp(keysT.a.rearrange("p s n -> p (s n)"), pb.bf[:, 0:256], [pb.b], [keysT.b])

    for kd in range(8):
        sf, sb = stage()
        dma(sf.a[:, 0:DIN], w_in[kd * 128:(kd + 1) * 128, :], [], [sf.b])
        ts(win.a[:, kd, :], sf.a[:, 0:DIN], gc.a[:, kd:kd + 1], None, ALU.mult, None, [sf.b, gc.b], [win.b])
    wscr_b = {"WO": Buf("WOs"), "WQ": Buf("WQs"), "WG": Buf("WGs")}
    for (src, dst, n, goff, key) in ((w_out, WOs, D, None, "WO"), (wq, WQs, 2048, 8, "WQ"), (ple_gw, WGs, D, 16, "WG")):
        for kd in range(8):
            for c0 in range(0, n, 1024):
                sf, sb = stage()
                dma(sf.a[:, 0:1024], src[kd * 128:(kd + 1) * 128, c0:c0 + 1024], [], [sf.b])
                if goff is None:
                    cp(sb.a[:, 0:1024], sf.a[:, 0:1024], [sf.b], [sb.b])
                else:
                    ts(sb.a[:, 0:1024], sf.a[:, 0:1024], gc.a[:, goff + kd:goff + kd + 1], None, ALU.mult, None,
                       [sf.b, gc.b], [sb.b])
                dma(dst[:, kd, c0:c0 + 1024], sb.a[:, 0:1024], [sb.b], [wscr_b[key]])

    ut_b = Buf("UT"); vb_b = Buf("VB")
    def conv_batch(bt):
        e0 = bt * 512
        sf, sb = stage()
        dma(sf.a.rearrange("p (a d) -> p a d", a=4), peer_u[e0:e0 + 512, :].rearrange("(a p) d -> p a d", p=128), [], [sf.b])
        cp(sb.a, sf.a, [sf.b], [sb.b])
        sf2, utsb = stage()
        for a in range(4):
            pb = bank()
            for kd in range(8):
                tr(pb.bf[:, kd * 128:(kd + 1) * 128], sb.a[:, a * 1024 + kd * 128:a * 1024 + (kd + 1) * 128], ident.a,
                   [sb.b, ident.b], [pb.b])
            tt(utsb.a.rearrange("p (k e) -> p k e", k=8)[:, :, a * 128:(a + 1) * 128],
               pb.bf.rearrange("p (k e) -> p k e", k=8),
               gc.a[:, 8:16].unsqueeze(2).broadcast_to([128, 8, 128]), ALU.mult, [pb.b, gc.b], [utsb.b])
        dma(UT[:, :, e0:e0 + 512].rearrange("k p e -> p k e"), utsb.a.rearrange("p (k e) -> p k e", k=8), [utsb.b], [ut_b])
        sf, sb = stage()
        dma(sf.a.rearrange("p (a d) -> p a d", a=4), peer_v[e0:e0 + 512, :].rearrange("(a p) d -> p a d", p=128), [], [sf.b])
        cp(sb.a, sf.a, [sf.b], [sb.b], eng="scalar")
        dma(VB[e0:e0 + 512, :].rearrange("(a p) d -> p a d", p=128), sb.a.rearrange("p (a d) -> p a d", a=4), [sb.b], [vb_b])

    ntile = 17
    sc_b = [Buf(f"SC{i}") for i in range(ntile)]
    z_b = [Buf(f"Z{i}") for i in range(ntile)]
    e_b = [Buf(f"E{i}") for i in range(ntile)]
    zinit = Buf("zinit")
    dma(Zp[0:1, :], zero.a[0:1, :], [zero.b], [zinit])
    dma(Ep[0:15, :], zero.a[0:15, 0:512], [zero.b], [zinit])
    arena_reset(mark1)

    mu_bc = bc_row("mu_bc", 0, DSH, A)
    w0_bc = bc_row("w0_bc", 1792, 512, A); a0_bc = bc_row("a0_bc", 2304, 512, A)
    kk_bc = bc_row("kk_bc", 2816, 512, A); ka_bc = bc_row("ka_bc", 3328, 512, A); rk_bc = bc_row("rk_bc", 3840, 512, A)
    xnT = A("xnT", [128, 8, 128], BF16)
    zsb = A("zsb", [128, DIN]); zpv = A("zpv", [128, DSH]); zs = A("zs", [128, DSH])
    Lb = A("Lb", [128, 256], BF16); LT = A("LT", [128, 2, 128], BF16)
    scst = A("scst", [128, 8, 8, 64]); aa = A("aa", [128, 512])

    def rms_T(src, nt, dstT):
        sumsq(src, nt)
        ts(rstd.a[:nt], ssq.a[:nt], 1.0 / D, 1e-6, ALU.mult, ALU.add, [ssq.b], [rstd.b])
        rsqrt(rstd.a[:nt], rstd.b)
        ts(xnb.a[:nt], src.a[:nt], rstd.a[:nt, 0:1], None, ALU.mult, None, [src.b, rstd.b], [xnb.b])
        pb = bank()
        for kd in range(8):
            tr(pb.bf[:, kd * 128:kd * 128 + nt], xnb.a[:nt, kd * 128:(kd + 1) * 128], ident.a[:nt, :nt], [xnb.b, ident.b], [pb.b])
        cp(dstT.a[:, :, :nt], pb.bf.rearrange("p (k t) -> p k t", k=8)[:, :, :nt], [pb.b], [dstT.b], eng="scalar")

    def v3(ap):
        return ap.rearrange("p (h n) -> p h n", h=8)

    def phase1(ti):
        samp = ti == 16
        nt = NS if samp else 128
        t0 = ti * 128
        dma(xt.a[:nt], x_s if samp else x_p[t0:t0 + nt, :], [], [xt.b])
        rms_T(xt, nt, xnT)
        for c0 in range(0, DIN, 512):
            n = min(512, DIN - c0)
            pb = bank()
            for kd in range(8):
                mm(pb.a[:nt, :n], xnT.a[:, kd, :nt], win.a[:, kd, c0:c0 + n], kd == 0, kd == 7, [xnT.b, win.b], [pb.b])
            cp(zsb.a[:nt, c0:c0 + n], pb.a[:nt, :n], [pb.b], [zsb.b], eng="scalar" if (c0 // 512) % 2 else "vector")
        if not samp:
            dma(Zp[1 + t0:1 + t0 + nt, :], zsb.a[:nt, 0:DSH], [zsb.b], [z_b[ti]])
            dma(Ep[15 + t0:15 + t0 + nt, :], zsb.a[:nt, DSH:DIN], [zsb.b], [e_b[ti]])
            rd = [z_b[ti], zinit] + ([z_b[ti - 1]] if ti > 0 else [])
            dma(zpv.a[:nt], Zp[t0:t0 + nt, :], rd, [zpv.b])
        else:
            dma(Zs, zsb.a[:nt, 0:DSH], [zsb.b], [z_b[ti]])
            dma(Us, zsb.a[:nt, DSH:DIN], [zsb.b], [e_b[ti]])
            zq = Buf("zsp")
            dma(Zsp.rearrange("(s t) f -> s t f", t=4)[:, 0, :], st_shift, [], [zq], q="sync")
            dma(Zsp.rearrange("(s t) f -> s t f", t=4)[:, 1:4, :], Zs.rearrange("(s t) f -> s t f", t=4)[:, 0:3, :],
                [z_b[ti]], [zq], q="sync")
            dma(zpv.a[:nt], Zsp, [zq], [zpv.b], q="sync")
            eq = Buf("es")
            dma(Es[:, 0:15, :], st_pool, [], [eq], q="sync")
            dma(Es[:, 15:19, :], Us.rearrange("(s t) c -> s t c", t=4), [e_b[ti]], [eq], q="sync")
            eq2 = Buf("esf")
            for k in range(16):
                dma(Esf[k].rearrange("(s t) c -> s t c", t=4), Es[:, 15 - k:19 - k, :], [eq], [eq2], q="sync")
            e_b[ti] = eq2
            dma(o_shs, Zs.rearrange("(s t) f -> s t f", t=4)[:, 3, :], [z_b[ti]], [Buf("o")], q="sync")
            dma(o_pos, Es[:, 4:19, :], [eq], [Buf("o")], q="sync")
        tt(zs.a[:nt], zpv.a[:nt], zsb.a[:nt, 0:DSH], ALU.subtract, [zpv.b, zsb.b], [zs.b])
        tt(zs.a[:nt], zs.a[:nt], mu_bc.a[:nt], ALU.mult, [zs.b, mu_bc.b], [zs.b])
        tt(zs.a[:nt], zs.a[:nt], zsb.a[:nt, 0:DSH], ALU.add, [zs.b, zsb.b], [zs.b])
        r_ = zs.a[:nt, 0:512]; k_ = zs.a[:nt, 512:1024]; v_ = zs.a[:nt, 1024:1536]
        sq = lambda q: scst.a[:nt, :, q, :]
        act(Lb.a[:nt, 0:64], zs.a[:nt, 1536:1600], AF.Tanh, [zs.b], [Lb.b])
        cp(Lb.a[:nt, 64:128], zs.a[:nt, 1600:1664], [zs.b], [Lb.b])
        act(Lb.a[:nt, 128:256], zs.a[:nt, 1664:1792], AF.Sigmoid, [zs.b], [Lb.b])
        pb = bank()
        for j in range(2):
            tr(pb.bf[:, j * 128:j * 128 + nt], Lb.a[:nt, j * 128:(j + 1) * 128], ident.a[:nt, :nt], [Lb.b, ident.b], [pb.b])
        cp(LT.a[:, :, :nt], pb.bf[:, 0:256].rearrange("p (k t) -> p k t", k=2)[:, :, :nt], [pb.b], [LT.b])
        pd = bank(); pa = bank(); pg = bank()
        mm(pd.a[:nt], LT.a[0:64, 0, :nt], decb.a[0:64, :], True, True, [LT.b, decb.b], [pd.b])
        mm(pa.a[:nt], LT.a[64:128, 0, :nt], abb.a[64:128, :], True, True, [LT.b, abb.b], [pa.b])
        mm(pg.a[:nt], LT.a[:, 1, :nt], gbb.a, True, True, [LT.b, gbb.b], [pg.b])
        tt(t1.a[:nt], pd.a[:nt], w0_bc.a[:nt], ALU.add, [pd.b, w0_bc.b], [t1.b])
        act(t1.a[:nt], t1.a[:nt], AF.Sigmoid, [t1.b], [t1.b])
        act(sq(1), v3(t1.a[:nt]), AF.Exp, [t1.b], [scst.b], scale=-0.6065306597126334)
        ts(sq(7), v3(t1.a[:nt]), -0.6065306597126334, None, ALU.mult, None, [t1.b], [scst.b])
        tt(aa.a[:nt], pa.a[:nt], a0_bc.a[:nt], ALU.add, [pa.b, a0_bc.b], [aa.b])
        act(aa.a[:nt], aa.a[:nt], AF.Sigmoid, [aa.b], [aa.b])
        cp(sq(6), v3(pg.a[:nt]), [pg.b], [scst.b], eng="scalar")
        cp(sq(0), v3(r_), [zs.b], [scst.b])
        cp(sq(5), v3(v_), [zs.b], [scst.b], eng="scalar")
        tt(t1.a[:nt], k_, kk_bc.a[:nt], ALU.mult, [zs.b, kk_bc.b], [t1.b])
        tt(t2.a[:nt], t1.a[:nt], t1.a[:nt], ALU.mult, [t1.b], [t2.b])
        red(h8.a[:nt], v3(t2.a[:nt]), [t2.b], [h8.b])
        ts(rn8.a[:nt], h8.a[:nt], 1e-24, None, ALU.add, None, [h8.b], [rn8.b])
        rsqrt(rn8.a[:nt], rn8.b)
        ts(rn8.a[:nt], rn8.a[:nt], -1.0, None, ALU.mult, None, [rn8.b], [rn8.b])
        tt(sq(3), v3(t1.a[:nt]), rn8.a[:nt].unsqueeze(2).broadcast_to([nt, 8, 64]), ALU.mult,
           [t1.b, rn8.b], [scst.b])
        stt(sq(4), sq(3), -1.0, v3(aa.a[:nt]), ALU.mult, ALU.mult, [scst.b, aa.b], [scst.b])
        stt(t1.a[:nt], aa.a[:nt], -1.0, ka_bc.a[:nt], ALU.add, ALU.mult, [aa.b, ka_bc.b], [t1.b])
        stt(sq(2), v3(t1.a[:nt]), 1.0, v3(k_), ALU.add, ALU.mult, [t1.b, zs.b], [scst.b])
        tt(v3(t2.a[:nt]), v3(r_), sq(2), ALU.mult, [zs.b, scst.b], [t2.b])
        tt(t2.a[:nt], t2.a[:nt], rk_bc.a[:nt], ALU.mult, [t2.b, rk_bc.b], [t2.b])
        red(bn8.a[:nt], v3(t2.a[:nt]), [t2.b], [bn8.b])
        tk0 = SEQ if samp else t0
        dma(SC.rearrange("t h q j -> t (h q j)")[tk0:tk0 + nt], scst.a[:nt].rearrange("p h q j -> p (h q j)"), [scst.b], [sc_b[ti]])
        dma(BN[tk0:tk0 + nt, :], bn8.a[:nt], [bn8.b], [sc_b[ti]])

    cb = 0
    for ti in range(ntile):
        phase1(ti)
        for _ in range(2):
            if cb < NE // 512:
                conv_batch(cb)
                cb += 1
    while cb < NE // 512:
        conv_batch(cb)
        cb += 1
    dma(o_shp, Zp[SEQ:SEQ + 1, :], [z_b[15]], [Buf("o")], q="sync")
    dma(o_pop, Ep[SEQ:SEQ + 15, :], [e_b[15]], [Buf("o")], q="sync")

    os_b = [Buf(f"OS{i}") for i in range(ntile)]
    arena_reset(0)
    cm = A("cm", [128, 4, 128])
    dma(cm.a, cmask, [], [cm.b])
    mk2 = A("mk2", [128, 4, 128])
    for i_ in range(4):
        cp(mk2.a[:, i_, :], cm.a[:, i_ % 2, :], [cm.b], [mk2.b])
    sct = [A("sct0", [128, 8, 8, 64]), A("sct1", [128, 8, 8, 64])]
    lwc = A("lwc", [128, 512]); cum = A("cum", [128, 512]); tmpc = A("tmpc", [128, 512])
    Epl = A("Epl", [128, 512]); Emi = A("Emi", [128, 512]); Eprv = A("Eprv", [128, 512]); Ehat = A("Ehat", [128, 512])
    gamT = A("gamT", [128, 4])
    At = A("At", [128, 512]); Rt = A("Rt", [128, 512]); Bt = A("Bt", [128, 512]); Kt = A("Kt", [128, 512])
    Bh = A("Bh", [128, 512]); Kh = A("Kh", [128, 512]); Vc = A("Vc", [128, 512])
    ART = A("ART", [128, 4, 2, 128]); BT = A("BT", [128, 4, 128]); KT = A("KT", [128, 4, 128])
    ARTz = [A("ARTz0", [128, 4, 2, 128]), A("ARTz1", [128, 4, 2, 128])]
    BTz = [A("BTz0", [128, 4, 128]), A("BTz1", [128, 4, 128])]
    Hz = [A("Hz0", [128, 4, 64]), A("Hz1", [128, 4, 64])]
    for z_ in ARTz + BTz + Hz:
        memset(z_.a, 0.0, [z_.b])
    G = A("G", [128, 8, 512])
    Yb = [A("Yb0", [128, 8, 128], BF16), A("Yb1", [128, 8, 128], BF16)]
    Lb2 = [A("Lb0", [128, 8, 128]), A("Lb1", [128, 8, 128], BF16), A("Lb2", [128, 8, 128], BF16)]
    TTm = A("TTm", [128, 8, 128]); TLm = A("TLm", [128, 8, 128])
    TTb = A("TTb", [128, 8, 128], BF16); TLb = A("TLb", [128, 8, 128], BF16)
    Wsb = A("Wsb", [128, 512]); Usb = A("Usb", [128, 512]); Osb = [A("Osb0", [128, 512]), A("Osb1", [128, 512])]
    Hst = A("Hst", [128, 4, 64]); Hs2 = A("Hs2", [128, 4, 64])
    ones_col = cm.a[:, 3, 0:1]
    mcol = [cm.a[:, 0, 64:65], cm.a[:, 2, 63:64]]
    evi = [0]

    def evac(out, in_, r, w):
        evi[0] += 1
        cp(out, in_, r, w, eng="scalar" if evi[0] % 2 else "vector")

    import os
    LVL = int(os.environ.get("KLVL", "9"))
    for ci in range(SEQ // 128):
        t0 = ci * 128
        sc_ = sct[ci % 2]
        dma(sc_.a.rearrange("p h q j -> p (h q j)"), SC.rearrange("t h q j -> t (h q j)")[t0:t0 + 128], [sc_b[ci]], [sc_.b])
        X = lambda q: sc_.a[:, :, q, :]
        cp(v3(lwc.a), X(7), [sc_.b], [lwc.b])
        cp(v3(Vc.a), X(5), [sc_.b], [Vc.b], eng="scalar")
        pc = bank(); pt = bank(); pg = bank()
        mm(pc.a, cm.a[:, 1, :], lwc.a, True, True, [cm.b, lwc.b], [pc.b])
        mm(pt.a, cm.a[:, 3, :], lwc.a, True, True, [cm.b, lwc.b], [pt.b])
        for hp in range(4):
            mm(pg.a[:, hp:hp + 1], lwc.a[:, hp * 128:(hp + 1) * 128], ones_col, True, True, [lwc.b, cm.b], [pg.b])
        cp(cum.a, pc.a, [pc.b], [cum.b])
        act(Epl.a, cum.a, AF.Exp, [cum.b], [Epl.b])
        act(Emi.a, cum.a, AF.Exp, [cum.b], [Emi.b], scale=-1.0)
        tt(tmpc.a, cum.a, lwc.a, ALU.subtract, [cum.b, lwc.b], [tmpc.b])
        act(Eprv.a, tmpc.a, AF.Exp, [tmpc.b], [Eprv.b])
        tt(tmpc.a, pt.a, cum.a, ALU.subtract, [pt.b, cum.b], [tmpc.b])
        act(Ehat.a, tmpc.a, AF.Exp, [tmpc.b], [Ehat.b])
        act(gamT.a, pg.a[:, 0:4], AF.Exp, [pg.b], [gamT.b])
        tt(v3(At.a), X(3), v3(Eprv.a), ALU.mult, [sc_.b, Eprv.b], [At.b])
        tt(v3(Bt.a), X(4), v3(Emi.a), ALU.mult, [sc_.b, Emi.b], [Bt.b])
        tt(v3(Kt.a), X(2), v3(Emi.a), ALU.mult, [sc_.b, Emi.b], [Kt.b])
        tt(v3(Rt.a), X(0), v3(Epl.a), ALU.mult, [sc_.b, Epl.b], [Rt.b])
        tt(v3(Bh.a), X(4), v3(Ehat.a), ALU.mult, [sc_.b, Ehat.b], [Bh.b])
        tt(v3(Kh.a), X(2), v3(Ehat.a), ALU.mult, [sc_.b, Ehat.b], [Kh.b])
        for (src_, dst_) in ((At, ART.a[:, :, 0, :]), (Rt, ART.a[:, :, 1, :]), (Bt, BT.a), (Kt, KT.a)):
            pb = bank()
            for hp in range(4):
                tr(pb.a[:, hp * 128:(hp + 1) * 128], src_.a[:, hp * 128:(hp + 1) * 128], ident_f.a, [src_.b, ident_f.b], [pb.b])
            dbuf = ART.b if src_ in (At, Rt) else (BT.b if src_ is Bt else KT.b)
            p3_ = pb.a.rearrange("p (a t) -> p a t", a=4)
            evac(dst_, p3_, [pb.b], [dbuf])
            if src_ is not Kt:
                for hh in range(2):
                    if src_ is Bt:
                        zt, zd = BTz[hh], BTz[hh].a
                    else:
                        zt, zd = ARTz[hh], ARTz[hh].a[:, :, 0 if src_ is At else 1, :]
                    ts(zd, dst_, mcol[hh], None, ALU.mult, None, [dbuf, cm.b], [zt.b])
        if LVL < 2:
            continue
        for h in range(8):
            hp, hh = divmod(h, 2)
            ps_ = slice(hh * 64, (hh + 1) * 64)
            pb = bank()
            arr = ARTz[hh].a[:, hp, :, :].rearrange("p a t -> p (a t)")
            mm(pb.a[:, 0:256], BT.a[:, hp, :], arr, True, True, [BT.b, ARTz[hh].b], [pb.b])
            mm(pb.a[:, 256:512], KT.a[:, hp, :], arr, True, True, [KT.b, ARTz[hh].b], [pb.b])
            tt(G.a[:, h, :], pb.a, mk2.a.rearrange("p a t -> p (a t)"), ALU.mult, [pb.b, mk2.b], [G.b])
        for g4 in range(2):
            pb = bank()
            for hq in range(4):
                h = g4 * 4 + hq
                hp, hh = divmod(h, 2)
                ps_ = slice(hh * 64, (hh + 1) * 64)
                mm(pb.a[:, hq * 128:(hq + 1) * 128], ART.a[:, hp, 0, :], BTz[hh].a[:, hp, :], True, True, [ART.b, BTz[hh].b], [pb.b])
            tt(Lb2[0].a[:, g4 * 4:(g4 + 1) * 4, :], pb.a.rearrange("p (a t) -> p a t", a=4),
               cm.a[:, 2:3, :].broadcast_to([128, 4, 128]), ALU.mult, [pb.b, cm.b], [Lb2[0].b])
        if LVL < 3:
            continue
        idb = ident_f.a.unsqueeze(1).broadcast_to([128, 8, 128])
        tt(TTm.a, G.a[:, :, 0:128], idb, ALU.add, [G.b, ident_f.b], [TTm.b])
        tt(TLm.a, Lb2[0].a, idb, ALU.add, [Lb2[0].b, ident_f.b], [TLm.b], eng="gpsimd")
        cp(Yb[0].a, G.a[:, :, 0:128], [G.b], [Yb[0].b], eng="scalar")
        cp(Lb2[1].a, Lb2[0].a, [Lb2[0].b], [Lb2[1].b], eng="scalar")
        cp(TTb.a, TTm.a, [TTm.b], [TTb.b])
        cp(TLb.a, TLm.a, [TLm.b], [TLb.b], eng="scalar")
        Yp_t = Yb[0]
        Lp = Lb2[1]
        for k in range(1, 7):
            Ln_ = Lb2[1 + (k % 2)]; Yn = Yb[k % 2]
            for g4 in range(2):
                py = bank(); pl_ = bank()
                for hq in range(4):
                    h = g4 * 4 + hq
                    mm(py.a[:, hq * 128:(hq + 1) * 128], Lp.a[:, h, :], Yp_t.a[:, h, :], True, True, [Lp.b, Yp_t.b], [py.b])
                for hq in range(4):
                    h = g4 * 4 + hq
                    mm(pl_.a[:, hq * 128:(hq + 1) * 128], Yp_t.a[:, h, :], Lp.a[:, h, :], True, True, [Lp.b, Yp_t.b], [pl_.b])
                cp(Yn.a[:, g4 * 4:(g4 + 1) * 4, :], py.a.rearrange("p (a t) -> p a t", a=4), [py.b], [Yn.b])
                cp(Ln_.a[:, g4 * 4:(g4 + 1) * 4, :], pl_.a.rearrange("p (a t) -> p a t", a=4), [pl_.b], [Ln_.b], eng="scalar")
            upd = []
            for g4 in range(2):
                p1 = bank()
                for hq in range(4):
                    h = g4 * 4 + hq
                    mm(p1.a[:, hq * 128:(hq + 1) * 128], TLb.a[:, h, :], Yn.a[:, h, :], True, True, [TLb.b, Yn.b], [p1.b])
                p2 = None
                if k < 6:
                    p2 = bank()
                    for hq in range(4):
                        h = g4 * 4 + hq
                        mm(p2.a[:, hq * 128:(hq + 1) * 128], TTb.a[:, h, :], Ln_.a[:, h, :], True, True, [TTb.b, Ln_.b], [p2.b])
                upd.append((g4, p1, p2))
            for (g4, p1, p2) in upd:
                sl_ = slice(g4 * 4, (g4 + 1) * 4)
                tt(TTm.a[:, sl_, :], TTm.a[:, sl_, :], p1.a.rearrange("p (a t) -> p a t", a=4), ALU.add, [TTm.b, p1.b], [TTm.b])
                if p2 is not None:
                    tt(TLm.a[:, sl_, :], TLm.a[:, sl_, :], p2.a.rearrange("p (a t) -> p a t", a=4), ALU.add, [TLm.b, p2.b], [TLm.b])
            if k < 6:
                cp(TTb.a, TTm.a, [TTm.b], [TTb.b], eng="scalar")
                cp(TLb.a, TLm.a, [TLm.b], [TLb.b], eng="scalar")
            Yp_t = Yn
            Lp = Ln_
        first = ci == 0
        hsl = lambda h: slice(h * 64, (h + 1) * 64)
        pw = bank()
        for h in range(8):
            hp, hh = divmod(h, 2)
            ps_ = slice(hh * 64, (hh + 1) * 64)
            if not first:
                mm(pw.a[:, hsl(h)], ART.a[:, hp, 0, :], Hz[hh].a[:, hp, :], True, False, [ART.b, Hz[hh].b], [pw.b])
            mm(pw.a[:, hsl(h)], G.a[:, h, 256:384], Vc.a[:, hsl(h)], first, True, [G.b, Vc.b], [pw.b])
        cp(Wsb.a, pw.a, [pw.b], [Wsb.b])
        pu = bank()
        for h in range(8):
            mm(pu.a[:, hsl(h)], TTm.a[:, h, :], Wsb.a[:, hsl(h)], True, True, [TTm.b, Wsb.b], [pu.b])
        cp(Usb.a, pu.a, [pu.b], [Usb.b], eng="scalar")
        po = bank()
        for h in range(8):
            hp, hh = divmod(h, 2)
            ps_ = slice(hh * 64, (hh + 1) * 64)
            if not first:
                mm(po.a[:, hsl(h)], ART.a[:, hp, 1, :], Hz[hh].a[:, hp, :], True, False, [ART.b, Hz[hh].b], [po.b])
            mm(po.a[:, hsl(h)], G.a[:, h, 128:256], Usb.a[:, hsl(h)], first, False, [G.b, Usb.b], [po.b])
            mm(po.a[:, hsl(h)], G.a[:, h, 384:512], Vc.a[:, hsl(h)], False, True, [G.b, Vc.b], [po.b])
        ob_ = Osb[ci % 2]
        cp(ob_.a, po.a, [po.b], [ob_.b])
        dma(OS[t0:t0 + 128, :], ob_.a, [ob_.b], [os_b[ci]])
        ph = bank()
        for h in range(8):
            hp, hh = divmod(h, 2)
            ps_ = slice(hh * 64, (hh + 1) * 64)
            mm(ph.a[:, h * 64:(h + 1) * 64], Bh.a[:, hp * 128:(hp + 1) * 128], Usb.a[:, hsl(h)], True, False, [Bh.b, Usb.b], [ph.b])
            mm(ph.a[:, h * 64:(h + 1) * 64], Kh.a[:, hp * 128:(hp + 1) * 128], Vc.a[:, hsl(h)], False, True, [Kh.b, Vc.b], [ph.b])
        ph4 = ph.a.rearrange("p (a b i) -> p a b i", a=4, b=2)
        for hh in range(2):
            if first:
                ts(Hz[hh].a, ph4[:, :, hh, :], mcol[hh], None, ALU.mult, None, [ph.b, cm.b], [Hz[hh].b])
            else:
                tt(Hs2.a, Hz[hh].a, gamT.a.unsqueeze(2).broadcast_to([128, 4, 64]), ALU.mult, [Hz[hh].b, gamT.b], [Hs2.b])
                tt(Hs2.a, Hs2.a, ph4[:, :, hh, :], ALU.add, [Hs2.b, ph.b], [Hs2.b])
                ts(Hz[hh].a, Hs2.a, mcol[hh], None, ALU.mult, None, [Hs2.b, cm.b], [Hz[hh].b])
    Sfin = A("Sfin", [64, 512])
    tt(Hst.a, Hz[0].a, Hz[1].a, ALU.add, [Hz[0].b, Hz[1].b], [Hst.b])
    pb = bank()
    for hp in range(4):
        tr(pb.a[0:64, hp * 128:(hp + 1) * 128], Hst.a[:, hp, :], ident_f.a, [Hst.b, ident_f.b], [pb.b])
    cp(Sfin.a, pb.a[0:64, :], [pb.b], [Sfin.b])
    dma(bass.AP(o_wkp.tensor, 0, [[64, 64], [4096, 8], [1, 64]]), Sfin.a.rearrange("p (h j) -> p h j", h=8), [Sfin.b], [Buf("o")])

    arena_reset(0)
    Sst = A("Sst", [128, 4096]); stmp = A("stmp", [128, 4096])
    vecs = [A("vec0", [128, 4, 5, 64])]
    vk = A("vk", [128, 64]); skk = A("skk", [128, 64])

    def scan_step(Sv, nI, kkn, w, kka, vkt, r, oslot, rb, tmpv):
        bcv = lambda a: a.unsqueeze(1).broadcast_to([128, nI, 64])
        tt(tmpv, Sv, bcv(kkn), ALU.mult, [Sst.b] + rb, [stmp.b])
        red(skk.a[:, :nI], tmpv, [stmp.b], [skk.b])
        tt(Sv, Sv, bcv(w), ALU.mult, [Sst.b] + rb, [Sst.b])
        tt(tmpv, skk.a[:, :nI].unsqueeze(2).broadcast_to([128, nI, 64]), bcv(kka), ALU.mult, [skk.b] + rb, [stmp.b])
        tt(Sv, Sv, tmpv, ALU.add, [Sst.b, stmp.b], [Sst.b])
        tt(Sv, Sv, vkt, ALU.add, [Sst.b, vk.b], [Sst.b])
        tt(tmpv, Sv, bcv(r), ALU.mult, [Sst.b] + rb, [stmp.b])
        return tmpv

    vec = vecs[0]; vt = A("vts", [128, 4, 64]); oo_s = A("oo_s", [128, 4, 64])
    dma(Sst.a, st_wkv, [], [Sst.b])
    for s_ in range(16):
        src = bass.AP(SC.tensor, (SEQ + s_ * 4) * 4096, [[512, 8], [4096, 4], [1, 320]])
        dma(vec.a[s_ * 8:(s_ + 1) * 8, 0:4].rearrange("p t q j -> p t (q j)"), src, [sc_b[16]], [vec.b])
        srcv = bass.AP(SC.tensor, (SEQ + s_ * 4) * 4096 + 320, [[512, 8], [4096, 4], [1, 64]])
        dma(vt.a[s_ * 8:(s_ + 1) * 8], srcv, [sc_b[16]], [vt.b])
    Ss = Sst.a.rearrange("p (i j) -> p i j", i=64)
    tmps = stmp.a.rearrange("p (i j) -> p i j", i=64)
    vks = A("vks", [128, 4096])
    vks3 = vks.a.rearrange("p (i j) -> p i j", i=64)
    for t in range(4):
        tt(vks3, vt.a[:, t, :].unsqueeze(2).broadcast_to([128, 64, 64]),
           vec.a[:, t, 2, :].unsqueeze(1).broadcast_to([128, 64, 64]), ALU.mult, [vt.b, vec.b], [vk.b])
        tmpv = scan_step(Ss, 64, vec.a[:, t, 3, :], vec.a[:, t, 1, :], vec.a[:, t, 4, :], vks3, vec.a[:, t, 0, :],
                         None, [vec.b], tmps)
        red(oo_s.a[:, t, :], tmpv, [stmp.b], [oo_s.b])
    dma(o_wks, Sst.a, [Sst.b], [Buf("o")])
    for s_ in range(16):
        dst = bass.AP(OS.tensor, (SEQ + s_ * 4) * 512, [[64, 8], [512, 4], [1, 64]])
        dma(dst, oo_s.a[s_ * 8:(s_ + 1) * 8], [oo_s.b], [os_b[16]])

    arena_reset(0)
    nbank[0] = 6
    wbuf = A("wbuf", [128, 8, 1024], BF16)
    Mb = [A("Mb0", [128, 8, 1024], BF16), A("Mb1", [128, 8, 1024], BF16)]
    actT = A("actT", [128, 16, 128], BF16)
    NLA = 2
    sig = [A(f"sig{i}", [128, 1024]) for i in range(NLA)]
    ebf = [A(f"ebf{i}", [128, 1024], BF16) for i in range(NLA)]
    utb = [A("utb0", [128, 8, 1024], BF16), A("utb1", [128, 8, 1024], BF16)]
    vbb = [A("vbb0", [128, 4, D], BF16), A("vbb1", [128, 4, D], BF16)]
    gl2 = [A(f"gl{i}", [128, 512], BF16) for i in range(2)]
    actb = [A(f"actb{i}", [128, 512], BF16) for i in range(2)]
    ht = A("ht", [128, D]); ob = A("ob", [128, 512]); vv = A("vv", [128, 512]); gg = A("gg", [128, 512]); ub = A("ub", [128, 512])
    shb = [A(f"shb{i}", [128, 512]) for i in range(3)]
    ysb = A("ysb", [128, D], BF16); yT = A("yT", [128, 8, 128], BF16); x2T = A("x2T", [128, 8, 128], BF16)
    qT = A("qT", [128, 16, 128], BF16); sall = A("sall", [128, 16, 128])
    v16 = A("v16", [128, 16, 16]); srep = A("srep", [128, 128]); cand = A("cand", [128, 256]); crep = A("crep", [128, 256])
    c16 = A("c16", [128, 8, 16]); thr = A("thr", [128, 8]); bia = A("bia", [128, 8]); zz = A("zz", [128, 8]); nm = A("nm", [128, 8])
    gel = A("gel", [128, 512], BF16); ptb = A("ptb", [128, 256]); ptbb = A("ptbb", [128, 256], BF16); pT = A("pT", [128, 2, 128], BF16)
    m8 = A("m8", [128, 8]); m4 = A("m4", [128, 8]); pl = A("pl", [128, D])
    psO = [PSfix for PSfix in banks[6:8]]

    def bank6():
        b = banks[bank_i[0] % 6]
        bank_i[0] += 1
        return b

    def loadw(scr, n, key):
        dma(wbuf.a[:, :, 0:n], scr, [wscr_b[key]], [wbuf.b], q="sync")

    def phase2(ti):
        samp = ti == 16
        nt = NS if samp else 128
        t0 = ti * 128
        tk0 = SEQ if samp else t0
        dma(ob.a[:nt], OS[tk0:tk0 + nt, :], [os_b[ti]], [ob.b])
        dma(v3(vv.a[:nt]), SC[tk0:tk0 + nt, :, 5, :], [sc_b[ti]], [vv.b])
        dma(v3(gg.a[:nt]), SC[tk0:tk0 + nt, :, 6, :], [sc_b[ti]], [gg.b])
        dma(bn8.a[:nt], BN[tk0:tk0 + nt, :], [sc_b[ti]], [bn8.b])
        red(h8.a[:nt], v3(ob.a[:nt]), [ob.b], [h8.b])
        ts(h8.a[:nt], h8.a[:nt], 1.0 / 64, None, ALU.mult, None, [h8.b], [h8.b])
        tt(v3(ob.a[:nt]), v3(ob.a[:nt]), h8.a[:nt].unsqueeze(2).broadcast_to([nt, 8, 64]), ALU.subtract, [ob.b, h8.b], [ob.b])
        tt(t2.a[:nt], ob.a[:nt], ob.a[:nt], ALU.mult, [ob.b], [t2.b])
        red(rn8.a[:nt], v3(t2.a[:nt]), [t2.b], [rn8.b])
        ts(rn8.a[:nt], rn8.a[:nt], 1.0 / 64, 64e-5, ALU.mult, ALU.add, [rn8.b], [rn8.b])
        rsqrt(rn8.a[:nt], rn8.b)
        tt(v3(ob.a[:nt]), v3(ob.a[:nt]), rn8.a[:nt].unsqueeze(2).broadcast_to([nt, 8, 64]), ALU.mult, [ob.b, rn8.b], [ob.b])
        tt(ob.a[:nt], ob.a[:nt], lg_bc.a[:nt], ALU.mult, [ob.b, lg_bc.b], [ob.b])
        tt(ob.a[:nt], ob.a[:nt], lb_bc.a[:nt], ALU.add, [ob.b, lb_bc.b], [ob.b])
        tt(v3(vv.a[:nt]), v3(vv.a[:nt]), bn8.a[:nt].unsqueeze(2).broadcast_to([nt, 8, 64]), ALU.mult, [vv.b, bn8.b], [vv.b])
        tt(ob.a[:nt], ob.a[:nt], vv.a[:nt], ALU.add, [ob.b, vv.b], [ob.b])
        tt(ysb.a[:nt, 0:512], ob.a[:nt], gg.a[:nt], ALU.mult, [ob.b, gg.b], [ysb.b])
        if samp:
            dma(ub.a[:nt], Esf[0], [e_b[ti]], [ub.b])
        else:
            dma(ub.a[:nt], Ep[15 + t0:15 + t0 + nt, :], [e_b[ti]], [ub.b])
        cp(t1.a[:nt], ub.a[:nt], [ub.b], [t1.b])
        rdE = [e_b[ti]] + ([e_b[ti - 1]] if (0 < ti < 16) else []) + [zinit]
        for k in range(1, 16):
            c0 = 0 if k < 2 else (128 if k < 4 else (256 if k < 8 else 384))
            sh = shb[k % 3]
            if samp:
                dma(sh.a[:nt, c0:512], Esf[k][:, c0:512], rdE, [sh.b])
            else:
                dma(sh.a[:nt, c0:512], Ep[15 + t0 - k:15 + t0 - k + nt, c0:512], rdE, [sh.b])
            tt(t1.a[:nt, c0:512], t1.a[:nt, c0:512], sh.a[:nt, c0:512], ALU.add, [t1.b, sh.b], [t1.b])
        rci = 0 if ti == 0 else 1
        tt(t1.a[:nt].rearrange("p (g c) -> p g c", g=4), t1.a[:nt].rearrange("p (g c) -> p g c", g=4),
           rc.a[:nt, rci, :].unsqueeze(2).broadcast_to([nt, 4, 128]), ALU.mult, [t1.b, rc.b], [t1.b])
        tt(ptbb_p(nt), t1.a[:nt], ub.a[:nt], ALU.subtract, [t1.b, ub.b], [gel.b])
        pb = bank6()
        for g in range(4):
            tr(pb.bf[:, g * 128:g * 128 + nt], gel.a[:nt, g * 128:(g + 1) * 128], ident.a[:nt, :nt], [gel.b, ident.b], [pb.b])
        cp(qT.a[:, 0:4, :nt], pb.bf[:, 0:512].rearrange("p (g t) -> p g t", g=4)[:, :, :nt], [pb.b], [qT.b])
        pb2 = bank6()
        for g in range(4):
            mm(pb2.a[:nt, g * 128:(g + 1) * 128], qT.a[:, g, :nt], poolw.a[:, g, :], True, True, [qT.b, poolw.b], [pb2.b])
        tt(ysb.a[:nt, 512:1024], pb2.a[:nt], psc_bc.a[:nt], ALU.mult, [pb2.b, psc_bc.b], [ysb.b])
        pb = bank6()
        for kd in range(8):
            tr(pb.bf[:, kd * 128:kd * 128 + nt], ysb.a[:nt, kd * 128:(kd + 1) * 128], ident.a[:nt, :nt], [ysb.b, ident.b], [pb.b])
        cp(yT.a[:, :, :nt], pb.bf.rearrange("p (k t) -> p k t", k=8)[:, :, :nt], [pb.b], [yT.b], eng="scalar")
        dma(xt.a[:nt], x_s if samp else x_p[t0:t0 + nt, :], [], [xt.b])
        loadw(WOs, D, "WO")
        for half in range(2):
            pb = bank6()
            for kd in range(8):
                mm(pb.a[:nt], yT.a[:, kd, :nt], wbuf.a[:, kd, half * 512:(half + 1) * 512], kd == 0, kd == 7, [yT.b, wbuf.b], [pb.b])
            tt(ht.a[:nt, half * 512:(half + 1) * 512], pb.a[:nt], xt.a[:nt, half * 512:(half + 1) * 512], ALU.add,
               [pb.b, xt.b], [ht.b])
        rms_T(ht, nt, x2T)
        for qb in range(4):
            if qb % 2 == 0:
                loadw(WQs[:, :, (qb // 2) * 1024:(qb // 2 + 1) * 1024], 1024, "WQ")
            pb = bank6()
            for j in range(4):
                c = qb * 4 + j
                cl = (qb % 2) * 4 + j
                for kd in range(8):
                    mm(pb.a[:, j * 128:j * 128 + nt], wbuf.a[:, kd, cl * 128:(cl + 1) * 128], x2T.a[:, kd, :nt], kd == 0, kd == 7,
                       [wbuf.b, x2T.b], [pb.b])
            cp(qT.a[:, qb * 4:(qb + 1) * 4, :nt], pb.a.rearrange("p (j t) -> p j t", j=4)[:, :, :nt], [pb.b], [qT.b],
               eng="scalar" if qb % 2 else "vector")
        for qb in range(4):
            pb = bank6()
            for j in range(4):
                c = qb * 4 + j
                mm(pb.a[:nt, j * 128:(j + 1) * 128], qT.a[:, c, :nt], keysT.a[:, c % 2, :], True, True, [qT.b, keysT.b], [pb.b])
            cp(sall.a[:nt, qb * 4:(qb + 1) * 4, :], pb.a[:nt].rearrange("p (j n) -> p j n", j=4), [pb.b], [sall.b],
               eng="scalar" if qb % 2 else "vector")
        for c in range(16):
            S.op("vector", lambda e, c=c: e.max(out=v16.a[:nt, c, 0:8], in_=sall.a[:nt, c, :]), [sall.b], [v16.b])
            S.op("vector", lambda e, c=c: e.match_replace(out=srep.a[:nt], in_to_replace=v16.a[:nt, c, 0:8],
                                                          in_values=sall.a[:nt, c, :], imm_value=-1e30), [sall.b, v16.b], [srep.b])
            S.op("vector", lambda e, c=c: e.max(out=v16.a[:nt, c, 8:16], in_=srep.a[:nt]), [srep.b], [v16.b])
        for h in range(8):
            tt(cand.a[:nt, :].rearrange("p (a b) -> p a b", a=16),
               v16.a[:nt, 2 * h, :].unsqueeze(2).broadcast_to([nt, 16, 16]),
               v16.a[:nt, 2 * h + 1, :].unsqueeze(1).broadcast_to([nt, 16, 16]), ALU.add, [v16.b], [cand.b])
            S.op("vector", lambda e, h=h: e.max(out=c16.a[:nt, h, 0:8], in_=cand.a[:nt, :]), [cand.b], [c16.b])
            S.op("vector", lambda e, h=h: e.match_replace(out=crep.a[:nt], in_to_replace=c16.a[:nt, h, 0:8],
                                                          in_values=cand.a[:nt, :], imm_value=-1e30), [cand.b, c16.b], [crep.b])
            S.op("vector", lambda e, h=h: e.max(out=c16.a[:nt, h, 8:16], in_=crep.a[:nt]), [crep.b], [c16.b])
        cp(thr.a[:nt], c16.a[:nt, :, 15], [c16.b], [thr.b])
        ts(nm.a[:nt], c16.a[:nt, :, 0], -1.0, None, ALU.mult, None, [c16.b], [nm.b])
        memset(zz.a, 0.0, [zz.b])
        for h in range(8):
            act(c16.a[:nt, h, :], c16.a[:nt, h, :], AF.Exp, [c16.b, nm.b], [c16.b, zz.b],
                bias=nm.a[:nt, h:h + 1], accum=zz.a[:nt, h:h + 1])
        act(zz.a[:nt], zz.a[:nt], AF.Ln, [zz.b], [zz.b])
        tt(bia.a[:nt], nm.a[:nt], zz.a[:nt], ALU.subtract, [nm.b, zz.b], [bia.b])
        def emit_sigma(i):
            q, h = divmod(i, 8)
            sb_, eb_ = sig[i % NLA], ebf[i % NLA]
            tt(sb_.a[:nt].rearrange("p (a b) -> p a b", a=8),
               sall.a[:nt, 2 * h, q * 8:q * 8 + 8].unsqueeze(2).broadcast_to([nt, 8, 128]),
               sall.a[:nt, 2 * h + 1, :].unsqueeze(1).broadcast_to([nt, 8, 128]), ALU.add, [sall.b], [sb_.b])
            act(eb_.a[:nt], sb_.a[:nt], AF.Exp, [sb_.b, bia.b], [eb_.b], bias=bia.a[:nt, h:h + 1])

        def emit_mask(i):
            q, h = divmod(i, 8)
            sb_, eb_ = sig[i % NLA], ebf[i % NLA]
            stt(Mb[q % 2].a[:nt, h, :], sb_.a[:nt], thr.a[:nt, h:h + 1], eb_.a[:nt], ALU.is_ge, ALU.mult,
                [sb_.b, thr.b, eb_.b], [Mb[q % 2].b])

        grp = {}

        def stage_A(q):
            ub_ = utb[q % 2]
            for hf in range(2):
                pG = bank(); pP = bank()
                grp[(q, hf)] = [pG, pP, None]
                for h in range(8):
                    mm(pG.a[:nt, :], ident.a[:nt, :nt], Mb[q % 2].a[:nt, h, hf * 512:(hf + 1) * 512], h == 0, h == 7,
                       [Mb[q % 2].b, ident.b], [pG.b])
                for kd in range(8):
                    mm(pP.a[:nt, :], x2T.a[:, kd, :nt], ub_.a[:, kd, hf * 512:(hf + 1) * 512], kd == 0, kd == 7,
                       [x2T.b, ub_.b], [pP.b])

        def stage_B(q, hf):
            pG, pP, _ = grp[(q, hf)]
            gl = gl2[hf]; ab = actb[hf]
            act(gl.a[:nt], pP.a[:nt], AF.Gelu, [pP.b], [gl.b])
            tt(ab.a[:nt], pG.a[:nt], gl.a[:nt], ALU.mult, [pG.b, gl.b], [ab.b])

        def stage_C(q, hf):
            pT = bank()
            grp[(q, hf)][2] = pT
            ab = actb[hf]
            for j in range(4):
                tr(pT.bf[:, j * 128:j * 128 + nt], ab.a[:nt, j * 128:(j + 1) * 128], ident.a[:nt, :nt], [ab.b, ident.b], [pT.b])

        def stage_D(q, hf):
            pT = grp[(q, hf)][2]
            sl = (q % 2) * 8 + hf * 4
            cp(actT.a[:, sl:sl + 4, :nt], pT.bf[:, 0:512].rearrange("p (j t) -> p j t", j=4)[:, :, :nt], [pT.b], [actT.b],
               eng="scalar" if hf else "vector")

        def stage_E(q):
            for c in range(8):
                cg = q * 8 + c
                vb_ = vbb[c // 4]
                for half in range(2):
                    mm(psO[half].a[:nt], actT.a[:, (q % 2) * 8 + c, :nt], vb_.a[:, c % 4, half * 512:(half + 1) * 512],
                       cg == 0, cg == 127, [actT.b, vb_.b], [psO[half].b])

        def emit_loads(q):
            dma(utb[q % 2].a, UT[:, :, q * 1024:(q + 1) * 1024].rearrange("k p e -> p k e"), [ut_b], [utb[q % 2].b])

        def emit_vloads(q):
            for hv in range(2):
                r0 = q * 1024 + hv * 512
                dma(vbb[hv].a, VB[r0:r0 + 512, :].rearrange("(a p) d -> p a d", p=128), [vb_b], [vbb[hv].b])

        def lazy(qp, h):
            if h == 1:
                pG0, pP0, _ = grp[(qp, 0)]; pG1, pP1, _ = grp[(qp, 1)]
                act(gl2[0].a[:nt], pP0.a[:nt], AF.Gelu, [pP0.b], [gl2[0].b])
                act(gl2[1].a[:nt], pP1.a[:nt], AF.Gelu, [pP1.b], [gl2[1].b])
            elif h == 2:
                for hf_ in range(2):
                    pG_, _, _ = grp[(qp, hf_)]
                    tt(actb[hf_].a[:nt], pG_.a[:nt], gl2[hf_].a[:nt], ALU.mult, [pG_.b, gl2[hf_].b], [actb[hf_].b])
            elif h == 3:
                stage_C(qp, 0); stage_C(qp, 1)
            elif h == 4:
                stage_D(qp, 0); stage_D(qp, 1)
            elif h == 5:
                emit_vloads(qp)
                stage_E(qp)

        emit_loads(0)
        for i0_ in range(NLA - 1):
            emit_sigma(i0_)
        for i in range(128):
            q, h = divmod(i, 8)
            if h == 0 and q + 1 < 16:
                emit_loads(q + 1)
            if i + NLA - 1 < 128:
                emit_sigma(i + NLA - 1)
            emit_mask(i)
            if q > 0:
                lazy(q - 1, h)
            if h == 7:
                stage_A(q)
        for h in range(1, 6):
            lazy(15, h)
        for half in range(2):
            tt(ht.a[:nt, half * 512:(half + 1) * 512], ht.a[:nt, half * 512:(half + 1) * 512], psO[half].a[:nt], ALU.add,
               [ht.b, psO[half].b], [ht.b])
        rms_T(ht, nt, x2T)
        loadw(WGs, D, "WG")
        dma(ptb.a[:nt], p_s if samp else p_p[t0:t0 + nt, :], [], [ptb.b])
        cp(ptbb.a[:nt], ptb.a[:nt], [ptb.b], [ptbb.b])
        pb = bank6()
        for j in range(2):
            tr(pb.bf[:, j * 128:j * 128 + nt], ptbb.a[:nt, j * 128:(j + 1) * 128], ident.a[:nt, :nt], [ptbb.b, ident.b], [pb.b])
        cp(pT.a[:, :, :nt], pb.bf[:, 0:256].rearrange("p (k t) -> p k t", k=2)[:, :, :nt], [pb.b], [pT.b])
        for half in range(2):
            pg_ = bank6(); pp_ = bank6()
            for kd in range(8):
                mm(pg_.a[:nt], x2T.a[:, kd, :nt], wbuf.a[:, kd, half * 512:(half + 1) * 512], kd == 0, kd == 7, [x2T.b, wbuf.b], [pg_.b])
            for kd in range(2):
                mm(pp_.a[:nt], pT.a[:, kd, :nt], plw.a[:, kd, half * 512:(half + 1) * 512], kd == 0, kd == 1, [pT.b, plw.b], [pp_.b])
            act(pl.a[:nt, half * 512:(half + 1) * 512], pg_.a[:nt], AF.Sigmoid, [pg_.b], [pl.b])
            tt(pl.a[:nt, half * 512:(half + 1) * 512], pl.a[:nt, half * 512:(half + 1) * 512], pp_.a[:nt], ALU.mult, [pl.b, pp_.b], [pl.b])
        tt(ht.a[:nt], ht.a[:nt], pl.a[:nt], ALU.add, [ht.b, pl.b], [ht.b])
        sumsq(ht, nt)
        ts(rstd.a[:nt], ssq.a[:nt], 1.0 / D, 1e-6, ALU.mult, ALU.add, [ssq.b], [rstd.b])
        rsqrt(rstd.a[:nt], rstd.b)
        stt(pl.a[:nt], ht.a[:nt], rstd.a[:nt, 0:1], fg_bc.a[:nt], ALU.mult, ALU.mult, [ht.b, rstd.b, fg_bc.b], [pl.b])
        dma(y_s if samp else y_p[t0:t0 + nt, :], pl.a[:nt], [pl.b], [Buf("o")])

    def ptbb_p(nt):
        return gel.a[:nt]

    for ti in range(ntile):
        phase2(ti)

    fin = S.final_waits()
    with nc.Block() as block:
        def emit(engine, name):
            for waits, fn, sem, inc in S.ops[name]:
                for (s_, v_) in waits:
                    engine.wait_ge(s_, v_)
                fn(engine).then_inc(sem, inc)

        @block.tensor
        def _(e):
            emit(e, "tensor")

        @block.vector
        def _(e):
            emit(e, "vector")

        @block.scalar
        def _(e):
            emit(e, "scalar")

        @block.gpsimd
        def _(e):
            emit(e, "gpsimd")

        @block.sync
        def _(e):
            emit(e, "sync")
            for (s_, v_) in fin:
                e.wait_ge(s_, v_)
    es.close()
    return nc


_NC = None


def kernel(**inp):
    global _NC
    f = lambda a: np.ascontiguousarray(a, dtype=np.float32)
    rowv = np.concatenate([inp[k][0].reshape(-1) for k in
                           ("shift_mu", "decay_w0", "a_0", "k_k", "k_a", "r_k", "lnx_g", "lnx_b", "pool_scale")]
                          + [inp["final_norm_g"].reshape(-1)]).astype(np.float32)[None, :]
    gcols = np.concatenate([inp[k][0].reshape(8, 128).T for k in ("norm_mix_g", "norm_ffn_g", "norm_ple_g")], axis=1)
    pos = np.arange(128)
    wd = np.array([2, 4, 8, 16])
    rc = np.stack([1.0 / np.minimum(pos[:, None] + 1, wd[None, :]), np.broadcast_to(1.0 / wd[None, :], (128, 4))]).astype(np.float32)
    ii = np.arange(128)
    su = (ii[:, None] < ii[None, :]).astype(np.float32)
    iu = (ii[:, None] <= ii[None, :]).astype(np.float32)
    cmk = np.stack([su, iu, su.T, np.ones((128, 128), np.float32)], axis=1)
    common = dict(
        w_in=f(inp["w_in"][0]), w_out=f(inp["w_out"][0]), wq=f(inp["peer_wq"][0]), ple_w=f(inp["ple_w"][0]),
        ple_gw=f(inp["ple_gate_w"][0]), decay_b=f(inp["decay_b"][0]), a_b=f(inp["a_b"][0]), g_b=f(inp["g_b"][0]),
        pool_w=f(inp["pool_w"][0]), keys=f(inp["peer_keys"][0]), peer_u=f(inp["peer_u"][0]), peer_v=f(inp["peer_v"][0]),
        gcols=f(gcols), rowv=f(rowv), ident=np.eye(128, dtype=np.float32), pool_rc=f(rc), cmask=f(cmk))
    in_maps = []
    for b in range(NCORES):
        m = dict(common)
        sl = slice(16 * b, 16 * b + 16)
        m["x_p"] = f(inp["x_prompt"][b]); m["x_s"] = f(inp["x_sample"][sl].reshape(NS, D))
        m["st_shift"] = f(inp["state_shift"][0, sl]); m["st_wkv"] = f(inp["state_wkv"][0, sl].reshape(128, 4096))
        m["st_pool"] = f(inp["state_pool"][0, sl]); m["p_p"] = f(inp["p_prompt"][0, b]); m["p_s"] = f(inp["p_sample"][0, sl].reshape(NS, 256))
        in_maps.append(m)
    if _NC is None:
        _NC = build_nc()
    res = run_bass_kernel_spmd(_NC, in_maps, core_ids=list(range(NCORES)))
    R = res.results
    y_p = np.stack([R[b]["y_p"] for b in range(NCORES)])
    y_s = np.concatenate([R[b]["y_s"].reshape(16, 4, D) for b in range(NCORES)])
    shp = np.stack([R[b]["o_shp"].reshape(DSH) for b in range(NCORES)])[None]
    wkp = np.stack([R[b]["o_wkp"].reshape(8, 64, 64) for b in range(NCORES)])[None]
    pop = np.stack([R[b]["o_pop"] for b in range(NCORES)])[None]
    shs = np.concatenate([R[b]["o_shs"] for b in range(NCORES)])[None]
    wks = np.concatenate([R[b]["o_wks"].reshape(16, 8, 64, 64) for b in range(NCORES)])[None]
    pos_ = np.concatenate([R[b]["o_pos"] for b in range(NCORES)])[None]
    return tuple(np.ascontiguousarray(a, dtype=np.float32) for a in (y_p, y_s, shp, wkp, pop, shs, wks, pos_))


if __name__ == "__main__":
    import time
    t = time.time()
    nc = build_nc()
    print("built", time.time() - t)
```
